# Optimizing a Trainium2 kernel written in Bass

```python
import math
import jax, jax.numpy as jnp
from jax import lax
import numpy as np

D_MODEL = 4096
BATCH = 4
SEQ = 4096
DEPTH = 4

N_MEM = 256
EPS = 1e-6
BLOCK = 128
MIX_WIDTH = D_MODEL
X_HEADS = 4
X_HEAD_DIM = MIX_WIDTH // 4 // X_HEADS
X_WIDTH = X_HEADS * X_HEAD_DIM
SELF_WIDTH = MIX_WIDTH - X_WIDTH
A_NOPE = 128
A_ROPE = 64
A_VDIM = 128
A_HEADS = SELF_WIDTH // A_VDIM
A_Q_RANK = 1024
A_KV_RANK = 512
ROPE_THETA = 10000.0
B_HEAD_DIM = 64
B_HEADS = SELF_WIDTH // B_HEAD_DIM
B_KV_HEADS = 8
B_GROUP = B_HEADS // B_KV_HEADS
WINDOW = 128
N_BUCKETS = 32
MAX_EXACT = N_BUCKETS // 2
MAX_DIST = WINDOW
N_MIXERS = 2
N_A = (DEPTH + 1) // 2
N_B = DEPTH // 2
A_IN = A_Q_RANK + A_KV_RANK + A_ROPE + X_WIDTH + MIX_WIDTH
B_IN = B_HEADS * B_HEAD_DIM + 2 * B_KV_HEADS * B_HEAD_DIM + X_WIDTH + MIX_WIDTH

kernel_name = "hybrid_mla_swa_sink_memxattn_gated"


def rmsnorm(x, g):
    xf = x.astype(jnp.float32)
    y = xf * lax.rsqrt(jnp.mean(xf * xf, axis=-1, keepdims=True) + EPS)
    return (y * g.astype(jnp.float32)).astype(x.dtype)


def split_cols(t, sizes):
    outs, off = [], 0
    for s in sizes:
        outs.append(t[..., off:off + s])
        off += s
    return outs


def rope_tables(positions):
    inv = ROPE_THETA ** (-jnp.arange(0, A_ROPE, 2, dtype=jnp.float32) / A_ROPE)
    ang = positions.astype(jnp.float32)[..., None] * inv
    return jnp.cos(ang), jnp.sin(ang)


def apply_rope(t, cos, sin):
    tf = t.astype(jnp.float32)
    t1, t2 = tf[..., :A_ROPE // 2], tf[..., A_ROPE // 2:]
    return jnp.concatenate([t1 * cos - t2 * sin, t2 * cos + t1 * sin], axis=-1).astype(t.dtype)


def t5_bucket(dist):
    n = jnp.maximum(dist, 0)
    nf = jnp.maximum(n, 1).astype(jnp.float32)
    large = MAX_EXACT + (jnp.log(nf / MAX_EXACT) / math.log(MAX_DIST / MAX_EXACT)
                         * (N_BUCKETS - MAX_EXACT)).astype(jnp.int32)
    large = jnp.minimum(large, N_BUCKETS - 1)
    return jnp.where(n < MAX_EXACT, n, large)


def memory_attention(q, mem_k, mem_v):
    B, S = q.shape[:2]
    s = jnp.einsum('bshd,bmhd->bhsm', q, mem_k).astype(jnp.float32) * (X_HEAD_DIM ** -0.5)
    p = jax.nn.softmax(s, axis=-1).astype(mem_v.dtype)
    return jnp.einsum('bhsm,bmhd->bshd', p, mem_v).reshape(B, S, X_WIDTH)


def mla_attention(q_nope, q_rope, k_nope, k_rope, v):
    B, S = q_nope.shape[:2]
    nblk = S // BLOCK
    scale = (A_NOPE + A_ROPE) ** -0.5
    key_idx = jnp.arange(S)
    q_local = jnp.arange(BLOCK)

    def to_blocks(t):
        return t.reshape(B, nblk, BLOCK, *t.shape[2:]).swapaxes(0, 1)

    def one_block(args):
        qn, qr, blk = args
        s = (jnp.einsum('bqhd,bkhd->bhqk', qn, k_nope)
             + jnp.einsum('bqhr,bkr->bhqk', qr, k_rope)).astype(jnp.float32) * scale
        causal = key_idx[None, :] <= (blk * BLOCK + q_local)[:, None]
        s = jnp.where(causal, s, -jnp.inf)
        p = jax.nn.softmax(s, axis=-1).astype(v.dtype)
        return jnp.einsum('bhqk,bkhd->bqhd', p, v)

    out = lax.map(one_block, (to_blocks(q_nope), to_blocks(q_rope), jnp.arange(nblk)))
    return out.swapaxes(0, 1).reshape(B, S, A_HEADS * A_VDIM)


def swa_attention(q, k, v, sinks, rel_bias):
    B, S = q.shape[:2]
    nblk = S // BLOCK

    def kv_bands(t):
        tp = jnp.pad(t, ((0, 0), (BLOCK, 0), (0, 0), (0, 0)))
        blocks = tp.reshape(B, nblk + 1, BLOCK, B_KV_HEADS, B_HEAD_DIM)
        return jnp.concatenate([blocks[:, :-1], blocks[:, 1:]], axis=2).swapaxes(0, 1)

    qb = q.reshape(B, nblk, BLOCK, B_KV_HEADS, B_GROUP, B_HEAD_DIM).swapaxes(0, 1)
    q_local = jnp.arange(BLOCK)[:, None]
    k_local = jnp.arange(2 * BLOCK)[None, :]
    dist = q_local + BLOCK - k_local
    in_window = (dist >= 0) & (dist < WINDOW)
    bias = rel_bias.astype(jnp.float32)[t5_bucket(dist)]
    bias = bias.transpose(2, 0, 1).reshape(B_KV_HEADS, B_GROUP, BLOCK, 2 * BLOCK)
    sink = sinks.astype(jnp.float32).reshape(B_KV_HEADS, B_GROUP)[None, :, :, None]
    scale = B_HEAD_DIM ** -0.5

    def one_block(args):
        qblk, kblk, vblk, blk = args
        s = jnp.einsum('bqhgd,bkhd->bhgqk', qblk, kblk).astype(jnp.float32) * scale + bias
        valid = in_window & (blk * BLOCK - BLOCK + k_local >= 0)
        s = jnp.where(valid, s, -jnp.inf)
        m = jnp.maximum(jnp.max(s, axis=-1), sink)
        p = jnp.exp(s - m[..., None])
        denom = jnp.sum(p, axis=-1) + jnp.exp(sink - m)
        p = (p / denom[..., None]).astype(vblk.dtype)
        return jnp.einsum('bhgqk,bkhd->bqhgd', p, vblk)

    out = lax.map(one_block, (qb, kv_bands(k), kv_bands(v), jnp.arange(nblk)))
    return out.swapaxes(0, 1).reshape(B, S, B_HEADS * B_HEAD_DIM)


def mla_mixer(h, cos, sin, w_in, q_norm_g, kv_norm_g, w_qb, w_kvb):
    B, S, _ = h.shape
    proj = jnp.einsum('bsd,de->bse', h, w_in)
    c_q, c_kv, k_rope, xq, z = split_cols(proj, [A_Q_RANK, A_KV_RANK, A_ROPE, X_WIDTH, MIX_WIDTH])
    q = jnp.einsum('bsr,re->bse', rmsnorm(c_q, q_norm_g), w_qb).reshape(B, S, A_HEADS, A_NOPE + A_ROPE)
    q_nope = q[..., :A_NOPE]
    q_rope = apply_rope(q[..., A_NOPE:], cos[:, :, None, :], sin[:, :, None, :])
    kv = jnp.einsum('bsr,re->bse', rmsnorm(c_kv, kv_norm_g), w_kvb).reshape(B, S, A_HEADS, A_NOPE + A_VDIM)
    k_nope, v = kv[..., :A_NOPE], kv[..., A_NOPE:]
    k_rope = apply_rope(k_rope, cos, sin)
    out = mla_attention(q_nope, q_rope, k_nope, k_rope, v)
    return out, xq.reshape(B, S, X_HEADS, X_HEAD_DIM), z


def swa_mixer(h, w_in, sinks, rel_bias):
    B, S, _ = h.shape
    proj = jnp.einsum('bsd,de->bse', h, w_in)
    q, k, v, xq, z = split_cols(proj, [B_HEADS * B_HEAD_DIM, B_KV_HEADS * B_HEAD_DIM,
                                       B_KV_HEADS * B_HEAD_DIM, X_WIDTH, MIX_WIDTH])
    q = q.reshape(B, S, B_HEADS, B_HEAD_DIM)
    k = k.reshape(B, S, B_KV_HEADS, B_HEAD_DIM)
    v = v.reshape(B, S, B_KV_HEADS, B_HEAD_DIM)
    out = swa_attention(q, k, v, sinks, rel_bias)
    return out, xq.reshape(B, S, X_HEADS, X_HEAD_DIM), z


def setup_inputs(seed: int = 0) -> dict:
    key = jax.random.key(seed)
    ks = jax.random.split(key, 20)
    f32 = jnp.float32

    def w(k, shape, fan_in):
        return jax.random.normal(k, shape, f32) * (fan_in ** -0.5)

    def gain(k, shape):
        return 1.0 + 0.02 * jax.random.normal(k, shape, f32)

    x = jax.random.normal(ks[0], (BATCH, SEQ, D_MODEL), f32)
    mem = jax.random.normal(ks[1], (BATCH, N_MEM, D_MODEL), f32)
    offset = jax.random.randint(ks[2], (BATCH, 1), 0, 1024, dtype=jnp.int32)
    positions = (offset + jnp.arange(SEQ, dtype=jnp.int32)[None, :]).astype(jnp.int32)
    return {
        "x": x,
        "mem": mem,
        "positions": positions,
        "norm_g": gain(ks[3], (DEPTH, D_MODEL)),
        "mem_norm_g": gain(ks[4], (DEPTH, D_MODEL)),
        "final_norm_g": gain(ks[5], (D_MODEL,)),
        "w_mem_kv": w(ks[6], (DEPTH, D_MODEL, 2 * X_WIDTH), D_MODEL),
        "w_out": w(ks[7], (DEPTH, MIX_WIDTH, D_MODEL), MIX_WIDTH),
        "a_w_in": w(ks[8], (N_A, D_MODEL, A_IN), D_MODEL),
        "a_q_norm_g": gain(ks[9], (N_A, A_Q_RANK)),
        "a_kv_norm_g": gain(ks[10], (N_A, A_KV_RANK)),
        "a_w_qb": w(ks[11], (N_A, A_Q_RANK, A_HEADS * (A_NOPE + A_ROPE)), A_Q_RANK),
        "a_w_kvb": w(ks[12], (N_A, A_KV_RANK, A_HEADS * (A_NOPE + A_VDIM)), A_KV_RANK),
        "b_w_in": w(ks[13], (N_B, D_MODEL, B_IN), D_MODEL),
        "b_sinks": 0.5 * jax.random.normal(ks[14], (N_B, B_HEADS), f32),
        "rel_bias": 0.5 * jax.random.normal(ks[15], (N_BUCKETS, B_HEADS), f32),
    }


def reference(x, mem, positions, norm_g, mem_norm_g, final_norm_g, w_mem_kv, w_out,
              a_w_in, a_q_norm_g, a_kv_norm_g, a_w_qb, a_w_kvb, b_w_in, b_sinks, rel_bias):
    B = x.shape[0]
    cos, sin = rope_tables(positions)
    for i in range(DEPTH):
        h = rmsnorm(x, norm_g[i])
        mn = rmsnorm(mem, mem_norm_g[i])
        mkv = jnp.einsum('bmd,de->bme', mn, w_mem_kv[i]).reshape(B, N_MEM, 2, X_HEADS, X_HEAD_DIM)
        j = i // N_MIXERS
        if i % N_MIXERS == 0:
            self_out, xq, z = mla_mixer(h, cos, sin, a_w_in[j], a_q_norm_g[j], a_kv_norm_g[j],
                                        a_w_qb[j], a_w_kvb[j])
        else:
            self_out, xq, z = swa_mixer(h, b_w_in[j], b_sinks[j], rel_bias)
        mem_out = memory_attention(xq, mkv[:, :, 0], mkv[:, :, 1])
        y = jnp.concatenate([self_out, mem_out], axis=-1) * jax.nn.silu(z)
        x = x + jnp.einsum('bse,ed->bsd', y, w_out[i])
    return rmsnorm(x, final_norm_g)
```

```python
import math
import numpy as np
from contextlib import ExitStack
import concourse.bass as bass
import concourse.mybir as mybir
from concourse.bass_utils import run_bass_kernel_spmd

F32, BF16, I32 = mybir.dt.float32, mybir.dt.bfloat16, mybir.dt.int32
AF = mybir.ActivationFunctionType
ALU = mybir.AluOpType

N_LAYERS = 4
P4SUB = 7
RUN = {"p1", "p2", "p3", "p4", "p5", "mem", "out"}
D = 4096
TOK = 2048
NB = 16
EPS = 1e-6
A_IN, B_IN = 6720, 9216
NEG = -30000.0
CHUNKS = {0: [0, 3, 4, 7], 1: [1, 2, 5, 6]}
NSEM_DMA = 20


class T:
    __slots__ = ("w", "r")

    def __init__(self):
        self.w = None
        self.r = {}


def Ts(n):
    return [T() for _ in range(n)]


class Op:
    __slots__ = ("eng", "fn", "deps", "kind", "sig", "ticket", "idx", "snap", "sem", "val", "waits")


class Prog:
    ENGS = ["pe", "act", "dve", "pool", "sp"]

    def __init__(self):
        self.ops = {e: [] for e in self.ENGS}
        self.allops = []
        self.dmah = {e: [] for e in self.ENGS}

    def op(self, eng, fn, reads=(), writes=(), kind="c"):
        o = Op()
        o.eng, o.fn, o.kind, o.sig = eng, fn, kind, False
        deps = []
        for t in reads:
            if t.w is not None:
                deps.append(t.w)
        for t in writes:
            for lst in t.r.values():
                deps.extend(lst)
            if t.w is not None:
                deps.append(t.w)
        if kind != "c":
            h = self.dmah[eng]
            if len(h) >= NSEM_DMA:
                deps.append(h[-NSEM_DMA])
            h.append(o)
        o.deps = deps
        key = eng if kind == "c" else "d" + eng
        for t in reads:
            lst = t.r.get(key)
            if lst is None:
                t.r[key] = [o]
            elif kind == "c":
                lst[0] = o
            else:
                lst.append(o)
                if len(lst) > NSEM_DMA:
                    del lst[0]
        for t in writes:
            t.w = o
            t.r = {}
        o.idx = len(self.ops[eng])
        self.ops[eng].append(o)
        self.allops.append(o)
        return o

    def resolve(self):
        known = {e: {f: -1 for f in self.ENGS} for e in self.ENGS}
        kdma = {e: set() for e in self.ENGS}
        for o in self.allops:
            E = o.eng
            kn = known[E]
            waits = []
            for x in o.deps:
                if x is o:
                    continue
                if x.kind != "c":
                    if id(x) not in kdma[E]:
                        kdma[E].add(id(x))
                        waits.append(x)
                    continue
                F = x.eng
                if F == E and E == "pe":
                    continue
                if x.idx <= kn[F]:
                    continue
                x.sig = True
                waits.append(x)
                kn[F] = x.idx
                for g, v in x.snap.items():
                    if v > kn[g]:
                        kn[g] = v
            o.waits = waits
            if o.kind == "c":
                o.snap = dict(kn)
            else:
                o.snap = None

    def emit(self, nc, es, handles_of_block):
        self.resolve()
        ROT = 20000
        sems = {}
        for e in self.ENGS:
            nsig = sum(1 for o in self.ops[e] if o.kind == "c" and o.sig)
            sems[e] = [es.enter_context(nc.semaphore(f"s_{e}_{i}")) for i in range(nsig // ROT + 1)]
            c = 0
            for o in self.ops[e]:
                if o.kind == "c" and o.sig:
                    o.sem = sems[e][c // ROT]
                    o.val = c % ROT + 1
                    c += 1
            nd = len(self.dmah[e])
            if nd:
                pool = [es.enter_context(nc.semaphore(f"d_{e}_{i}")) for i in range(min(nd, NSEM_DMA))]
                cnt = [0] * len(pool)
                ncc = sum(1 for o in self.dmah[e] if o.kind == "cc")
                ccpool = [es.enter_context(nc.semaphore(f"cc_{e}_{i}")) for i in range(min(ncc, 8))]
                cccnt = [0] * len(ccpool)
                ci = 0
                for i, o in enumerate(self.dmah[e]):
                    if o.kind == "cc":
                        k = ci % len(ccpool)
                        ci += 1
                        cccnt[k] += 1
                        o.sem, o.val = ccpool[k], cccnt[k]
                        continue
                    k = i % len(pool)
                    cnt[k] += 16
                    o.sem, o.val = pool[k], cnt[k]
        block = es.enter_context(nc.Block())

        def run(ename):
            def body(e):
                for o in self.ops[ename]:
                    for x in o.waits:
                        e.wait_ge(x.sem, x.val)
                    ins = o.fn(e)
                    if o.kind == "d":
                        ins.then_inc(o.sem, 16)
                    elif o.kind == "cc":
                        ins.then_inc(o.sem)
                    elif o.sig:
                        ins.then_inc(o.sem, 1)
                for o in self.dmah[ename][-NSEM_DMA:]:
                    e.wait_ge(o.sem, o.val)
            return body

        block.tensor(run("pe"))
        block.scalar(run("act"))
        block.vector(run("dve"))
        block.gpsimd(run("pool"))
        block.sync(run("sp"))


def t5_bucket_np(dist):
    n = np.maximum(dist, 0)
    nf = np.maximum(n, 1).astype(np.float32)
    large = 16 + (np.log(nf / 16) / math.log(128 / 16) * 16).astype(np.int32)
    large = np.minimum(large, 31)
    return np.where(n < 16, n, large)


def build(n_layers):
    nc = bass.Bass("TRN2", target_bir_lowering=False)
    P = Prog()

    def din(name, shape, dt=F32):
        need = {"a_w_in": "p1", "b_w_in": "p1", "a_w_qb": "p3", "a_w_kvb": "p4", "w_mem_kv": "mem", "w_out": "out"}
        for pre, ph in need.items():
            if name.startswith(pre) and ph not in RUN:
                shape = [1, 1]
        return nc.dram_tensor(name, list(shape), dt, kind="ExternalInput").ap()

    def dscr(name, shape, dt):
        return nc.dram_tensor(name, list(shape), dt).ap()

    x_in = din("x", [TOK, D])
    mem_in = din("mem", [256, D])
    pos_in = din("pos", [1, TOK], I32)
    norm_g = din("norm_g", [4, D])
    mem_norm_g = din("mem_norm_g", [4, D])
    final_g = din("final_norm_g", [1, D])
    nA, nB = (n_layers + 1) // 2, n_layers // 2
    w_mem_kv = [din(f"w_mem_kv_{l}", [D, 2048]) for l in range(n_layers)]
    w_out = [din(f"w_out_{l}", [D, D]) for l in range(n_layers)]
    a_w_in = [din(f"a_w_in_{l}", [D, A_IN]) for l in range(nA)]
    a_qg = din("a_q_norm_g", [2, 128, 8])
    a_kvg = din("a_kv_norm_g", [2, 128, 4])
    a_w_qb = [din(f"a_w_qb_{l}", [1024, 4608]) for l in range(nA)]
    a_w_kvb = [din(f"a_w_kvb_{l}", [512, 6144]) for l in range(nA)]
    b_w_in = [din(f"b_w_in_{l}", [D, B_IN]) for l in range(nB)]
    b_sinks = din("b_sinks", [2, 48])
    rel_bias = din("rel_bias", [32, 48])
    c_ident = din("c_ident", [128, 128])
    c_rope = din("c_rope", [128, 2])
    c_mask = din("c_mask", [128, 16, 512])
    c_sel = din("c_sel", [128, 16])
    out_d = nc.dram_tensor("out", [TOK, D], F32, kind="ExternalOutput").ap()

    xres = dscr("xres", [TOK, D], F32)
    projT = dscr("projT", [B_IN, TOK], F32)
    yT_d = dscr("yT_d", [D, TOK], BF16)
    sendKV_f = [dscr(f"sendKV{p}", [128, 512], F32) for p in range(10)]
    gathKV_f = [dscr(f"gathKV{p}", [256, 512], F32) for p in range(10)]
    sendKV = [a.bitcast(BF16).rearrange("(r two) c -> r (two c)", two=2) for a in sendKV_f]
    gathKV = [a.bitcast(BF16).rearrange("(r i two) c -> r i (two c)", r=2, two=2) for a in gathKV_f]
    QnT_d = dscr("QnT_d", [24, 128, TOK], BF16)
    QrT_d = dscr("QrT_d", [12, 128, TOK], BF16)
    KnT_d = dscr("KnT_d", [24, 128, 2 * TOK], BF16)
    V_d = dscr("V_d", [24, 128, 32, 128], BF16)
    cs_d = dscr("cs_d", [2, 128, TOK], F32)
    Gd = dscr("Gd", [48, 384], F32)
    BM_d = dscr("BM_d", [2, 128, 48 * 128], F32)
    sendS = [dscr(f"sendS{p}", [128, 512], F32) for p in range(8)]
    gathS = [dscr(f"gathS{p}", [256, 512], F32) for p in range(8)]
    kvp_d = dscr("kvp_d", [1024, 512], F32)

    xresT, projTT, yTT = Ts(NB), Ts(72), Ts(32)
    sendKVT, gathKVT, csT, GdT, BMT, sendST, gathST = T(), T(), T(), T(), T(), T(), T()
    QnTT, QrTT, KnTT, VTT, kvpT = Ts(24), Ts(12), Ts(24), Ts(24), Ts(8)

    es = ExitStack()
    with es:
        def sb(name, shape, dt):
            return es.enter_context(nc.sbuf_tensor(name, list(shape), dt))

        BIG = sb("BIG", [128, 32768], BF16)
        BIGT = Ts(32)
        BIGF = BIG[:].bitcast(F32)
        WB = [sb(f"WB{i}", [128, 8192], BF16) for i in range(3)]
        WBT = Ts(3)
        XS = [sb(f"XS{i}", [128, 4096], F32) for i in range(2)]
        XST = Ts(2)
        AUX = sb("AUX", [128, 4096], F32)
        AUXT = T()
        MASK = sb("MASK", [128, 16, 512], BF16)
        MASKT = T()
        HB = sb("HB", [128, 4096], BF16)
        HBT = T()
        EV = [sb(f"EV{i}", [128, 1024], F32) for i in range(4)]
        EVT = Ts(4)
        PT = [sb(f"PT{i}", [128, 512], BF16) for i in range(3)]
        PTT = Ts(3)
        ident = sb("ident", [128, 128], BF16)
        ones = sb("ones", [128, 128], BF16)
        epsT = sb("epsT", [128, 1], F32)
        crope = sb("crope", [128, 2], F32)
        csel = sb("csel", [128, 16], F32)
        stat = sb("stat", [128, 8], F32)
        statT = Ts(8)
        gq = sb("gq", [128, 12], F32)
        gqT = T()
        RB = sb("RB", [32, 48], F32)
        QG = [sb(f"QG{i}", [64, 768], BF16) for i in range(2)]
        QGT = Ts(2)
        ES = sb("ES", [64, 48], F32)
        EST = T()
        constT = T()
        PS = [es.enter_context(nc.psum_tensor(f"PS{i}", [128, 512], F32)) for i in range(6)]
        PST = Ts(6)
        PSBs = [es.enter_context(nc.psum_tensor(f"PSB{i}", [128, 1024], BF16)) for i in range(2)]
        PSBT = Ts(2)

        cnt = {"ps": 0, "ev": 0, "wb": 0, "pt": 0, "sps": 0}

        def nxt(k, n):
            v = cnt[k] % n
            cnt[k] += 1
            return v

        def dma(q, out, in_, reads, writes, slow=False):
            if slow:
                return P.op(q, lambda e: e.dma_start(out=out, in_=in_, allow_slow_non_contiguous=True), reads, writes, "d")
            return P.op(q, lambda e: e.dma_start(out=out, in_=in_), reads, writes, "d")

        def mm(out, lhsT, rhs, start, stop, reads, writes):
            return P.op("pe", lambda e: e.matmul(out, lhsT=lhsT, rhs=rhs, start=start, stop=stop), reads, writes)

        def act(out, in_, func, reads, writes, scale=1.0, bias=None, accum=None):
            def f(e):
                kw = {}
                if bias is not None:
                    kw["bias"] = bias
                if accum is not None:
                    kw["accum_out"] = accum
                return e.activation(out=out, in_=in_, func=func, scale=scale, **kw)
            return P.op("act", f, reads, writes)

        def rows(r0, r1):
            return projTT[r0 // 128:(r1 - 1) // 128 + 1]

        dma("pool", ident[:], c_ident[:, :], [], [constT])
        dma("sp", crope[:], c_rope[:, :], [], [constT])
        dma("sp", csel[:], c_sel[:, :], [], [constT])
        dma("pool", MASK[:], c_mask[:, :, :], [], [MASKT])
        P.op("dve", lambda e: e.memset(ones[:], 1.0), [], [constT])
        P.op("dve", lambda e: e.memset(epsT[:], EPS), [], [constT])

        posi = XS[0][:].bitcast(I32)
        dma("sp", posi[:, 0:TOK], pos_in[0:1, :].partition_broadcast(128), [], [XST[0]])
        ang = XS[1]
        P.op("dve", lambda e: e.tensor_copy(out=ang[:, 0:TOK], in_=posi[:, 0:TOK]), [XST[0]], [XST[1]])
        P.op("dve", lambda e: e.tensor_scalar(out=ang[:, 0:TOK], in0=ang[:, 0:TOK], scalar1=crope[:, 0:1], scalar2=None, op0=ALU.mult),
             [XST[1], constT], [XST[1]])
        P.op("dve", lambda e: e.tensor_scalar(out=ang[:, TOK:2 * TOK], in0=ang[:, 0:TOK], scalar1=0.25, scalar2=None, op0=ALU.add),
             [XST[1]], [XST[1]])
        x0f = XS[0]
        for (lo, hi) in ((0, TOK), (TOK, 2 * TOK)):
            P.op("dve", lambda e, lo=lo, hi=hi: e.tensor_copy(out=posi[:, TOK:2 * TOK], in_=ang[:, lo:hi]), [XST[1]], [XST[0]])
            P.op("dve", lambda e: e.tensor_copy(out=x0f[:, 0:TOK], in_=posi[:, TOK:2 * TOK]), [XST[0]], [XST[0]])
            P.op("dve", lambda e, lo=lo, hi=hi: e.tensor_tensor(out=ang[:, lo:hi], in0=ang[:, lo:hi], in1=x0f[:, 0:TOK], op=ALU.subtract),
                 [XST[0], XST[1]], [XST[1]])
            P.op("dve", lambda e, lo=lo, hi=hi: e.scalar_tensor_tensor(out=ang[:, lo:hi], in0=ang[:, lo:hi], scalar=0.5, in1=ang[:, lo:hi],
                                                                       op0=ALU.is_gt, op1=ALU.subtract), [XST[1]], [XST[1]])
        SC = -(2.0 * math.pi - 2e-6)
        act(AUX[:, TOK:2 * TOK], ang[:, 0:TOK], AF.Sin, [XST[1]], [AUXT], scale=SC)
        act(AUX[:, 0:TOK], ang[:, TOK:2 * TOK], AF.Sin, [XST[1]], [AUXT], scale=SC)
        P.op("dve", lambda e: e.tensor_scalar(out=AUX[:, TOK:2 * TOK], in0=AUX[:, TOK:2 * TOK], scalar1=crope[:, 1:2], scalar2=None, op0=ALU.mult),
             [AUXT, constT], [AUXT])
        dma("sp", cs_d[0], AUX[:, 0:TOK], [AUXT], [csT])
        dma("sp", cs_d[1], AUX[:, TOK:2 * TOK], [AUXT], [csT])

        def load_gain(g_ap_row):
            dma("sp", AUX[:, :], g_ap_row.partition_broadcast(128), [], [AUXT])

        def norm_block(src_ap, src_reads, xi, dst_fn):
            dma("sp", XS[xi][:], src_ap, src_reads, [XST[xi]])
            act(HB[:], XS[xi][:], AF.Square, [XST[xi]], [HBT, statT[xi]], accum=stat[:, xi:xi + 1])
            act(stat[:, xi:xi + 1], stat[:, xi:xi + 1], AF.Sqrt, [statT[xi], constT], [statT[xi]], scale=1.0 / D, bias=epsT[:, 0:1])
            P.op("dve", lambda e: e.reciprocal(out=stat[:, xi:xi + 1], in_=stat[:, xi:xi + 1]), [statT[xi]], [statT[xi]])
            dst_fn(xi)

        def h_transposed(xi, tb_local, nchunk_tok):
            P.op("dve", lambda e: e.scalar_tensor_tensor(out=HB[:], in0=XS[xi][:], scalar=stat[:, xi:xi + 1], in1=AUX[:, :],
                                                         op0=ALU.mult, op1=ALU.mult), [XST[xi], statT[xi], AUXT], [HBT])
            W = nchunk_tok
            for kg in range(8):
                hb = kg % 2
                for kk in range(4):
                    k = kg * 4 + kk
                    P.op("pe", lambda e, k=k, kk=kk, hb=hb: e.transpose(out=PSBs[hb][:, kk * 128:(kk + 1) * 128],
                                                                        in_=HB[:, k * 128:(k + 1) * 128], identity=ident[:]),
                         [HBT, constT], [PSBT[hb]])
                def cp(e, kg=kg, hb=hb):
                    o = BIG[:].rearrange("p (k t) -> p k t", t=W)[:, kg * 4:(kg + 1) * 4, tb_local * 128:(tb_local + 1) * 128]
                    i = PSBs[hb][:, 0:512].rearrange("p (k t) -> p k t", t=128)
                    return e.tensor_copy(out=o, in_=i)
                wr = [BIGT[(k * W) // 1024] for k in range(kg * 4, kg * 4 + 4)]
                P.op("dve", cp, [PSBT[hb]], wr)

        def load_w(src_ap, ncols):
            wi = nxt("wb", 3)
            o = WB[wi][:, 0:32 * ncols].rearrange("p (k n) -> p k n", n=ncols)
            sv = src_ap.rearrange("(k p) n -> p k n", p=128)
            for k in range(32):
                dma("pool", o[:, k, :], sv[:, k, :], [], [WBT[wi]])
            return wi, o

        def in_proj(w_ap, E, zstart, half):
            hT = BIG[:].rearrange("p (k t) -> p k t", t=1024)
            nblk = (E + 255) // 256
            for cb in range(nblk):
                c0 = cb * 256
                ncol = min(256, E - c0)
                wi, wv = load_w(w_ap[:, c0:c0 + ncol], ncol)
                for cc in range(0, ncol, 128):
                    m = min(128, ncol - cc)
                    e0 = c0 + cc
                    for tt in range(2):
                        pi = nxt("ps", 6)
                        for k in range(32):
                            mm(PS[pi][0:m, :], wv[:, k, cc:cc + m], hT[:, k, tt * 512:(tt + 1) * 512], k == 0, k == 31,
                               [WBT[wi], BIGT[k]], [PST[pi]])
                        ei = nxt("ev", 4)
                        segs = []
                        if e0 + m <= zstart:
                            segs = [(0, m, AF.Copy)]
                        elif e0 >= zstart:
                            segs = [(0, m, AF.Silu)]
                        else:
                            segs = [(0, zstart - e0, AF.Copy), (zstart - e0, m, AF.Silu)]
                        for (p0, p1, fn) in segs:
                            act(EV[ei][p0:p1, 0:512], PS[pi][p0:p1, :], fn, [PST[pi]], [EVT[ei]])
                        t0 = half * 1024 + tt * 512
                        dma("sp", projT[e0:e0 + m, t0:t0 + 512], EV[ei][0:m, 0:512], [EVT[ei]], rows(e0, e0 + m))

        def out_proj(layer, half, src_is_input):
            yv = BIG[:].rearrange("p (k t) -> p k t", t=1024)
            for kq in range(4):
                dma("sp", yv[:, kq * 8:(kq + 1) * 8, :],
                    yT_d[kq * 1024:(kq + 1) * 1024, half * 1024:(half + 1) * 1024].rearrange("(k p) t -> p k t", p=128),
                    yTT[kq * 8:(kq + 1) * 8], BIGT[kq * 8:(kq + 1) * 8])
            xsrc = x_in if src_is_input else xres
            for db in range(16):
                wi, wv = load_w(w_out[layer][:, db * 256:(db + 1) * 256], 256)
                for tbh in range(4):
                    pi = nxt("ps", 6)
                    ei = nxt("ev", 4)
                    for u in range(2):
                        tb = tbh * 2 + u
                        for k in range(32):
                            mm(PS[pi][:, u * 256:(u + 1) * 256], yv[:, k, tb * 128:(tb + 1) * 128], wv[:, k, :], k == 0, k == 31,
                               [WBT[wi], BIGT[k]], [PST[pi]])
                    gtb = half * 8 + tbh * 2
                    xrd = [] if src_is_input else xresT[gtb:gtb + 2]
                    xv = xsrc[gtb * 128:(gtb + 2) * 128, db * 256:(db + 1) * 256].rearrange("(u p) n -> p u n", p=128)
                    dma("sp", EV[ei][:, 0:512].rearrange("p (u n) -> p u n", n=256), xv, xrd, [EVT[ei]])
                    P.op("dve", lambda e, ei=ei, pi=pi: e.tensor_tensor(out=EV[ei][:, 0:512], in0=PS[pi][:, :], in1=EV[ei][:, 0:512], op=ALU.add),
                         [PST[pi], EVT[ei]], [EVT[ei]])
                    dma("sp", xres[gtb * 128:(gtb + 2) * 128, db * 256:(db + 1) * 256].rearrange("(u p) n -> p u n", p=128),
                        EV[ei][:, 0:512].rearrange("p (u n) -> p u n", n=256), [EVT[ei]], xresT[gtb:gtb + 2])

        def gate_store(o_ps, o_pst, den_ps, den_pst, zrow0, yrow0, t0, npart=128, width=512):
            e1, e2 = nxt("ev", 4), nxt("ev", 4)
            dma("sp", EV[e1][0:npart, 0:width], projT[zrow0:zrow0 + npart, t0:t0 + width], rows(zrow0, zrow0 + npart), [EVT[e1]])
            P.op("dve", lambda e: e.reciprocal(out=EV[e2][0:npart, 0:width], in_=den_ps), den_pst, [EVT[e2]])
            P.op("dve", lambda e: e.tensor_tensor(out=EV[e2][0:npart, 0:width], in0=o_ps, in1=EV[e2][0:npart, 0:width], op=ALU.mult),
                 o_pst + [EVT[e2]], [EVT[e2]])
            yb = EV[e2][0:npart, 512:1024].bitcast(BF16)[:, 0:width]
            P.op("dve", lambda e: e.tensor_tensor(out=yb, in0=EV[e2][0:npart, 0:width], in1=EV[e1][0:npart, 0:width], op=ALU.mult),
                 [EVT[e1], EVT[e2]], [EVT[e2]])
            dma("sp", yT_d[yrow0:yrow0 + npart, t0:t0 + width], yb, [EVT[e2]], yTT[yrow0 // 128:(yrow0 + npart - 1) // 128 + 1])

        def latent_norm(row0, nk, gcol0, tt, dst_fn):
            xi = nxt("xs2", 2) if False else (tt % 2)
            cv = XS[xi][:, 0:nk * 512].rearrange("p (k t) -> p k t", t=512)
            dma("sp", cv, projT[row0:row0 + nk * 128, tt * 512:(tt + 1) * 512].rearrange("(k p) t -> p k t", p=128),
                rows(row0, row0 + nk * 128), [XST[xi]])
            sq = HB[:, 0:nk * 512].rearrange("p (k t) -> p k t", t=512)
            act(sq, cv, AF.Square, [XST[xi]], [HBT])
            pi = nxt("ps", 6)
            for k in range(nk):
                mm(PS[pi][:, :], ones[:], sq[:, k, :], k == 0, k == nk - 1, [HBT, constT], [PST[pi]])
            ei = nxt("ev", 4)
            act(EV[ei][:, 0:512], PS[pi][:, :], AF.Sqrt, [PST[pi], constT], [EVT[ei]], scale=1.0 / (nk * 128), bias=epsT[:, 0:1])
            P.op("dve", lambda e: e.reciprocal(out=EV[ei][:, 0:512], in_=EV[ei][:, 0:512]), [EVT[ei]], [EVT[ei]])
            for k in range(nk):
                dst_fn(k, cv[:, k, :], EV[ei][:, 0:512], gq[:, gcol0 + k:gcol0 + k + 1], [XST[xi], EVT[ei], gqT])

        def mla_layer(layer, j):
            w_in = a_w_in[j]
            XQ0, Z0 = 1600, 2624
            dma("sp", gq[:, 0:8], a_qg[j], [], [gqT])
            dma("sp", gq[:, 8:12], a_kvg[j], [], [gqT])
            for half in range(2):
                if "p1" not in RUN:
                    break
                load_gain(norm_g[layer:layer + 1, :])
                for tbl in range(8):
                    tb = half * 8 + tbl
                    src = x_in if layer == 0 else xres
                    rd = [] if layer == 0 else [xresT[tb]]
                    norm_block(src[tb * 128:(tb + 1) * 128, :], rd, tb % 2, lambda xi, tbl=tbl: h_transposed(xi, tbl, 1024))
                in_proj(w_in, A_IN, Z0, half)
            CQN = BIG[:, 0:16384].rearrange("p (k t) -> p k t", t=TOK)
            CKV = BIG[:, 16384:32768].rearrange("p (k t) -> p k t", t=2 * TOK)
            KR = BIG[:, 24576:28672]
            if "p2" in RUN:
                dma("sp", AUX[:, 0:TOK], cs_d[0], [csT], [AUXT])
                dma("sp", AUX[:, TOK:2 * TOK], cs_d[1], [csT], [AUXT])
                CQN = BIG[:, 0:16384].rearrange("p (k t) -> p k t", t=TOK)
                for tt in range(4):
                    def put_cq(k, src, rstd, g, rd, tt=tt):
                        P.op("dve", lambda e: e.scalar_tensor_tensor(out=CQN[:, k, tt * 512:(tt + 1) * 512], in0=src, scalar=g, in1=rstd,
                                                                     op0=ALU.mult, op1=ALU.mult), rd, [BIGT[(k * TOK + tt * 512) // 1024]])
                    latent_norm(0, 8, 0, tt, put_cq)

                    def put_ckv(k, src, rstd, g, rd, tt=tt):
                        pt = nxt("pt", 3)
                        P.op("dve", lambda e: e.scalar_tensor_tensor(out=PT[pt][:, :], in0=src, scalar=g, in1=rstd,
                                                                     op0=ALU.mult, op1=ALU.mult), rd, [PTT[pt]])
                        for hh in range(2):
                            dma("sp", sendKV[2 * k + hh][:, tt * 512:(tt + 1) * 512], PT[pt][hh * 64:(hh + 1) * 64, :], [PTT[pt]], [sendKVT])
                    latent_norm(1024, 4, 8, tt, put_ckv)
                    e1, e2 = nxt("ev", 4), nxt("ev", 4)
                    tsl = slice(tt * 512, (tt + 1) * 512)
                    for hh in range(2):
                        dma("sp", EV[e1][hh * 64:(hh + 1) * 64, 0:512], projT[1536:1600, tsl], rows(1536, 1600), [EVT[e1]])
                        dma("sp", EV[e2][hh * 64:hh * 64 + 32, 0:512], projT[1568:1600, tsl], rows(1536, 1600), [EVT[e2]])
                        dma("sp", EV[e2][hh * 64 + 32:(hh + 1) * 64, 0:512], projT[1536:1568, tsl], rows(1536, 1600), [EVT[e2]])
                    P.op("dve", lambda e, e1=e1, tsl=tsl: e.tensor_tensor(out=EV[e1][:, 0:512], in0=EV[e1][:, 0:512], in1=AUX[:, tsl], op=ALU.mult),
                         [EVT[e1], AUXT], [EVT[e1]])
                    P.op("dve", lambda e, e2=e2, tt=tt: e.tensor_tensor(out=EV[e2][:, 0:512], in0=EV[e2][:, 0:512],
                                                                       in1=AUX[:, TOK + tt * 512:TOK + (tt + 1) * 512], op=ALU.mult),
                         [EVT[e2], AUXT], [EVT[e2]])
                    pt = nxt("pt", 3)
                    P.op("dve", lambda e, e1=e1, e2=e2, pt=pt: e.tensor_tensor(out=PT[pt][:, :], in0=EV[e1][:, 0:512], in1=EV[e2][:, 0:512], op=ALU.add),
                         [EVT[e1], EVT[e2]], [PTT[pt]])
                    for hh in range(2):
                        dma("sp", sendKV[8 + hh][:, tsl], PT[pt][hh * 64:(hh + 1) * 64, :], [PTT[pt]], [sendKVT])
                for pc in range(10):
                    P.op("pool", lambda e, pc=pc: e.collective_compute("AllGather", ALU.bypass, replica_groups=[[0, 1], [2, 3], [4, 5], [6, 7]],
                                                                     ins=[sendKV_f[pc][:, :]], outs=[gathKV_f[pc][:, :]]),
                         [sendKVT], [gathKVT], "cc")
            if "p3" in RUN:
                for hp in range(12):
                    wi = nxt("wb", 3)
                    wv = WB[wi][:, 0:4096].rearrange("p (k g n) -> p k g n", g=4, n=128)
                    wq = a_w_qb[j].rearrange("(k p) n -> p k n", p=128)
                    for u in range(2):
                        h = hp * 2 + u
                        dma("pool", wv[:, :, u, :], wq[:, :, h * 192:h * 192 + 128], [], [WBT[wi]])
                        dma("pool", wv[:, :, 2, u * 64:(u + 1) * 64], wq[:, :, h * 192 + 128:h * 192 + 192], [], [WBT[wi]])
                        dma("pool", wv[:, :, 3, u * 64:u * 64 + 32], wq[:, :, h * 192 + 160:h * 192 + 192], [], [WBT[wi]])
                        dma("pool", wv[:, :, 3, u * 64 + 32:(u + 1) * 64], wq[:, :, h * 192 + 128:h * 192 + 160], [], [WBT[wi]])
                    stg = [nxt("ev", 4) for _ in range(3)]
                    sv = [EV[s][:].bitcast(BF16) for s in stg]
                    for tt in range(4):
                        tsl = slice(tt * 512, (tt + 1) * 512)
                        pis = [nxt("ps", 6) for _ in range(4)]
                        for g in range(4):
                            for k in range(8):
                                mm(PS[pis[g]][:, :], wv[:, k, g, :], CQN[:, k, tsl], k == 0, k == 7,
                                   [WBT[wi], BIGT[(k * TOK + tt * 512) // 1024]], [PST[pis[g]]])
                        for u in range(2):
                            act(sv[u][:, tsl], PS[pis[u]][:, :], AF.Copy, [PST[pis[u]]], [EVT[stg[u]]])
                        x1, x2 = XS[0][:, 0:512], XS[0][:, 512:1024]
                        P.op("dve", lambda e, pi=pis[2], tsl=tsl: e.tensor_tensor(out=x1, in0=PS[pi][:, :], in1=AUX[:, tsl], op=ALU.mult),
                             [PST[pis[2]], AUXT], [XST[0]])
                        P.op("dve", lambda e, pi=pis[3], tt=tt: e.tensor_tensor(out=x2, in0=PS[pi][:, :], in1=AUX[:, TOK + tt * 512:TOK + (tt + 1) * 512], op=ALU.mult),
                             [PST[pis[3]], AUXT], [XST[0]])
                        P.op("dve", lambda e, tsl=tsl, s2=sv[2]: e.tensor_tensor(out=s2[:, tsl], in0=x1, in1=x2, op=ALU.add), [XST[0]], [EVT[stg[2]]])
                    for u in range(2):
                        dma("sp", QnT_d[hp * 2 + u], sv[u][:, 0:TOK], [EVT[stg[u]]], [QnTT[hp * 2 + u]])
                    dma("sp", QrT_d[hp], sv[2][:, 0:TOK], [EVT[stg[2]]], [QrTT[hp]])
            if "p4" in RUN:
                CKV = BIG[:, 16384:32768].rearrange("p (k t) -> p k t", t=2 * TOK)
                for r in range(2):
                    for k in range(4):
                        for hh in range(2):
                            dma("sp", CKV[hh * 64:(hh + 1) * 64, k, r * TOK:(r + 1) * TOK], gathKV[2 * k + hh][r, :, :], [gathKVT],
                                BIGT[16 + k * 4 + r * 2:16 + k * 4 + r * 2 + 2])
                wkv = a_w_kvb[j].rearrange("(k p) (h n) -> p k h n", p=128, n=256)
                for hg in range(6):
                    wi = nxt("wb", 3)
                    wk = WB[wi][:, 0:2048].rearrange("p (k h n) -> p k h n", h=4, n=128)
                    wvv = WB[wi][:, 2048:4096].rearrange("p (k h n) -> p k h n", h=4, n=128)
                    for k in range(4):
                        dma("pool", wk[:, k], wkv[:, k, hg * 4:(hg + 1) * 4, 0:128], [], [WBT[wi]])
                        dma("pool", wvv[:, k], wkv[:, k, hg * 4:(hg + 1) * 4, 128:256], [], [WBT[wi]])
                    for u in range(4):
                        if not (P4SUB & 2):
                            break
                        h = hg * 4 + u
                        for th in range(2):
                            ei = nxt("ev", 4)
                            stv = EV[ei][:].bitcast(BF16)
                            for t4 in range(4):
                                t0 = th * 2048 + t4 * 512
                                pi = nxt("ps", 6)
                                for k in range(4):
                                    mm(PS[pi][:, :], wk[:, k, u, :], CKV[:, k, t0:t0 + 512], k == 0, k == 3,
                                       [WBT[wi], BIGT[16 + k * 4 + t0 // 1024]], [PST[pi]])
                                act(stv[:, t4 * 512:(t4 + 1) * 512], PS[pi][:, :], AF.Copy, [PST[pi]], [EVT[ei]])
                            dma("sp", KnT_d[h, :, th * 2048:(th + 1) * 2048], stv[:, 0:2048], [EVT[ei]], [KnTT[h]])
                    for kb4 in range(8):
                        if not (P4SUB & 4):
                            break
                        ei = nxt("ev", 4)
                        stv = EV[ei][:].bitcast(BF16).rearrange("p (b n) -> p b n", n=512)
                        for b_ in range(4):
                            kb = kb4 * 4 + b_
                            pi = nxt("ps", 6)
                            for k in range(4):
                                mm(PS[pi][:, :], CKV[:, k, kb * 128:(kb + 1) * 128], WB[wi][:, 2048 + k * 512:2048 + (k + 1) * 512],
                                   k == 0, k == 3, [WBT[wi], BIGT[16 + k * 4 + (kb * 128) // 1024]], [PST[pi]])
                            if True:
                                act(stv[:, b_, :], PS[pi][:, :], AF.Copy, [PST[pi]], [EVT[ei]])
                            else:
                                P.op("dve", lambda e, b_=b_, pi=pi, stv=stv: e.tensor_copy(out=stv[:, b_, :], in_=PS[pi][:, :]), [PST[pi]], [EVT[ei]])
                        for u in range(4):
                            if P4SUB & 16:
                                break
                            dma("sp", V_d[hg * 4 + u, :, kb4 * 4:(kb4 + 1) * 4, :], stv[:, :, u * 128:(u + 1) * 128], [EVT[ei]], [VTT[hg * 4 + u]])
            if "p5" in RUN:
                KR = BIG[:, 24576:28672]
                for r in range(2):
                    for hh in range(2):
                        dma("sp", KR[hh * 64:(hh + 1) * 64, r * TOK:(r + 1) * TOK], gathKV[8 + hh][r, :, :], [gathKVT], BIGT[24 + r * 2:24 + r * 2 + 2])
                scale = 192.0 ** -0.5
                for h in range(24):
                    b2 = h % 2
                    hp = h // 2
                    KN = BIG[:, b2 * 4096:(b2 + 1) * 4096]
                    KNT_ = BIGT[b2 * 4:(b2 + 1) * 4]
                    VH = BIG[:, 8192 + b2 * 4096:8192 + (b2 + 1) * 4096].rearrange("p (b n) -> p b n", n=128)
                    VHT_ = BIGT[8 + b2 * 4:8 + (b2 + 1) * 4]
                    QN = BIG[:, 16384 + b2 * 2048:16384 + (b2 + 1) * 2048]
                    QNT_ = BIGT[16 + b2 * 2:16 + (b2 + 1) * 2]
                    QR = BIG[:, 20480 + (hp % 2) * 2048:20480 + (hp % 2 + 1) * 2048]
                    QRT_ = BIGT[20 + (hp % 2) * 2:20 + (hp % 2 + 1) * 2]
                    dma("sp", KN, KnT_d[h], [KnTT[h]], KNT_)
                    dma("sp", VH, V_d[h], [VTT[h]], VHT_)
                    dma("sp", QN, QnT_d[h], [QnTT[h]], QNT_)
                    if b2 == 0:
                        dma("sp", QR, QrT_d[hp], [QrTT[hp]], QRT_)
                    ro = b2 * 64
                    for J in range(4):
                        qs = slice(J * 512, (J + 1) * 512)
                        tiles = [(c, r, kb) for c in range(J + 1) for r in range(2) for kb in range(4)]
                        gidx = h * 4 + J
                        opi, dpi = (2, 3) if gidx % 2 == 0 else (4, 5)
                        pend = None
                        n = len(tiles)

                        def qk(ti):
                            c, r, kb = tiles[ti]
                            kbi = r * 16 + c * 4 + kb
                            ks = slice(kbi * 128, (kbi + 1) * 128)
                            pi = nxt("sps", 2)
                            kt = [KNT_[(kbi * 128) // 1024]]
                            last_is_mask = (c == J)
                            mm(PS[pi][:, :], KN[:, ks], QN[:, qs], True, False, kt + [QNT_[J // 2]], [PST[pi]])
                            mm(PS[pi][:, :], KR[ro:ro + 64, ks], QR[ro:ro + 64, qs], False, not last_is_mask,
                               [BIGT[24 + (kbi * 128) // 1024], QRT_[J // 2]], [PST[pi]])
                            if last_is_mask:
                                mm(PS[pi][:, :], ident[:], MASK[:, (J % 2) * 8 + r * 4 + kb, :], False, True, [constT, MASKT], [PST[pi]])
                            pt = nxt("pt", 3)
                            act(PT[pt][:, :], PS[pi][:, :], AF.Exp, [PST[pi]], [PTT[pt]], scale=scale)
                            return (pt, kbi)

                        def pv(ti, st):
                            pt, kbi = st
                            mm(PS[opi][:, :], VH[:, kbi, :], PT[pt][:, :], ti == 0, ti == n - 1, [VHT_[(kbi * 128) // 1024], PTT[pt]], [PST[opi]])
                            mm(PS[dpi][:, :], ones[:], PT[pt][:, :], ti == 0, ti == n - 1, [constT, PTT[pt]], [PST[dpi]])

                        for ti in range(n):
                            st = qk(ti)
                            if pend is not None:
                                pv(ti - 1, pend)
                            pend = st
                        pv(n - 1, pend)
                        gate_store(PS[opi][:, :], [PST[opi]], PS[dpi][:, :], [PST[dpi]], Z0 + h * 128, h * 128, J * 512)
            if "mem" in RUN:
                mem_attention(layer, XQ0, Z0 + 3072)
            if "out" in RUN:
                for half in range(2):
                    out_proj(layer, half, layer == 0)

        def mem_attention(layer, XQ0, ZM0):
            load_gain(mem_norm_g[layer:layer + 1, :])
            MN = BIG[:, 0:8192].rearrange("p (k t) -> p k t", t=256)
            for mb in range(2):
                def tr(xi, mb=mb):
                    P.op("dve", lambda e: e.scalar_tensor_tensor(out=HB[:], in0=XS[xi][:], scalar=stat[:, xi:xi + 1], in1=AUX[:, :],
                                                                 op0=ALU.mult, op1=ALU.mult), [XST[xi], statT[xi], AUXT], [HBT])
                    for kg in range(8):
                        hb = kg % 2
                        for kk in range(4):
                            k = kg * 4 + kk
                            P.op("pe", lambda e, k=k, kk=kk, hb=hb: e.transpose(out=PSBs[hb][:, kk * 128:(kk + 1) * 128],
                                                                                in_=HB[:, k * 128:(k + 1) * 128], identity=ident[:]),
                                 [HBT, constT], [PSBT[hb]])
                        P.op("dve", lambda e, kg=kg, hb=hb: e.tensor_copy(out=MN[:, kg * 4:(kg + 1) * 4, mb * 128:(mb + 1) * 128],
                                                                         in_=PSBs[hb][:, 0:512].rearrange("p (k t) -> p k t", t=128)),
                             [PSBT[hb]], [BIGT[kg]])
                norm_block(mem_in[mb * 128:(mb + 1) * 128, :], [], mb, tr)
            MKT = BIG[:, 8192:10240].rearrange("p (c m) -> p c m", m=256)
            MV = BIG[:, 10240:12288].rearrange("p (c n) -> p c n", n=1024)
            for cb in range(8):
                wi, wv = load_w(w_mem_kv[layer][:, cb * 256:(cb + 1) * 256], 256)
                if cb < 4:
                    for cc in range(2):
                        c = cb * 2 + cc
                        pi = nxt("ps", 6)
                        for k in range(32):
                            mm(PS[pi][:, 0:256], wv[:, k, cc * 128:(cc + 1) * 128], MN[:, k, :], k == 0, k == 31, [WBT[wi], BIGT[k // 4]], [PST[pi]])
                        act(MKT[:, c, :], PS[pi][:, 0:256], AF.Copy, [PST[pi]], [BIGT[8 + c // 4]])
                else:
                    n0 = (cb - 4) * 256
                    for mc in range(2):
                        pi = nxt("ps", 6)
                        for k in range(32):
                            mm(PS[pi][:, 0:256], MN[:, k, mc * 128:(mc + 1) * 128], wv[:, k, :], k == 0, k == 31, [WBT[wi], BIGT[k // 4]], [PST[pi]])
                        act(MV[:, mc, n0:n0 + 256], PS[pi][:, 0:256], AF.Copy, [PST[pi]], [BIGT[10 + mc]])
            XQ = BIG[:, 16384:32768].rearrange("p (c t) -> p c t", t=TOK)
            for c in range(8):
                dma("pool", XQ[:, c, :], projT[XQ0 + c * 128:XQ0 + (c + 1) * 128, :], rows(XQ0 + c * 128, XQ0 + (c + 1) * 128),
                    BIGT[16 + c * 2:16 + c * 2 + 2])
            for hx in range(4):
                for tt in range(4):
                    tsl = slice(tt * 512, (tt + 1) * 512)
                    pts = []
                    for mc in range(2):
                        pi = nxt("ps", 6)
                        for dc in range(2):
                            c = hx * 2 + dc
                            mm(PS[pi][:, :], MKT[:, c, mc * 128:(mc + 1) * 128], XQ[:, c, tsl], dc == 0, dc == 1,
                               [BIGT[8 + c // 4], BIGT[16 + c * 2 + tt // 2]], [PST[pi]])
                        pt = nxt("pt", 3)
                        act(PT[pt][:, :], PS[pi][:, :], AF.Exp, [PST[pi]], [PTT[pt]], scale=1.0 / 16.0)
                        pts.append(pt)
                    dpi = nxt("ps", 6)
                    for mc in range(2):
                        mm(PS[dpi][:, :], ones[:], PT[pts[mc]][:, :], mc == 0, mc == 1, [constT, PTT[pts[mc]]], [PST[dpi]])
                    for dvc in range(2):
                        opi = nxt("ps", 6)
                        for mc in range(2):
                            mm(PS[opi][:, :], MV[:, mc, hx * 256 + dvc * 128:hx * 256 + (dvc + 1) * 128], PT[pts[mc]][:, :], mc == 0, mc == 1,
                               [BIGT[10 + mc], PTT[pts[mc]]], [PST[opi]])
                        gate_store(PS[opi][:, :], [PST[opi]], PS[dpi][:, :], [PST[dpi]], ZM0 + hx * 256 + dvc * 128,
                                   3072 + hx * 256 + dvc * 128, tt * 512)

        swa_state = {"bm": False}

        def swa_setup():
            dist = np.arange(128)
            bucket = t5_bucket_np(dist)
            gfill = XS[0][0:48, 0:384]
            P.op("dve", lambda e: e.memset(gfill, NEG), [], [XST[0]])
            dma("sp", Gd[:, :], gfill, [XST[0]], [GdT])
            dma("sp", RB[:], rel_bias[:, :], [], [constT])
            for d in range(128):
                b = int(bucket[d])
                dma("sp", Gd[:, 127 + d:128 + d].rearrange("h o -> o h"), RB[b:b + 1, :], [constT], [GdT], slow=True)
            BMv = BIGF[:, 0:6144].rearrange("p (h q) -> p h q", q=128)
            for which in range(2):
                for k in range(128):
                    off = (127 - k) if which == 0 else (255 - k)
                    dma("sp", BMv[k:k + 1, :, :], Gd[:, off:off + 128].rearrange("(o h) q -> o h q", o=1), [GdT], BIGT[0:12])
                dma("sp", BM_d[which], BIGF[:, 0:6144], BIGT[0:12], [BMT])

        def swa_layer(layer, j):
            w_in = b_w_in[j]
            K0, V0, XQ0, Z0 = 3072, 3584, 4096, 5120
            if not swa_state["bm"]:
                swa_setup()
                swa_state["bm"] = True
            for half in range(2):
                load_gain(norm_g[layer:layer + 1, :])
                for tbl in range(8):
                    tb = half * 8 + tbl
                    norm_block(xres[tb * 128:(tb + 1) * 128, :], [xresT[tb]], tb % 2, lambda xi, tbl=tbl: h_transposed(xi, tbl, 1024))
                in_proj(w_in, B_IN, Z0, half)
            for i in range(4):
                ei = nxt("ev", 4)
                for kq in range(8):
                    dma("sp", EV[ei][:, kq * 128:(kq + 1) * 128], projT[K0 + kq * 128:K0 + (kq + 1) * 128, (4 * i + 3) * 128:(4 * i + 4) * 128],
                        rows(K0 + kq * 128, K0 + (kq + 1) * 128), [EVT[ei]])
                for kq in range(8):
                    dma("sp", sendS[kq][:, i * 128:(i + 1) * 128], EV[ei][:, kq * 128:(kq + 1) * 128], [EVT[ei]], [sendST])
            for pc in range(8):
                P.op("pool", lambda e, pc=pc: e.collective_compute("AllGather", ALU.bypass, replica_groups=[[0, 1], [2, 3], [4, 5], [6, 7]],
                                                                 ins=[sendS[pc][:, :]], outs=[gathS[pc][:, :]]),
                     [sendST], [gathST], "cc")
            for i in range(4):
                e_o, e_a, e_b = nxt("ev", 4), nxt("ev", 4), nxt("ev", 4)
                if i > 0:
                    for kq in range(8):
                        dma("sp", EV[e_o][:, kq * 128:(kq + 1) * 128], projT[K0 + kq * 128:K0 + (kq + 1) * 128, (4 * i - 1) * 128:(4 * i) * 128],
                            rows(K0 + kq * 128, K0 + (kq + 1) * 128), [EVT[e_o]])
                else:
                    P.op("dve", lambda e, e_o=e_o: e.memset(EV[e_o][:, :], 0.0), [], [EVT[e_o]])
                P.op("dve", lambda e, e_o=e_o, i=i: e.tensor_scalar(out=EV[e_o][:, :], in0=EV[e_o][:, :], scalar1=csel[:, 3 * i:3 * i + 1], scalar2=None, op0=ALU.mult),
                     [EVT[e_o], constT], [EVT[e_o]])
                for r, e_r in ((0, e_a), (1, e_b)):
                    for kq in range(8):
                        dma("sp", EV[e_r][:, kq * 128:(kq + 1) * 128], gathS[kq][r * 128:(r + 1) * 128, i * 128:(i + 1) * 128], [gathST], [EVT[e_r]])
                    P.op("dve", lambda e, e_o=e_o, e_r=e_r, i=i, r=r: e.scalar_tensor_tensor(out=EV[e_o][:, :], in0=EV[e_r][:, :], scalar=csel[:, 3 * i + 1 + r:3 * i + 2 + r],
                                                                                          in1=EV[e_o][:, :], op0=ALU.mult, op1=ALU.add),
                         [EVT[e_o], EVT[e_r], constT], [EVT[e_o]])
                for kq in range(8):
                    dma("sp", kvp_d[kq * 128:(kq + 1) * 128, i * 128:(i + 1) * 128], EV[e_o][:, kq * 128:(kq + 1) * 128], [EVT[e_o]], [kvpT[kq]])
            BMv = BIGF[:, 0:12288].rearrange("p (w h q) -> p w h q", w=2, q=128)
            for which in range(2):
                dma("sp", BIGF[:, which * 6144:(which + 1) * 6144], BM_d[which], [BMT], BIGT[which * 12:(which + 1) * 12])
            dma("sp", ES[:, :], b_sinks[j:j + 1, :].partition_broadcast(64), [], [EST])
            act(ES[:, :], ES[:, :], AF.Exp, [EST], [EST])
            sc = 0.125
            KC = BIG[0:64, 24576:26624]
            VCt = BIG[0:64, 26624:28672]
            KB = BIG[0:64, 28672:29184]
            VBt = BIG[0:64, 29696:30208]
            VC = HB[:, 0:1024].rearrange("p (b n) -> p b n", n=64)
            VB = HB[:, 1024:1280].rearrange("p (b n) -> p b n", n=64)
            for g in range(8):
                dma("pool", KC, projT[K0 + g * 64:K0 + (g + 1) * 64, :], rows(K0 + g * 64, K0 + (g + 1) * 64), BIGT[24:26])
                dma("pool", VCt, projT[V0 + g * 64:V0 + (g + 1) * 64, :], rows(V0 + g * 64, V0 + (g + 1) * 64), BIGT[26:28])
                dma("pool", KB, kvp_d[g * 64:(g + 1) * 64, :], [kvpT[g // 2]], [BIGT[28]])
                dma("pool", VBt, kvp_d[512 + g * 64:512 + (g + 1) * 64, :], [kvpT[4 + g // 2]], [BIGT[29]])
                for hb in range(2):
                    for bb in range(8):
                        b_ = hb * 8 + bb
                        P.op("pe", lambda e, b_=b_, bb=bb, hb=hb: e.transpose(out=PSBs[hb][:, bb * 64:(bb + 1) * 64],
                                                                           in_=VCt[:, b_ * 128:(b_ + 1) * 128], identity=ident[0:64, 0:64]),
                             [BIGT[26 + b_ // 8], constT], [PSBT[hb]])
                    P.op("dve", lambda e, hb=hb: e.tensor_copy(out=HB[:, hb * 512:(hb + 1) * 512], in_=PSBs[hb][:, 0:512]), [PSBT[hb]], [HBT])
                for bb in range(4):
                    P.op("pe", lambda e, bb=bb: e.transpose(out=PSBs[0][:, bb * 64:(bb + 1) * 64], in_=VBt[:, bb * 128:(bb + 1) * 128],
                                                          identity=ident[0:64, 0:64]), [BIGT[29], constT], [PSBT[0]])
                P.op("dve", lambda e: e.tensor_copy(out=HB[:, 1024:1280], in_=PSBs[0][:, 0:256]), [PSBT[0]], [HBT])
                for jb in range(NB):
                    qsl = slice(jb * 128, (jb + 1) * 128)
                    if jb % 4 == 0:
                        i = jb // 4
                        kprev, kprevT = KB[:, i * 128:(i + 1) * 128], BIGT[28]
                        vprev = VB[:, i, :]
                    else:
                        kprev, kprevT = KC[:, (jb - 1) * 128:jb * 128], BIGT[24 + (jb - 1) // 8]
                        vprev = VC[:, jb - 1, :]
                    qi = jb % 2
                    QGv = QG[qi][:, :].rearrange("p (h q) -> p h q", q=128)
                    dma("pool", QGv, projT[g * 384:(g + 1) * 384, qsl].rearrange("(h p) q -> p h q", p=64), rows(g * 384, (g + 1) * 384), [QGT[qi]])
                    for hf in range(2):
                        h0 = g * 6 + hf * 3
                        qrhs = QGv[:, hf * 3:hf * 3 + 3, :]
                        p_prev, p_cur = nxt("ps", 6), nxt("ps", 6)
                        mm(PS[p_prev][:, 0:384], kprev, qrhs, True, True, [kprevT, QGT[qi]], [PST[p_prev]])
                        mm(PS[p_cur][:, 0:384], KC[:, qsl], qrhs, True, True, [BIGT[24 + jb // 8], QGT[qi]], [PST[p_cur]])
                        pts = []
                        for which, pp in ((1, p_prev), (0, p_cur)):
                            e_t = nxt("ev", 4)
                            P.op("dve", lambda e, pp=pp, e_t=e_t, which=which, h0=h0: e.scalar_tensor_tensor(
                                out=EV[e_t][:, 0:384].rearrange("p (h q) -> p h q", q=128), in0=PS[pp][:, 0:384].rearrange("p (h q) -> p h q", q=128),
                                scalar=sc, in1=BMv[:, which, h0:h0 + 3, :], op0=ALU.mult, op1=ALU.add),
                                [PST[pp]] + BIGT[which * 12:(which + 1) * 12], [EVT[e_t]])
                            pt = nxt("pt", 3)
                            act(PT[pt][:, 0:384], EV[e_t][:, 0:384], AF.Exp, [EVT[e_t]], [PTT[pt]])
                            pts.append(pt)
                        opi, dpi = nxt("ps", 6), nxt("ps", 6)
                        mm(PS[opi][0:64, 0:384], vprev, PT[pts[0]][:, 0:384], True, False, [HBT, PTT[pts[0]]], [PST[opi]])
                        mm(PS[opi][0:64, 0:384], VC[:, jb, :], PT[pts[1]][:, 0:384], False, True, [HBT, PTT[pts[1]]], [PST[opi]])
                        mm(PS[dpi][0:64, 0:384], ONP_l[jb], PT[pts[0]][:, 0:384], True, False, [constT, PTT[pts[0]]], [PST[dpi]])
                        mm(PS[dpi][0:64, 0:384], ones[:, 0:64], PT[pts[1]][:, 0:384], False, True, [constT, PTT[pts[1]]], [PST[dpi]])
                        e_d = nxt("ev", 4)
                        for u in range(3):
                            P.op("dve", lambda e, u=u, e_d=e_d, dpi=dpi, h0=h0: e.tensor_scalar(
                                out=EV[e_d][0:64, u * 128:(u + 1) * 128], in0=PS[dpi][0:64, u * 128:(u + 1) * 128],
                                scalar1=ES[:, h0 + u:h0 + u + 1], scalar2=None, op0=ALU.add), [PST[dpi], EST], [EVT[e_d]])
                        P.op("dve", lambda e, e_d=e_d: e.reciprocal(out=EV[e_d][0:64, 0:384], in_=EV[e_d][0:64, 0:384]), [EVT[e_d]], [EVT[e_d]])
                        P.op("dve", lambda e, e_d=e_d, opi=opi: e.tensor_tensor(out=EV[e_d][0:64, 0:384], in0=PS[opi][0:64, 0:384], in1=EV[e_d][0:64, 0:384], op=ALU.mult),
                             [PST[opi], EVT[e_d]], [EVT[e_d]])
                        e_z = nxt("ev", 4)
                        zv = EV[e_z][0:64, 0:384].rearrange("p (h q) -> p h q", q=128)
                        dma("sp", zv, projT[Z0 + h0 * 64:Z0 + (h0 + 3) * 64, qsl].rearrange("(h p) q -> p h q", p=64), rows(Z0 + h0 * 64, Z0 + (h0 + 3) * 64), [EVT[e_z]])
                        yb = EV[e_d][0:64, 512:1024].bitcast(BF16)[:, 0:384]
                        P.op("dve", lambda e, e_d=e_d, e_z=e_z, yb=yb: e.tensor_tensor(out=yb, in0=EV[e_d][0:64, 0:384], in1=EV[e_z][0:64, 0:384], op=ALU.mult),
                             [EVT[e_d], EVT[e_z]], [EVT[e_d]])
                        dma("sp", yT_d[h0 * 64:(h0 + 3) * 64, qsl].rearrange("(h p) q -> p h q", p=64), yb.rearrange("p (h q) -> p h q", q=128),
                            [EVT[e_d]], yTT[(h0 * 64) // 128:((h0 + 3) * 64 - 1) // 128 + 1])
            mem_attention(layer, XQ0, Z0 + 3072)
            for half in range(2):
                out_proj(layer, half, False)

        onp0 = sb("onp0", [128, 64], BF16)
        P.op("dve", lambda e: e.memset(onp0[:], 1.0), [], [constT])
        P.op("dve", lambda e: e.tensor_scalar(out=onp0[:], in0=onp0[:], scalar1=csel[:, 12:13], scalar2=None, op0=ALU.mult), [constT], [constT])
        ONP_l = [onp0[:, :]] + [ones[:, 0:64]] * (NB - 1)

        for layer in range(n_layers):
            if layer % 2 == 0:
                mla_layer(layer, layer // 2)
            else:
                swa_layer(layer, layer // 2)
        load_gain(final_g[0:1, :])
        for tb in range(NB):
            def fin(xi, tb=tb):
                P.op("dve", lambda e: e.scalar_tensor_tensor(out=XS[xi][:], in0=XS[xi][:], scalar=stat[:, xi:xi + 1], in1=AUX[:, :],
                                                             op0=ALU.mult, op1=ALU.mult), [XST[xi], statT[xi], AUXT], [XST[xi]])
                dma("sp", out_d[tb * 128:(tb + 1) * 128, :], XS[xi][:], [XST[xi]], [])
            use_res = n_layers > 0 and "out" in RUN
            src = xres if use_res else x_in
            norm_block(src[tb * 128:(tb + 1) * 128, :], [xresT[tb]] if use_res else [], tb % 2, fin)

        P.emit(nc, es, None)
    return nc


def _tok_idx(s):
    return np.concatenate([np.arange(c * 512, (c + 1) * 512) for c in CHUNKS[s]])


def _const_tables(s):
    ident = np.eye(128, dtype=np.float32)
    inv = (10000.0 ** (-np.arange(0, 64, 2, dtype=np.float32) / 64)).astype(np.float32)
    rope = np.zeros((128, 2), np.float32)
    for p in range(128):
        rope[p, 0] = np.float32(inv[p % 32]) / np.float32(2 * math.pi)
        rope[p, 1] = -1.0 if (p % 64) < 32 else 1.0
    mask = np.zeros((128, 16, 512), np.float32)
    kk = np.arange(128)[:, None]
    qq = np.arange(512)[None, :]
    for par in range(2):
        gq = CHUNKS[s][par]
        for r in range(2):
            gk = CHUNKS[r][par]
            for kb in range(4):
                if gk < gq:
                    m = np.zeros((128, 512), np.float32)
                elif gk > gq:
                    m = np.full((128, 512), NEG, np.float32)
                else:
                    m = np.where(kb * 128 + kk <= qq, 0.0, NEG).astype(np.float32)
                mask[:, par * 8 + r * 4 + kb, :] = m
    sel = np.zeros((128, 8), np.float32)
    return ident, rope, mask, sel


def _sel_table(s):
    sel = np.zeros((128, 16), np.float32)
    for i in range(4):
        g = CHUNKS[s][i]
        if g == 0:
            continue
        prev = g - 1
        if prev in CHUNKS[s]:
            sel[:, 3 * i] = 1.0
        else:
            sel[:, 3 * i + 1 + (1 - s)] = 1.0
    sel[:, 12] = 0.0 if CHUNKS[s][0] == 0 else 1.0
    return sel


def kernel(**inputs):
    x = np.asarray(inputs["x"], np.float32)
    mem = np.asarray(inputs["mem"], np.float32)
    pos = np.asarray(inputs["positions"], np.int32)
    nc = build(N_LAYERS)
    shared = {k: np.ascontiguousarray(np.asarray(inputs[k], np.float32)) for k in
              ["norm_g", "mem_norm_g", "a_q_norm_g", "a_kv_norm_g", "b_sinks", "rel_bias"]}
    nA, nB = (N_LAYERS + 1) // 2, N_LAYERS // 2
    for l in range(N_LAYERS):
        shared[f"w_mem_kv_{l}"] = np.ascontiguousarray(np.asarray(inputs["w_mem_kv"][l], np.float32))
        shared[f"w_out_{l}"] = np.ascontiguousarray(np.asarray(inputs["w_out"][l], np.float32))
    for l in range(nA):
        shared[f"a_w_in_{l}"] = np.ascontiguousarray(np.asarray(inputs["a_w_in"][l], np.float32))
        shared[f"a_w_qb_{l}"] = np.ascontiguousarray(np.asarray(inputs["a_w_qb"][l], np.float32))
        shared[f"a_w_kvb_{l}"] = np.ascontiguousarray(np.asarray(inputs["a_w_kvb"][l], np.float32))
    for l in range(nB):
        shared[f"b_w_in_{l}"] = np.ascontiguousarray(np.asarray(inputs["b_w_in"][l], np.float32))
    shared["a_q_norm_g"] = np.ascontiguousarray(shared["a_q_norm_g"].reshape(2, 8, 128).transpose(0, 2, 1))
    shared["a_kv_norm_g"] = np.ascontiguousarray(shared["a_kv_norm_g"].reshape(2, 4, 128).transpose(0, 2, 1))
    shared["final_norm_g"] = np.ascontiguousarray(np.asarray(inputs["final_norm_g"], np.float32).reshape(1, D))
    need = {"a_w_in": "p1", "b_w_in": "p1", "a_w_qb": "p3", "a_w_kvb": "p4", "w_mem_kv": "mem", "w_out": "out"}
    for k in list(shared):
        for pre, ph in need.items():
            if k.startswith(pre) and ph not in RUN:
                shared[k] = np.zeros((1, 1), np.float32)
    in_maps = []
    for c in range(8):
        b, s = c // 2, c % 2
        idx = _tok_idx(s)
        ident, rope, mask, _ = _const_tables(s)
        m = dict(shared)
        m["x"] = np.ascontiguousarray(x[b][idx])
        m["mem"] = np.ascontiguousarray(mem[b])
        m["pos"] = np.ascontiguousarray(pos[b][idx].reshape(1, TOK))
        m["c_ident"] = ident
        m["c_rope"] = rope
        m["c_mask"] = mask
        m["c_sel"] = _sel_table(s)
        in_maps.append(m)
    res = run_bass_kernel_spmd(nc, in_maps, core_ids=list(range(8)))
    out = np.zeros((4, 4096, D), np.float32)
    for c in range(8):
        b, s = c // 2, c % 2
        out[b, _tok_idx(s)] = res.results[c]["out"]
    return out
```

```python
import math
import numpy as np
from contextlib import ExitStack
import concourse.bass as bass
import concourse.mybir as mybir
from concourse.bass_utils import run_bass_kernel_spmd

F32, BF16, I32 = mybir.dt.float32, mybir.dt.bfloat16, mybir.dt.int32
AF = mybir.ActivationFunctionType
ALU = mybir.AluOpType

N_LAYERS = 4
P4SUB = 7
RUN = {"p1", "p2", "p3", "p4", "p5", "mem", "out"}
D = 4096
TOK = 2048
NB = 16
EPS = 1e-6
A_IN, B_IN = 6720, 9216
NEG = -30000.0
CHUNKS = {0: [0, 3, 4, 7], 1: [1, 2, 5, 6]}
NSEM_DMA = 20


class T:
    __slots__ = ("w", "r")

    def __init__(self):
        self.w = None
        self.r = {}


def Ts(n):
    return [T() for _ in range(n)]


class Op:
    __slots__ = ("eng", "fn", "deps", "kind", "sig", "ticket", "idx", "snap", "sem", "val", "waits")


class Prog:
    ENGS = ["pe", "act", "dve", "pool", "sp"]

    def __init__(self):
        self.ops = {e: [] for e in self.ENGS}
        self.allops = []
        self.dmah = {e: [] for e in self.ENGS}

    def op(self, eng, fn, reads=(), writes=(), kind="c"):
        o = Op()
        o.eng, o.fn, o.kind, o.sig = eng, fn, kind, False
        deps = []
        for t in reads:
            if t.w is not None:
                deps.append(t.w)
        for t in writes:
            for lst in t.r.values():
                deps.extend(lst)
            if t.w is not None:
                deps.append(t.w)
        if kind != "c":
            h = self.dmah[eng]
            if len(h) >= NSEM_DMA:
                deps.append(h[-NSEM_DMA])
            h.append(o)
        o.deps = deps
        key = eng if kind == "c" else "d" + eng
        for t in reads:
            lst = t.r.get(key)
            if lst is None:
                t.r[key] = [o]
            elif kind == "c":
                lst[0] = o
            else:
                lst.append(o)
                if len(lst) > NSEM_DMA:
                    del lst[0]
        for t in writes:
            t.w = o
            t.r = {}
        o.idx = len(self.ops[eng])
        self.ops[eng].append(o)
        self.allops.append(o)
        return o

    def resolve(self):
        known = {e: {f: -1 for f in self.ENGS} for e in self.ENGS}
        kdma = {e: set() for e in self.ENGS}
        for o in self.allops:
            E = o.eng
            kn = known[E]
            waits = []
            for x in o.deps:
                if x is o:
                    continue
                if x.kind != "c":
                    if id(x) not in kdma[E]:
                        kdma[E].add(id(x))
                        waits.append(x)
                    continue
                F = x.eng
                if F == E and E == "pe":
                    continue
                if x.idx <= kn[F]:
                    continue
                x.sig = True
                waits.append(x)
                kn[F] = x.idx
                for g, v in x.snap.items():
                    if v > kn[g]:
                        kn[g] = v
            o.waits = waits
            if o.kind == "c":
                o.snap = dict(kn)
            else:
                o.snap = None

    def emit(self, nc, es, handles_of_block):
        self.resolve()
        ROT = 20000
        sems = {}
        for e in self.ENGS:
            nsig = sum(1 for o in self.ops[e] if o.kind == "c" and o.sig)
            sems[e] = [es.enter_context(nc.semaphore(f"s_{e}_{i}")) for i in range(nsig // ROT + 1)]
            c = 0
            for o in self.ops[e]:
                if o.kind == "c" and o.sig:
                    o.sem = sems[e][c // ROT]
                    o.val = c % ROT + 1
                    c += 1
            nd = len(self.dmah[e])
            if nd:
                pool = [es.enter_context(nc.semaphore(f"d_{e}_{i}")) for i in range(min(nd, NSEM_DMA))]
                cnt = [0] * len(pool)
                ncc = sum(1 for o in self.dmah[e] if o.kind == "cc")
                ccpool = [es.enter_context(nc.semaphore(f"cc_{e}_{i}")) for i in range(min(ncc, 8))]
                cccnt = [0] * len(ccpool)
                ci = 0
                for i, o in enumerate(self.dmah[e]):
                    if o.kind == "cc":
                        k = ci % len(ccpool)
                        ci += 1
                        cccnt[k] += 1
                        o.sem, o.val = ccpool[k], cccnt[k]
                        continue
                    k = i % len(pool)
                    cnt[k] += 16
                    o.sem, o.val = pool[k], cnt[k]
        block = es.enter_context(nc.Block())

        def run(ename):
            def body(e):
                for o in self.ops[ename]:
                    for x in o.waits:
                        e.wait_ge(x.sem, x.val)
                    ins = o.fn(e)
                    if o.kind == "d":
                        ins.then_inc(o.sem, 16)
                    elif o.kind == "cc":
                        ins.then_inc(o.sem)
                    elif o.sig:
                        ins.then_inc(o.sem, 1)
                for o in self.dmah[ename][-NSEM_DMA:]:
                    e.wait_ge(o.sem, o.val)
            return body

        block.tensor(run("pe"))
        block.scalar(run("act"))
        block.vector(run("dve"))
        block.gpsimd(run("pool"))
        block.sync(run("sp"))


def t5_bucket_np(dist):
    n = np.maximum(dist, 0)
    nf = np.maximum(n, 1).astype(np.float32)
    large = 16 + (np.log(nf / 16) / math.log(128 / 16) * 16).astype(np.int32)
    large = np.minimum(large, 31)
    return np.where(n < 16, n, large)


def build(n_layers):
    nc = bass.Bass("TRN2", target_bir_lowering=False)
    P = Prog()

    def din(name, shape, dt=F32):
        need = {"a_w_in": "p1", "b_w_in": "p1", "a_w_qb": "p3", "a_w_kvb": "p4", "w_mem_kv": "mem", "w_out": "out"}
        for pre, ph in need.items():
            if name.startswith(pre) and ph not in RUN:
                shape = [1, 1]
        return nc.dram_tensor(name, list(shape), dt, kind="ExternalInput").ap()

    def dscr(name, shape, dt):
        return nc.dram_tensor(name, list(shape), dt).ap()

    x_in = din("x", [TOK, D])
    mem_in = din("mem", [256, D])
    pos_in = din("pos", [1, TOK], I32)
    norm_g = din("norm_g", [4, D])
    mem_norm_g = din("mem_norm_g", [4, D])
    final_g = din("final_norm_g", [1, D])
    nA, nB = (n_layers + 1) // 2, n_layers // 2
    w_mem_kv = [din(f"w_mem_kv_{l}", [D, 2048]) for l in range(n_layers)]
    w_out = [din(f"w_out_{l}", [D, D]) for l in range(n_layers)]
    a_w_in = [din(f"a_w_in_{l}", [D, A_IN]) for l in range(nA)]
    a_qg = din("a_q_norm_g", [2, 128, 8])
    a_kvg = din("a_kv_norm_g", [2, 128, 4])
    a_w_qb = [din(f"a_w_qb_{l}", [1024, 4608]) for l in range(nA)]
    a_w_kvb = [din(f"a_w_kvb_{l}", [512, 6144]) for l in range(nA)]
    b_w_in = [din(f"b_w_in_{l}", [D, B_IN]) for l in range(nB)]
    b_sinks = din("b_sinks", [2, 48])
    rel_bias = din("rel_bias", [32, 48])
    c_ident = din("c_ident", [128, 128])
    c_rope = din("c_rope", [128, 2])
    c_mask = din("c_mask", [128, 16, 512])
    c_sel = din("c_sel", [128, 16])
    out_d = nc.dram_tensor("out", [TOK, D], F32, kind="ExternalOutput").ap()

    xres = dscr("xres", [TOK, D], F32)
    projT = dscr("projT", [B_IN, TOK], F32)
    yT_d = dscr("yT_d", [D, TOK], BF16)
    sendKV_f = [dscr(f"sendKV{p}", [128, 512], F32) for p in range(10)]
    gathKV_f = [dscr(f"gathKV{p}", [256, 512], F32) for p in range(10)]
    sendKV = [a.bitcast(BF16).rearrange("(r two) c -> r (two c)", two=2) for a in sendKV_f]
    gathKV = [a.bitcast(BF16).rearrange("(r i two) c -> r i (two c)", r=2, two=2) for a in gathKV_f]
    QnT_d = dscr("QnT_d", [24, 128, TOK], BF16)
    QrT_d = dscr("QrT_d", [12, 128, TOK], BF16)
    KnT_d = dscr("KnT_d", [24, 128, 2 * TOK], BF16)
    V_d = dscr("V_d", [24, 128, 32, 128], BF16)
    cs_d = dscr("cs_d", [2, 128, TOK], F32)
    Gd = dscr("Gd", [48, 384], F32)
    BM_d = dscr("BM_d", [2, 128, 48 * 128], F32)
    sendS = [dscr(f"sendS{p}", [128, 512], F32) for p in range(8)]
    gathS = [dscr(f"gathS{p}", [256, 512], F32) for p in range(8)]
    kvp_d = dscr("kvp_d", [1024, 512], F32)

    xresT, projTT, yTT = Ts(NB), Ts(72), Ts(32)
    sendKVT, gathKVT, csT, GdT, BMT, sendST, gathST = T(), T(), T(), T(), T(), T(), T()
    QnTT, QrTT, KnTT, VTT, kvpT = Ts(24), Ts(12), Ts(24), Ts(24), Ts(8)

    es = ExitStack()
    with es:
        def sb(name, shape, dt):
            return es.enter_context(nc.sbuf_tensor(name, list(shape), dt))

        BIG = sb("BIG", [128, 32768], BF16)
        BIGT = Ts(32)
        BIGF = BIG[:].bitcast(F32)
        WBall = sb("WBall", [128, 24576], BF16)
        WB = [WBall[:, i * 8192:(i + 1) * 8192] for i in range(3)]
        WBT = Ts(3)
        XSall = sb("XSall", [128, 8192], F32)
        XS = [XSall[:, i * 4096:(i + 1) * 4096] for i in range(2)]
        XST = Ts(2)
        AUX = sb("AUX", [128, 4096], F32)
        AUXT = T()
        MASK = sb("MASK", [128, 16, 512], BF16)
        MASKT = T()
        HB = sb("HB", [128, 4096], BF16)
        HBT = T()
        EV = [sb(f"EV{i}", [128, 1024], F32) for i in range(4)]
        EVT = Ts(4)
        PT = [sb(f"PT{i}", [128, 512], BF16) for i in range(3)]
        PTT = Ts(3)
        ident = sb("ident", [128, 128], BF16)
        ones = sb("ones", [128, 128], BF16)
        epsT = sb("epsT", [128, 1], F32)
        crope = sb("crope", [128, 2], F32)
        csel = sb("csel", [128, 16], F32)
        stat = sb("stat", [128, 8], F32)
        statT = Ts(8)
        gq = sb("gq", [128, 12], F32)
        gqT = T()
        RB = sb("RB", [32, 48], F32)
        QG = [sb(f"QG{i}", [64, 768], BF16) for i in range(2)]
        QGT = Ts(2)
        ES = sb("ES", [64, 48], F32)
        EST = T()
        constT = T()
        PS = [es.enter_context(nc.psum_tensor(f"PS{i}", [128, 512], F32)) for i in range(6)]
        PST = Ts(6)
        PSBs = [es.enter_context(nc.psum_tensor(f"PSB{i}", [128, 1024], BF16)) for i in range(2)]
        PSBT = Ts(2)

        cnt = {"ps": 0, "ev": 0, "wb": 0, "pt": 0, "sps": 0}

        def nxt(k, n):
            v = cnt[k] % n
            cnt[k] += 1
            return v

        def dma(q, out, in_, reads, writes, slow=False):
            if slow:
                return P.op(q, lambda e: e.dma_start(out=out, in_=in_, allow_slow_non_contiguous=True), reads, writes, "d")
            return P.op(q, lambda e: e.dma_start(out=out, in_=in_), reads, writes, "d")

        def mm(out, lhsT, rhs, start, stop, reads, writes):
            return P.op("pe", lambda e: e.matmul(out, lhsT=lhsT, rhs=rhs, start=start, stop=stop), reads, writes)

        def act(out, in_, func, reads, writes, scale=1.0, bias=None, accum=None):
            def f(e):
                kw = {}
                if bias is not None:
                    kw["bias"] = bias
                if accum is not None:
                    kw["accum_out"] = accum
                return e.activation(out=out, in_=in_, func=func, scale=scale, **kw)
            return P.op("act", f, reads, writes)

        def rows(r0, r1):
            return projTT[r0 // 128:(r1 - 1) // 128 + 1]

        dma("pool", ident[:], c_ident[:, :], [], [constT])
        dma("sp", crope[:], c_rope[:, :], [], [constT])
        dma("sp", csel[:], c_sel[:, :], [], [constT])
        dma("pool", MASK[:], c_mask[:, :, :], [], [MASKT])
        P.op("dve", lambda e: e.memset(ones[:], 1.0), [], [constT])
        P.op("dve", lambda e: e.memset(epsT[:], EPS), [], [constT])

        posi = XS[0][:].bitcast(I32)
        dma("sp", posi[:, 0:TOK], pos_in[0:1, :].partition_broadcast(128), [], [XST[0]])
        ang = XS[1]
        P.op("dve", lambda e: e.tensor_copy(out=ang[:, 0:TOK], in_=posi[:, 0:TOK]), [XST[0]], [XST[1]])
        P.op("dve", lambda e: e.tensor_scalar(out=ang[:, 0:TOK], in0=ang[:, 0:TOK], scalar1=crope[:, 0:1], scalar2=None, op0=ALU.mult),
             [XST[1], constT], [XST[1]])
        P.op("dve", lambda e: e.tensor_scalar(out=ang[:, TOK:2 * TOK], in0=ang[:, 0:TOK], scalar1=0.25, scalar2=None, op0=ALU.add),
             [XST[1]], [XST[1]])
        x0f = XS[0]
        for (lo, hi) in ((0, TOK), (TOK, 2 * TOK)):
            P.op("dve", lambda e, lo=lo, hi=hi: e.tensor_copy(out=posi[:, TOK:2 * TOK], in_=ang[:, lo:hi]), [XST[1]], [XST[0]])
            P.op("dve", lambda e: e.tensor_copy(out=x0f[:, 0:TOK], in_=posi[:, TOK:2 * TOK]), [XST[0]], [XST[0]])
            P.op("dve", lambda e, lo=lo, hi=hi: e.tensor_tensor(out=ang[:, lo:hi], in0=ang[:, lo:hi], in1=x0f[:, 0:TOK], op=ALU.subtract),
                 [XST[0], XST[1]], [XST[1]])
            P.op("dve", lambda e, lo=lo, hi=hi: e.scalar_tensor_tensor(out=ang[:, lo:hi], in0=ang[:, lo:hi], scalar=0.5, in1=ang[:, lo:hi],
                                                                       op0=ALU.is_gt, op1=ALU.subtract), [XST[1]], [XST[1]])
        SC = -(2.0 * math.pi - 2e-6)
        act(AUX[:, TOK:2 * TOK], ang[:, 0:TOK], AF.Sin, [XST[1]], [AUXT], scale=SC)
        act(AUX[:, 0:TOK], ang[:, TOK:2 * TOK], AF.Sin, [XST[1]], [AUXT], scale=SC)
        P.op("dve", lambda e: e.tensor_scalar(out=AUX[:, TOK:2 * TOK], in0=AUX[:, TOK:2 * TOK], scalar1=crope[:, 1:2], scalar2=None, op0=ALU.mult),
             [AUXT, constT], [AUXT])
        dma("sp", cs_d[0], AUX[:, 0:TOK], [AUXT], [csT])
        dma("sp", cs_d[1], AUX[:, TOK:2 * TOK], [AUXT], [csT])

        def load_gain(g_ap_row):
            dma("sp", AUX[:, :], g_ap_row.partition_broadcast(128), [], [AUXT])

        def norm_block(src_ap, src_reads, xi, dst_fn):
            dma("sp", XS[xi][:], src_ap, src_reads, [XST[xi]])
            act(HB[:], XS[xi][:], AF.Square, [XST[xi]], [HBT, statT[xi]], accum=stat[:, xi:xi + 1])
            act(stat[:, xi:xi + 1], stat[:, xi:xi + 1], AF.Sqrt, [statT[xi], constT], [statT[xi]], scale=1.0 / D, bias=epsT[:, 0:1])
            P.op("dve", lambda e: e.reciprocal(out=stat[:, xi:xi + 1], in_=stat[:, xi:xi + 1]), [statT[xi]], [statT[xi]])
            dst_fn(xi)

        def h_transposed(xi, tb_local, nchunk_tok):
            P.op("dve", lambda e: e.scalar_tensor_tensor(out=HB[:], in0=XS[xi][:], scalar=stat[:, xi:xi + 1], in1=AUX[:, :],
                                                         op0=ALU.mult, op1=ALU.mult), [XST[xi], statT[xi], AUXT], [HBT])
            W = nchunk_tok
            for kg in range(8):
                hb = kg % 2
                for kk in range(4):
                    k = kg * 4 + kk
                    P.op("pe", lambda e, k=k, kk=kk, hb=hb: e.transpose(out=PSBs[hb][:, kk * 128:(kk + 1) * 128],
                                                                        in_=HB[:, k * 128:(k + 1) * 128], identity=ident[:]),
                         [HBT, constT], [PSBT[hb]])
                def cp(e, kg=kg, hb=hb):
                    o = BIG[:].rearrange("p (k t) -> p k t", t=W)[:, kg * 4:(kg + 1) * 4, tb_local * 128:(tb_local + 1) * 128]
                    i = PSBs[hb][:, 0:512].rearrange("p (k t) -> p k t", t=128)
                    return e.tensor_copy(out=o, in_=i)
                wr = [BIGT[(k * W) // 1024] for k in range(kg * 4, kg * 4 + 4)]
                P.op("dve", cp, [PSBT[hb]], wr)

        def load_w(src_ap, ncols):
            wi = nxt("wb", 3)
            o = WB[wi][:, 0:32 * ncols].rearrange("p (k n) -> p k n", n=ncols)
            sv = src_ap.rearrange("(k p) n -> p k n", p=128)
            for k in range(32):
                dma("pool", o[:, k, :], sv[:, k, :], [], [WBT[wi]])
            return wi, o

        def in_proj(w_ap, E, zstart, half):
            hT = BIG[:].rearrange("p (k t) -> p k t", t=1024)
            nblk = (E + 255) // 256
            for cb in range(nblk):
                c0 = cb * 256
                ncol = min(256, E - c0)
                wi, wv = load_w(w_ap[:, c0:c0 + ncol], ncol)
                for cc in range(0, ncol, 128):
                    m = min(128, ncol - cc)
                    e0 = c0 + cc
                    for tt in range(2):
                        pi = nxt("ps", 6)
                        for k in range(32):
                            mm(PS[pi][0:m, :], wv[:, k, cc:cc + m], hT[:, k, tt * 512:(tt + 1) * 512], k == 0, k == 31,
                               [WBT[wi], BIGT[k]], [PST[pi]])
                        ei = nxt("ev", 4)
                        segs = []
                        if e0 + m <= zstart:
                            segs = [(0, m, AF.Copy)]
                        elif e0 >= zstart:
                            segs = [(0, m, AF.Silu)]
                        else:
                            segs = [(0, zstart - e0, AF.Copy), (zstart - e0, m, AF.Silu)]
                        for (p0, p1, fn) in segs:
                            act(EV[ei][p0:p1, 0:512], PS[pi][p0:p1, :], fn, [PST[pi]], [EVT[ei]])
                        t0 = half * 1024 + tt * 512
                        dma("sp", projT[e0:e0 + m, t0:t0 + 512], EV[ei][0:m, 0:512], [EVT[ei]], rows(e0, e0 + m))

        def out_proj(layer, half, src_is_input):
            yv = BIG[:].rearrange("p (k t) -> p k t", t=1024)
            for kq in range(4):
                dma("sp", yv[:, kq * 8:(kq + 1) * 8, :],
                    yT_d[kq * 1024:(kq + 1) * 1024, half * 1024:(half + 1) * 1024].rearrange("(k p) t -> p k t", p=128),
                    yTT[kq * 8:(kq + 1) * 8], BIGT[kq * 8:(kq + 1) * 8])
            xsrc = x_in if src_is_input else xres
            for db in range(8):
                if db % 2 == 0:
                    wbuf, wT = WBall[:, 0:16384], WBT[0:2]
                else:
                    wbuf, wT = XSall[:, :].bitcast(BF16), XST[0:2]
                wv = wbuf.rearrange("p (k n) -> p k n", n=512)
                sv = w_out[layer][:, db * 512:(db + 1) * 512].rearrange("(k p) n -> p k n", p=128)
                for k in range(32):
                    dma("pool", wv[:, k, :], sv[:, k, :], [], wT)
                for tb in range(8):
                    pi = nxt("ps", 6)
                    ei = nxt("ev", 4)
                    for k in range(32):
                        mm(PS[pi][:, :], yv[:, k, tb * 128:(tb + 1) * 128], wv[:, k, :], k == 0, k == 31, wT + [BIGT[k]], [PST[pi]])
                    gtb = half * 8 + tb
                    xrd = [] if src_is_input else [xresT[gtb]]
                    dma("sp", EV[ei][:, 0:512], xsrc[gtb * 128:(gtb + 1) * 128, db * 512:(db + 1) * 512], xrd, [EVT[ei]])
                    P.op("dve", lambda e, ei=ei, pi=pi: e.tensor_tensor(out=EV[ei][:, 0:512], in0=PS[pi][:, :], in1=EV[ei][:, 0:512], op=ALU.add),
                         [PST[pi], EVT[ei]], [EVT[ei]])
                    dma("sp", xres[gtb * 128:(gtb + 1) * 128, db * 512:(db + 1) * 512], EV[ei][:, 0:512], [EVT[ei]], [xresT[gtb]])

        def gate_store(o_ps, o_pst, den_ps, den_pst, zrow0, yrow0, t0, npart=128, width=512):
            e1, e2 = nxt("ev", 4), nxt("ev", 4)
            dma("sp", EV[e1][0:npart, 0:width], projT[zrow0:zrow0 + npart, t0:t0 + width], rows(zrow0, zrow0 + npart), [EVT[e1]])
            P.op("dve", lambda e: e.reciprocal(out=EV[e2][0:npart, 0:width], in_=den_ps), den_pst, [EVT[e2]])
            P.op("dve", lambda e: e.tensor_tensor(out=EV[e2][0:npart, 0:width], in0=o_ps, in1=EV[e2][0:npart, 0:width], op=ALU.mult),
                 o_pst + [EVT[e2]], [EVT[e2]])
            yb = EV[e2][0:npart, 512:1024].bitcast(BF16)[:, 0:width]
            P.op("dve", lambda e: e.tensor_tensor(out=yb, in0=EV[e2][0:npart, 0:width], in1=EV[e1][0:npart, 0:width], op=ALU.mult),
                 [EVT[e1], EVT[e2]], [EVT[e2]])
            dma("sp", yT_d[yrow0:yrow0 + npart, t0:t0 + width], yb, [EVT[e2]], yTT[yrow0 // 128:(yrow0 + npart - 1) // 128 + 1])

        def latent_norm(row0, nk, gcol0, tt, dst_fn):
            xi = nxt("xs2", 2) if False else (tt % 2)
            cv = XS[xi][:, 0:nk * 512].rearrange("p (k t) -> p k t", t=512)
            dma("sp", cv, projT[row0:row0 + nk * 128, tt * 512:(tt + 1) * 512].rearrange("(k p) t -> p k t", p=128),
                rows(row0, row0 + nk * 128), [XST[xi]])
            sq = HB[:, 0:nk * 512].rearrange("p (k t) -> p k t", t=512)
            act(sq, cv, AF.Square, [XST[xi]], [HBT])
            pi = nxt("ps", 6)
            for k in range(nk):
                mm(PS[pi][:, :], ones[:], sq[:, k, :], k == 0, k == nk - 1, [HBT, constT], [PST[pi]])
            ei = nxt("ev", 4)
            act(EV[ei][:, 0:512], PS[pi][:, :], AF.Sqrt, [PST[pi], constT], [EVT[ei]], scale=1.0 / (nk * 128), bias=epsT[:, 0:1])
            P.op("dve", lambda e: e.reciprocal(out=EV[ei][:, 0:512], in_=EV[ei][:, 0:512]), [EVT[ei]], [EVT[ei]])
            for k in range(nk):
                dst_fn(k, cv[:, k, :], EV[ei][:, 0:512], gq[:, gcol0 + k:gcol0 + k + 1], [XST[xi], EVT[ei], gqT])

        def mla_layer(layer, j):
            w_in = a_w_in[j]
            XQ0, Z0 = 1600, 2624
            dma("sp", gq[:, 0:8], a_qg[j], [], [gqT])
            dma("sp", gq[:, 8:12], a_kvg[j], [], [gqT])
            for half in range(2):
                if "p1" not in RUN:
                    break
                load_gain(norm_g[layer:layer + 1, :])
                for tbl in range(8):
                    tb = half * 8 + tbl
                    src = x_in if layer == 0 else xres
                    rd = [] if layer == 0 else [xresT[tb]]
                    norm_block(src[tb * 128:(tb + 1) * 128, :], rd, tb % 2, lambda xi, tbl=tbl: h_transposed(xi, tbl, 1024))
                in_proj(w_in, A_IN, Z0, half)
            CQN = BIG[:, 0:16384].rearrange("p (k t) -> p k t", t=TOK)
            CKV = BIG[:, 16384:32768].rearrange("p (k t) -> p k t", t=2 * TOK)
            KR = BIG[:, 24576:28672]
            if "p2" in RUN:
                dma("sp", AUX[:, 0:TOK], cs_d[0], [csT], [AUXT])
                dma("sp", AUX[:, TOK:2 * TOK], cs_d[1], [csT], [AUXT])
                CQN = BIG[:, 0:16384].rearrange("p (k t) -> p k t", t=TOK)
                for tt in range(4):
                    def put_cq(k, src, rstd, g, rd, tt=tt):
                        P.op("dve", lambda e: e.scalar_tensor_tensor(out=CQN[:, k, tt * 512:(tt + 1) * 512], in0=src, scalar=g, in1=rstd,
                                                                     op0=ALU.mult, op1=ALU.mult), rd, [BIGT[(k * TOK + tt * 512) // 1024]])
                    latent_norm(0, 8, 0, tt, put_cq)

                    def put_ckv(k, src, rstd, g, rd, tt=tt):
                        pt = nxt("pt", 3)
                        P.op("dve", lambda e: e.scalar_tensor_tensor(out=PT[pt][:, :], in0=src, scalar=g, in1=rstd,
                                                                     op0=ALU.mult, op1=ALU.mult), rd, [PTT[pt]])
                        for hh in range(2):
                            dma("sp", sendKV[2 * k + hh][:, tt * 512:(tt + 1) * 512], PT[pt][hh * 64:(hh + 1) * 64, :], [PTT[pt]], [sendKVT])
                    latent_norm(1024, 4, 8, tt, put_ckv)
                    e1, e2 = nxt("ev", 4), nxt("ev", 4)
                    tsl = slice(tt * 512, (tt + 1) * 512)
                    for hh in range(2):
                        dma("sp", EV[e1][hh * 64:(hh + 1) * 64, 0:512], projT[1536:1600, tsl], rows(1536, 1600), [EVT[e1]])
                        dma("sp", EV[e2][hh * 64:hh * 64 + 32, 0:512], projT[1568:1600, tsl], rows(1536, 1600), [EVT[e2]])
                        dma("sp", EV[e2][hh * 64 + 32:(hh + 1) * 64, 0:512], projT[1536:1568, tsl], rows(1536, 1600), [EVT[e2]])
                    P.op("dve", lambda e, e1=e1, tsl=tsl: e.tensor_tensor(out=EV[e1][:, 0:512], in0=EV[e1][:, 0:512], in1=AUX[:, tsl], op=ALU.mult),
                         [EVT[e1], AUXT], [EVT[e1]])
                    P.op("dve", lambda e, e2=e2, tt=tt: e.tensor_tensor(out=EV[e2][:, 0:512], in0=EV[e2][:, 0:512],
                                                                       in1=AUX[:, TOK + tt * 512:TOK + (tt + 1) * 512], op=ALU.mult),
                         [EVT[e2], AUXT], [EVT[e2]])
                    pt = nxt("pt", 3)
                    P.op("dve", lambda e, e1=e1, e2=e2, pt=pt: e.tensor_tensor(out=PT[pt][:, :], in0=EV[e1][:, 0:512], in1=EV[e2][:, 0:512], op=ALU.add),
                         [EVT[e1], EVT[e2]], [PTT[pt]])
                    for hh in range(2):
                        dma("sp", sendKV[8 + hh][:, tsl], PT[pt][hh * 64:(hh + 1) * 64, :], [PTT[pt]], [sendKVT])
                for pc in range(10):
                    P.op("pool", lambda e, pc=pc: e.collective_compute("AllGather", ALU.bypass, replica_groups=[[0, 1], [2, 3], [4, 5], [6, 7]],
                                                                     ins=[sendKV_f[pc][:, :]], outs=[gathKV_f[pc][:, :]]),
                         [sendKVT], [gathKVT], "cc")
            if "p3" in RUN:
                for hp in range(12):
                    wi = nxt("wb", 3)
                    wv = WB[wi][:, 0:4096].rearrange("p (k g n) -> p k g n", g=4, n=128)
                    wq = a_w_qb[j].rearrange("(k p) n -> p k n", p=128)
                    for u in range(2):
                        h = hp * 2 + u
                        dma("pool", wv[:, :, u, :], wq[:, :, h * 192:h * 192 + 128], [], [WBT[wi]])
                        dma("pool", wv[:, :, 2, u * 64:(u + 1) * 64], wq[:, :, h * 192 + 128:h * 192 + 192], [], [WBT[wi]])
                        dma("pool", wv[:, :, 3, u * 64:u * 64 + 32], wq[:, :, h * 192 + 160:h * 192 + 192], [], [WBT[wi]])
                        dma("pool", wv[:, :, 3, u * 64 + 32:(u + 1) * 64], wq[:, :, h * 192 + 128:h * 192 + 160], [], [WBT[wi]])
                    stg = [nxt("ev", 4) for _ in range(3)]
                    sv = [EV[s][:].bitcast(BF16) for s in stg]
                    for tt in range(4):
                        tsl = slice(tt * 512, (tt + 1) * 512)
                        pis = [nxt("ps", 6) for _ in range(4)]
                        for g in range(4):
                            for k in range(8):
                                mm(PS[pis[g]][:, :], wv[:, k, g, :], CQN[:, k, tsl], k == 0, k == 7,
                                   [WBT[wi], BIGT[(k * TOK + tt * 512) // 1024]], [PST[pis[g]]])
                        for u in range(2):
                            act(sv[u][:, tsl], PS[pis[u]][:, :], AF.Copy, [PST[pis[u]]], [EVT[stg[u]]])
                        x1, x2 = XS[0][:, 0:512], XS[0][:, 512:1024]
                        P.op("dve", lambda e, pi=pis[2], tsl=tsl: e.tensor_tensor(out=x1, in0=PS[pi][:, :], in1=AUX[:, tsl], op=ALU.mult),
                             [PST[pis[2]], AUXT], [XST[0]])
                        P.op("dve", lambda e, pi=pis[3], tt=tt: e.tensor_tensor(out=x2, in0=PS[pi][:, :], in1=AUX[:, TOK + tt * 512:TOK + (tt + 1) * 512], op=ALU.mult),
                             [PST[pis[3]], AUXT], [XST[0]])
                        P.op("dve", lambda e, tsl=tsl, s2=sv[2]: e.tensor_tensor(out=s2[:, tsl], in0=x1, in1=x2, op=ALU.add), [XST[0]], [EVT[stg[2]]])
                    for u in range(2):
                        dma("sp", QnT_d[hp * 2 + u], sv[u][:, 0:TOK], [EVT[stg[u]]], [QnTT[hp * 2 + u]])
                    dma("sp", QrT_d[hp], sv[2][:, 0:TOK], [EVT[stg[2]]], [QrTT[hp]])
            if "p4" in RUN:
                CKV = BIG[:, 16384:32768].rearrange("p (k t) -> p k t", t=2 * TOK)
                for r in range(2):
                    for k in range(4):
                        for hh in range(2):
                            dma("sp", CKV[hh * 64:(hh + 1) * 64, k, r * TOK:(r + 1) * TOK], gathKV[2 * k + hh][r, :, :], [gathKVT],
                                BIGT[16 + k * 4 + r * 2:16 + k * 4 + r * 2 + 2])
                wkv = a_w_kvb[j].rearrange("(k p) (h n) -> p k h n", p=128, n=256)
                for hg in range(6):
                    wi = nxt("wb", 3)
                    wk = WB[wi][:, 0:2048].rearrange("p (k h n) -> p k h n", h=4, n=128)
                    wvv = WB[wi][:, 2048:4096].rearrange("p (k h n) -> p k h n", h=4, n=128)
                    for k in range(4):
                        dma("pool", wk[:, k], wkv[:, k, hg * 4:(hg + 1) * 4, 0:128], [], [WBT[wi]])
                        dma("pool", wvv[:, k], wkv[:, k, hg * 4:(hg + 1) * 4, 128:256], [], [WBT[wi]])
                    for u in range(4):
                        if not (P4SUB & 2):
                            break
                        h = hg * 4 + u
                        for th in range(2):
                            ei = nxt("ev", 4)
                            stv = EV[ei][:].bitcast(BF16)
                            for t4 in range(4):
                                t0 = th * 2048 + t4 * 512
                                pi = nxt("ps", 6)
                                for k in range(4):
                                    mm(PS[pi][:, :], wk[:, k, u, :], CKV[:, k, t0:t0 + 512], k == 0, k == 3,
                                       [WBT[wi], BIGT[16 + k * 4 + t0 // 1024]], [PST[pi]])
                                act(stv[:, t4 * 512:(t4 + 1) * 512], PS[pi][:, :], AF.Copy, [PST[pi]], [EVT[ei]])
                            dma("sp", KnT_d[h, :, th * 2048:(th + 1) * 2048], stv[:, 0:2048], [EVT[ei]], [KnTT[h]])
                    for kb4 in range(8):
                        if not (P4SUB & 4):
                            break
                        ei = nxt("ev", 4)
                        stv = EV[ei][:].bitcast(BF16).rearrange("p (b n) -> p b n", n=512)
                        for b_ in range(4):
                            kb = kb4 * 4 + b_
                            pi = nxt("ps", 6)
                            for k in range(4):
                                mm(PS[pi][:, :], CKV[:, k, kb * 128:(kb + 1) * 128], WB[wi][:, 2048 + k * 512:2048 + (k + 1) * 512],
                                   k == 0, k == 3, [WBT[wi], BIGT[16 + k * 4 + (kb * 128) // 1024]], [PST[pi]])
                            if True:
                                act(stv[:, b_, :], PS[pi][:, :], AF.Copy, [PST[pi]], [EVT[ei]])
                            else:
                                P.op("dve", lambda e, b_=b_, pi=pi, stv=stv: e.tensor_copy(out=stv[:, b_, :], in_=PS[pi][:, :]), [PST[pi]], [EVT[ei]])
                        for u in range(4):
                            if P4SUB & 16:
                                break
                            dma("sp", V_d[hg * 4 + u, :, kb4 * 4:(kb4 + 1) * 4, :], stv[:, :, u * 128:(u + 1) * 128], [EVT[ei]], [VTT[hg * 4 + u]])
            if "p5" in RUN:
                KR = BIG[:, 24576:28672]
                for r in range(2):
                    for hh in range(2):
                        dma("sp", KR[hh * 64:(hh + 1) * 64, r * TOK:(r + 1) * TOK], gathKV[8 + hh][r, :, :], [gathKVT], BIGT[24 + r * 2:24 + r * 2 + 2])
                scale = 192.0 ** -0.5
                for h in range(24):
                    b2 = h % 2
                    hp = h // 2
                    KN = BIG[:, b2 * 4096:(b2 + 1) * 4096]
                    KNT_ = BIGT[b2 * 4:(b2 + 1) * 4]
                    VH = BIG[:, 8192 + b2 * 4096:8192 + (b2 + 1) * 4096].rearrange("p (b n) -> p b n", n=128)
                    VHT_ = BIGT[8 + b2 * 4:8 + (b2 + 1) * 4]
                    QN = BIG[:, 16384 + b2 * 2048:16384 + (b2 + 1) * 2048]
                    QNT_ = BIGT[16 + b2 * 2:16 + (b2 + 1) * 2]
                    QR = BIG[:, 20480 + (hp % 2) * 2048:20480 + (hp % 2 + 1) * 2048]
                    QRT_ = BIGT[20 + (hp % 2) * 2:20 + (hp % 2 + 1) * 2]
                    dma("sp", KN, KnT_d[h], [KnTT[h]], KNT_)
                    dma("sp", VH, V_d[h], [VTT[h]], VHT_)
                    dma("sp", QN, QnT_d[h], [QnTT[h]], QNT_)
                    if b2 == 0:
                        dma("sp", QR, QrT_d[hp], [QrTT[hp]], QRT_)
                    ro = b2 * 64
                    for J in range(4):
                        qs = slice(J * 512, (J + 1) * 512)
                        tiles = [(c, r, kb) for c in range(J + 1) for r in range(2) for kb in range(4)]
                        gidx = h * 4 + J
                        opi, dpi = (2, 3) if gidx % 2 == 0 else (4, 5)
                        pend = None
                        n = len(tiles)

                        def qk(ti):
                            c, r, kb = tiles[ti]
                            kbi = r * 16 + c * 4 + kb
                            ks = slice(kbi * 128, (kbi + 1) * 128)
                            pi = nxt("sps", 2)
                            kt = [KNT_[(kbi * 128) // 1024]]
                            last_is_mask = (c == J)
                            mm(PS[pi][:, :], KN[:, ks], QN[:, qs], True, False, kt + [QNT_[J // 2]], [PST[pi]])
                            mm(PS[pi][:, :], KR[ro:ro + 64, ks], QR[ro:ro + 64, qs], False, not last_is_mask,
                               [BIGT[24 + (kbi * 128) // 1024], QRT_[J // 2]], [PST[pi]])
                            if last_is_mask:
                                mm(PS[pi][:, :], ident[:], MASK[:, (J % 2) * 8 + r * 4 + kb, :], False, True, [constT, MASKT], [PST[pi]])
                            pt = nxt("pt", 3)
                            act(PT[pt][:, :], PS[pi][:, :], AF.Exp, [PST[pi]], [PTT[pt]], scale=scale)
                            return (pt, kbi)

                        def pv(ti, st):
                            pt, kbi = st
                            mm(PS[opi][:, :], VH[:, kbi, :], PT[pt][:, :], ti == 0, ti == n - 1, [VHT_[(kbi * 128) // 1024], PTT[pt]], [PST[opi]])
                            mm(PS[dpi][:, :], ones[:], PT[pt][:, :], ti == 0, ti == n - 1, [constT, PTT[pt]], [PST[dpi]])

                        for ti in range(n):
                            st = qk(ti)
                            if pend is not None:
                                pv(ti - 1, pend)
                            pend = st
                        pv(n - 1, pend)
                        gate_store(PS[opi][:, :], [PST[opi]], PS[dpi][:, :], [PST[dpi]], Z0 + h * 128, h * 128, J * 512)
            if "mem" in RUN:
                mem_attention(layer, XQ0, Z0 + 3072)
            if "out" in RUN:
                for half in range(2):
                    out_proj(layer, half, layer == 0)

        def mem_attention(layer, XQ0, ZM0):
            load_gain(mem_norm_g[layer:layer + 1, :])
            MN = BIG[:, 0:8192].rearrange("p (k t) -> p k t", t=256)
            for mb in range(2):
                def tr(xi, mb=mb):
                    P.op("dve", lambda e: e.scalar_tensor_tensor(out=HB[:], in0=XS[xi][:], scalar=stat[:, xi:xi + 1], in1=AUX[:, :],
                                                                 op0=ALU.mult, op1=ALU.mult), [XST[xi], statT[xi], AUXT], [HBT])
                    for kg in range(8):
                        hb = kg % 2
                        for kk in range(4):
                            k = kg * 4 + kk
                            P.op("pe", lambda e, k=k, kk=kk, hb=hb: e.transpose(out=PSBs[hb][:, kk * 128:(kk + 1) * 128],
                                                                                in_=HB[:, k * 128:(k + 1) * 128], identity=ident[:]),
                                 [HBT, constT], [PSBT[hb]])
                        P.op("dve", lambda e, kg=kg, hb=hb: e.tensor_copy(out=MN[:, kg * 4:(kg + 1) * 4, mb * 128:(mb + 1) * 128],
                                                                         in_=PSBs[hb][:, 0:512].rearrange("p (k t) -> p k t", t=128)),
                             [PSBT[hb]], [BIGT[kg]])
                norm_block(mem_in[mb * 128:(mb + 1) * 128, :], [], mb, tr)
            MKT = BIG[:, 8192:10240].rearrange("p (c m) -> p c m", m=256)
            MV = BIG[:, 10240:12288].rearrange("p (c n) -> p c n", n=1024)
            for cb in range(8):
                wi, wv = load_w(w_mem_kv[layer][:, cb * 256:(cb + 1) * 256], 256)
                if cb < 4:
                    for cc in range(2):
                        c = cb * 2 + cc
                        pi = nxt("ps", 6)
                        for k in range(32):
                            mm(PS[pi][:, 0:256], wv[:, k, cc * 128:(cc + 1) * 128], MN[:, k, :], k == 0, k == 31, [WBT[wi], BIGT[k // 4]], [PST[pi]])
                        act(MKT[:, c, :], PS[pi][:, 0:256], AF.Copy, [PST[pi]], [BIGT[8 + c // 4]])
                else:
                    n0 = (cb - 4) * 256
                    for mc in range(2):
                        pi = nxt("ps", 6)
                        for k in range(32):
                            mm(PS[pi][:, 0:256], MN[:, k, mc * 128:(mc + 1) * 128], wv[:, k, :], k == 0, k == 31, [WBT[wi], BIGT[k // 4]], [PST[pi]])
                        act(MV[:, mc, n0:n0 + 256], PS[pi][:, 0:256], AF.Copy, [PST[pi]], [BIGT[10 + mc]])
            XQ = BIG[:, 16384:32768].rearrange("p (c t) -> p c t", t=TOK)
            for c in range(8):
                dma("pool", XQ[:, c, :], projT[XQ0 + c * 128:XQ0 + (c + 1) * 128, :], rows(XQ0 + c * 128, XQ0 + (c + 1) * 128),
                    BIGT[16 + c * 2:16 + c * 2 + 2])
            for hx in range(4):
                for tt in range(4):
                    tsl = slice(tt * 512, (tt + 1) * 512)
                    pts = []
                    for mc in range(2):
                        pi = nxt("ps", 6)
                        for dc in range(2):
                            c = hx * 2 + dc
                            mm(PS[pi][:, :], MKT[:, c, mc * 128:(mc + 1) * 128], XQ[:, c, tsl], dc == 0, dc == 1,
                               [BIGT[8 + c // 4], BIGT[16 + c * 2 + tt // 2]], [PST[pi]])
                        pt = nxt("pt", 3)
                        act(PT[pt][:, :], PS[pi][:, :], AF.Exp, [PST[pi]], [PTT[pt]], scale=1.0 / 16.0)
                        pts.append(pt)
                    dpi = nxt("ps", 6)
                    for mc in range(2):
                        mm(PS[dpi][:, :], ones[:], PT[pts[mc]][:, :], mc == 0, mc == 1, [constT, PTT[pts[mc]]], [PST[dpi]])
                    for dvc in range(2):
                        opi = nxt("ps", 6)
                        for mc in range(2):
                            mm(PS[opi][:, :], MV[:, mc, hx * 256 + dvc * 128:hx * 256 + (dvc + 1) * 128], PT[pts[mc]][:, :], mc == 0, mc == 1,
                               [BIGT[10 + mc], PTT[pts[mc]]], [PST[opi]])
                        gate_store(PS[opi][:, :], [PST[opi]], PS[dpi][:, :], [PST[dpi]], ZM0 + hx * 256 + dvc * 128,
                                   3072 + hx * 256 + dvc * 128, tt * 512)

        swa_state = {"bm": False}

        def swa_setup():
            dist = np.arange(128)
            bucket = t5_bucket_np(dist)
            gfill = XS[0][0:48, 0:384]
            P.op("dve", lambda e: e.memset(gfill, NEG), [], [XST[0]])
            dma("sp", Gd[:, :], gfill, [XST[0]], [GdT])
            dma("sp", RB[:], rel_bias[:, :], [], [constT])
            for d in range(128):
                b = int(bucket[d])
                dma("sp", Gd[:, 127 + d:128 + d].rearrange("h o -> o h"), RB[b:b + 1, :], [constT], [GdT], slow=True)
            BMv = BIGF[:, 0:6144].rearrange("p (h q) -> p h q", q=128)
            for which in range(2):
                for k in range(128):
                    off = (127 - k) if which == 0 else (255 - k)
                    dma("sp", BMv[k:k + 1, :, :], Gd[:, off:off + 128].rearrange("(o h) q -> o h q", o=1), [GdT], BIGT[0:12])
                dma("sp", BM_d[which], BIGF[:, 0:6144], BIGT[0:12], [BMT])

        def swa_layer(layer, j):
            w_in = b_w_in[j]
            K0, V0, XQ0, Z0 = 3072, 3584, 4096, 5120
            if not swa_state["bm"]:
                swa_setup()
                swa_state["bm"] = True
            for half in range(2):
                load_gain(norm_g[layer:layer + 1, :])
                for tbl in range(8):
                    tb = half * 8 + tbl
                    norm_block(xres[tb * 128:(tb + 1) * 128, :], [xresT[tb]], tb % 2, lambda xi, tbl=tbl: h_transposed(xi, tbl, 1024))
                in_proj(w_in, B_IN, Z0, half)
            for i in range(4):
                ei = nxt("ev", 4)
                for kq in range(8):
                    dma("sp", EV[ei][:, kq * 128:(kq + 1) * 128], projT[K0 + kq * 128:K0 + (kq + 1) * 128, (4 * i + 3) * 128:(4 * i + 4) * 128],
                        rows(K0 + kq * 128, K0 + (kq + 1) * 128), [EVT[ei]])
                for kq in range(8):
                    dma("sp", sendS[kq][:, i * 128:(i + 1) * 128], EV[ei][:, kq * 128:(kq + 1) * 128], [EVT[ei]], [sendST])
            for pc in range(8):
                P.op("pool", lambda e, pc=pc: e.collective_compute("AllGather", ALU.bypass, replica_groups=[[0, 1], [2, 3], [4, 5], [6, 7]],
                                                                 ins=[sendS[pc][:, :]], outs=[gathS[pc][:, :]]),
                     [sendST], [gathST], "cc")
            for i in range(4):
                e_o, e_a, e_b = nxt("ev", 4), nxt("ev", 4), nxt("ev", 4)
                if i > 0:
                    for kq in range(8):
                        dma("sp", EV[e_o][:, kq * 128:(kq + 1) * 128], projT[K0 + kq * 128:K0 + (kq + 1) * 128, (4 * i - 1) * 128:(4 * i) * 128],
                            rows(K0 + kq * 128, K0 + (kq + 1) * 128), [EVT[e_o]])
                else:
                    P.op("dve", lambda e, e_o=e_o: e.memset(EV[e_o][:, :], 0.0), [], [EVT[e_o]])
                P.op("dve", lambda e, e_o=e_o, i=i: e.tensor_scalar(out=EV[e_o][:, :], in0=EV[e_o][:, :], scalar1=csel[:, 3 * i:3 * i + 1], scalar2=None, op0=ALU.mult),
                     [EVT[e_o], constT], [EVT[e_o]])
                for r, e_r in ((0, e_a), (1, e_b)):
                    for kq in range(8):
                        dma("sp", EV[e_r][:, kq * 128:(kq + 1) * 128], gathS[kq][r * 128:(r + 1) * 128, i * 128:(i + 1) * 128], [gathST], [EVT[e_r]])
                    P.op("dve", lambda e, e_o=e_o, e_r=e_r, i=i, r=r: e.scalar_tensor_tensor(out=EV[e_o][:, :], in0=EV[e_r][:, :], scalar=csel[:, 3 * i + 1 + r:3 * i + 2 + r],
                                                                                          in1=EV[e_o][:, :], op0=ALU.mult, op1=ALU.add),
                         [EVT[e_o], EVT[e_r], constT], [EVT[e_o]])
                for kq in range(8):
                    dma("sp", kvp_d[kq * 128:(kq + 1) * 128, i * 128:(i + 1) * 128], EV[e_o][:, kq * 128:(kq + 1) * 128], [EVT[e_o]], [kvpT[kq]])
            BMv = BIGF[:, 0:12288].rearrange("p (w h q) -> p w h q", w=2, q=128)
            for which in range(2):
                dma("sp", BIGF[:, which * 6144:(which + 1) * 6144], BM_d[which], [BMT], BIGT[which * 12:(which + 1) * 12])
            dma("sp", ES[:, :], b_sinks[j:j + 1, :].partition_broadcast(64), [], [EST])
            act(ES[:, :], ES[:, :], AF.Exp, [EST], [EST])
            sc = 0.125
            KC = BIG[0:64, 24576:26624]
            VCt = BIG[0:64, 26624:28672]
            KB = BIG[0:64, 28672:29184]
            VBt = BIG[0:64, 29696:30208]
            VC = HB[:, 0:1024].rearrange("p (b n) -> p b n", n=64)
            VB = HB[:, 1024:1280].rearrange("p (b n) -> p b n", n=64)
            for g in range(8):
                dma("pool", KC, projT[K0 + g * 64:K0 + (g + 1) * 64, :], rows(K0 + g * 64, K0 + (g + 1) * 64), BIGT[24:26])
                dma("pool", VCt, projT[V0 + g * 64:V0 + (g + 1) * 64, :], rows(V0 + g * 64, V0 + (g + 1) * 64), BIGT[26:28])
                dma("pool", KB, kvp_d[g * 64:(g + 1) * 64, :], [kvpT[g // 2]], [BIGT[28]])
                dma("pool", VBt, kvp_d[512 + g * 64:512 + (g + 1) * 64, :], [kvpT[4 + g // 2]], [BIGT[29]])
                for hb in range(2):
                    for bb in range(8):
                        b_ = hb * 8 + bb
                        P.op("pe", lambda e, b_=b_, bb=bb, hb=hb: e.transpose(out=PSBs[hb][:, bb * 64:(bb + 1) * 64],
                                                                           in_=VCt[:, b_ * 128:(b_ + 1) * 128], identity=ident[0:64, 0:64]),
                             [BIGT[26 + b_ // 8], constT], [PSBT[hb]])
                    P.op("dve", lambda e, hb=hb: e.tensor_copy(out=HB[:, hb * 512:(hb + 1) * 512], in_=PSBs[hb][:, 0:512]), [PSBT[hb]], [HBT])
                for bb in range(4):
                    P.op("pe", lambda e, bb=bb: e.transpose(out=PSBs[0][:, bb * 64:(bb + 1) * 64], in_=VBt[:, bb * 128:(bb + 1) * 128],
                                                          identity=ident[0:64, 0:64]), [BIGT[29], constT], [PSBT[0]])
                P.op("dve", lambda e: e.tensor_copy(out=HB[:, 1024:1280], in_=PSBs[0][:, 0:256]), [PSBT[0]], [HBT])
                for jb in range(NB):
                    qsl = slice(jb * 128, (jb + 1) * 128)
                    if jb % 4 == 0:
                        i = jb // 4
                        kprev, kprevT = KB[:, i * 128:(i + 1) * 128], BIGT[28]
                        vprev = VB[:, i, :]
                    else:
                        kprev, kprevT = KC[:, (jb - 1) * 128:jb * 128], BIGT[24 + (jb - 1) // 8]
                        vprev = VC[:, jb - 1, :]
                    qi = jb % 2
                    QGv = QG[qi][:, :].rearrange("p (h q) -> p h q", q=128)
                    dma("pool", QGv, projT[g * 384:(g + 1) * 384, qsl].rearrange("(h p) q -> p h q", p=64), rows(g * 384, (g + 1) * 384), [QGT[qi]])
                    for hf in range(2):
                        h0 = g * 6 + hf * 3
                        qrhs = QGv[:, hf * 3:hf * 3 + 3, :]
                        p_prev, p_cur = nxt("ps", 6), nxt("ps", 6)
                        mm(PS[p_prev][:, 0:384], kprev, qrhs, True, True, [kprevT, QGT[qi]], [PST[p_prev]])
                        mm(PS[p_cur][:, 0:384], KC[:, qsl], qrhs, True, True, [BIGT[24 + jb // 8], QGT[qi]], [PST[p_cur]])
                        pts = []
                        for which, pp in ((1, p_prev), (0, p_cur)):
                            e_t = nxt("ev", 4)
                            P.op("dve", lambda e, pp=pp, e_t=e_t, which=which, h0=h0: e.scalar_tensor_tensor(
                                out=EV[e_t][:, 0:384].rearrange("p (h q) -> p h q", q=128), in0=PS[pp][:, 0:384].rearrange("p (h q) -> p h q", q=128),
                                scalar=sc, in1=BMv[:, which, h0:h0 + 3, :], op0=ALU.mult, op1=ALU.add),
                                [PST[pp]] + BIGT[which * 12:(which + 1) * 12], [EVT[e_t]])
                            pt = nxt("pt", 3)
                            act(PT[pt][:, 0:384], EV[e_t][:, 0:384], AF.Exp, [EVT[e_t]], [PTT[pt]])
                            pts.append(pt)
                        opi, dpi = nxt("ps", 6), nxt("ps", 6)
                        mm(PS[opi][0:64, 0:384], vprev, PT[pts[0]][:, 0:384], True, False, [HBT, PTT[pts[0]]], [PST[opi]])
                        mm(PS[opi][0:64, 0:384], VC[:, jb, :], PT[pts[1]][:, 0:384], False, True, [HBT, PTT[pts[1]]], [PST[opi]])
                        mm(PS[dpi][0:64, 0:384], ONP_l[jb], PT[pts[0]][:, 0:384], True, False, [constT, PTT[pts[0]]], [PST[dpi]])
                        mm(PS[dpi][0:64, 0:384], ones[:, 0:64], PT[pts[1]][:, 0:384], False, True, [constT, PTT[pts[1]]], [PST[dpi]])
                        e_d = nxt("ev", 4)
                        for u in range(3):
                            P.op("dve", lambda e, u=u, e_d=e_d, dpi=dpi, h0=h0: e.tensor_scalar(
                                out=EV[e_d][0:64, u * 128:(u + 1) * 128], in0=PS[dpi][0:64, u * 128:(u + 1) * 128],
                                scalar1=ES[:, h0 + u:h0 + u + 1], scalar2=None, op0=ALU.add), [PST[dpi], EST], [EVT[e_d]])
                        P.op("dve", lambda e, e_d=e_d: e.reciprocal(out=EV[e_d][0:64, 0:384], in_=EV[e_d][0:64, 0:384]), [EVT[e_d]], [EVT[e_d]])
                        P.op("dve", lambda e, e_d=e_d, opi=opi: e.tensor_tensor(out=EV[e_d][0:64, 0:384], in0=PS[opi][0:64, 0:384], in1=EV[e_d][0:64, 0:384], op=ALU.mult),
                             [PST[opi], EVT[e_d]], [EVT[e_d]])
                        e_z = nxt("ev", 4)
                        zv = EV[e_z][0:64, 0:384].rearrange("p (h q) -> p h q", q=128)
                        dma("sp", zv, projT[Z0 + h0 * 64:Z0 + (h0 + 3) * 64, qsl].rearrange("(h p) q -> p h q", p=64), rows(Z0 + h0 * 64, Z0 + (h0 + 3) * 64), [EVT[e_z]])
                        yb = EV[e_d][0:64, 512:1024].bitcast(BF16)[:, 0:384]
                        P.op("dve", lambda e, e_d=e_d, e_z=e_z, yb=yb: e.tensor_tensor(out=yb, in0=EV[e_d][0:64, 0:384], in1=EV[e_z][0:64, 0:384], op=ALU.mult),
                             [EVT[e_d], EVT[e_z]], [EVT[e_d]])
                        dma("sp", yT_d[h0 * 64:(h0 + 3) * 64, qsl].rearrange("(h p) q -> p h q", p=64), yb.rearrange("p (h q) -> p h q", q=128),
                            [EVT[e_d]], yTT[(h0 * 64) // 128:((h0 + 3) * 64 - 1) // 128 + 1])
            mem_attention(layer, XQ0, Z0 + 3072)
            for half in range(2):
                out_proj(layer, half, False)

        onp0 = sb("onp0", [128, 64], BF16)
        P.op("dve", lambda e: e.memset(onp0[:], 1.0), [], [constT])
        P.op("dve", lambda e: e.tensor_scalar(out=onp0[:], in0=onp0[:], scalar1=csel[:, 12:13], scalar2=None, op0=ALU.mult), [constT], [constT])
        ONP_l = [onp0[:, :]] + [ones[:, 0:64]] * (NB - 1)

        for layer in range(n_layers):
            if layer % 2 == 0:
                mla_layer(layer, layer // 2)
            else:
                swa_layer(layer, layer // 2)
        load_gain(final_g[0:1, :])
        for tb in range(NB):
            def fin(xi, tb=tb):
                P.op("dve", lambda e: e.scalar_tensor_tensor(out=XS[xi][:], in0=XS[xi][:], scalar=stat[:, xi:xi + 1], in1=AUX[:, :],
                                                             op0=ALU.mult, op1=ALU.mult), [XST[xi], statT[xi], AUXT], [XST[xi]])
                dma("sp", out_d[tb * 128:(tb + 1) * 128, :], XS[xi][:], [XST[xi]], [])
            use_res = n_layers > 0 and "out" in RUN
            src = xres if use_res else x_in
            norm_block(src[tb * 128:(tb + 1) * 128, :], [xresT[tb]] if use_res else [], tb % 2, fin)

        P.emit(nc, es, None)
    return nc


def _tok_idx(s):
    return np.concatenate([np.arange(c * 512, (c + 1) * 512) for c in CHUNKS[s]])


def _const_tables(s):
    ident = np.eye(128, dtype=np.float32)
    inv = (10000.0 ** (-np.arange(0, 64, 2, dtype=np.float32) / 64)).astype(np.float32)
    rope = np.zeros((128, 2), np.float32)
    for p in range(128):
        rope[p, 0] = np.float32(inv[p % 32]) / np.float32(2 * math.pi)
        rope[p, 1] = -1.0 if (p % 64) < 32 else 1.0
    mask = np.zeros((128, 16, 512), np.float32)
    kk = np.arange(128)[:, None]
    qq = np.arange(512)[None, :]
    for par in range(2):
        gq = CHUNKS[s][par]
        for r in range(2):
            gk = CHUNKS[r][par]
            for kb in range(4):
                if gk < gq:
                    m = np.zeros((128, 512), np.float32)
                elif gk > gq:
                    m = np.full((128, 512), NEG, np.float32)
                else:
                    m = np.where(kb * 128 + kk <= qq, 0.0, NEG).astype(np.float32)
                mask[:, par * 8 + r * 4 + kb, :] = m
    sel = np.zeros((128, 8), np.float32)
    return ident, rope, mask, sel


def _sel_table(s):
    sel = np.zeros((128, 16), np.float32)
    for i in range(4):
        g = CHUNKS[s][i]
        if g == 0:
            continue
        prev = g - 1
        if prev in CHUNKS[s]:
            sel[:, 3 * i] = 1.0
        else:
            sel[:, 3 * i + 1 + (1 - s)] = 1.0
    sel[:, 12] = 0.0 if CHUNKS[s][0] == 0 else 1.0
    return sel


def kernel(**inputs):
    x = np.asarray(inputs["x"], np.float32)
    mem = np.asarray(inputs["mem"], np.float32)
    pos = np.asarray(inputs["positions"], np.int32)
    nc = build(N_LAYERS)
    shared = {k: np.ascontiguousarray(np.asarray(inputs[k], np.float32)) for k in
              ["norm_g", "mem_norm_g", "a_q_norm_g", "a_kv_norm_g", "b_sinks", "rel_bias"]}
    nA, nB = (N_LAYERS + 1) // 2, N_LAYERS // 2
    for l in range(N_LAYERS):
        shared[f"w_mem_kv_{l}"] = np.ascontiguousarray(np.asarray(inputs["w_mem_kv"][l], np.float32))
        shared[f"w_out_{l}"] = np.ascontiguousarray(np.asarray(inputs["w_out"][l], np.float32))
    for l in range(nA):
        shared[f"a_w_in_{l}"] = np.ascontiguousarray(np.asarray(inputs["a_w_in"][l], np.float32))
        shared[f"a_w_qb_{l}"] = np.ascontiguousarray(np.asarray(inputs["a_w_qb"][l], np.float32))
        shared[f"a_w_kvb_{l}"] = np.ascontiguousarray(np.asarray(inputs["a_w_kvb"][l], np.float32))
    for l in range(nB):
        shared[f"b_w_in_{l}"] = np.ascontiguousarray(np.asarray(inputs["b_w_in"][l], np.float32))
    shared["a_q_norm_g"] = np.ascontiguousarray(shared["a_q_norm_g"].reshape(2, 8, 128).transpose(0, 2, 1))
    shared["a_kv_norm_g"] = np.ascontiguousarray(shared["a_kv_norm_g"].reshape(2, 4, 128).transpose(0, 2, 1))
    shared["final_norm_g"] = np.ascontiguousarray(np.asarray(inputs["final_norm_g"], np.float32).reshape(1, D))
    need = {"a_w_in": "p1", "b_w_in": "p1", "a_w_qb": "p3", "a_w_kvb": "p4", "w_mem_kv": "mem", "w_out": "out"}
    for k in list(shared):
        for pre, ph in need.items():
            if k.startswith(pre) and ph not in RUN:
                shared[k] = np.zeros((1, 1), np.float32)
    in_maps = []
    for c in range(8):
        b, s = c // 2, c % 2
        idx = _tok_idx(s)
        ident, rope, mask, _ = _const_tables(s)
        m = dict(shared)
        m["x"] = np.ascontiguousarray(x[b][idx])
        m["mem"] = np.ascontiguousarray(mem[b])
        m["pos"] = np.ascontiguousarray(pos[b][idx].reshape(1, TOK))
        m["c_ident"] = ident
        m["c_rope"] = rope
        m["c_mask"] = mask
        m["c_sel"] = _sel_table(s)
        in_maps.append(m)
    res = run_bass_kernel_spmd(nc, in_maps, core_ids=list(range(8)))
    out = np.zeros((4, 4096, D), np.float32)
    for c in range(8):
        b, s = c // 2, c % 2
        out[b, _tok_idx(s)] = res.results[c]["out"]
    return out
```

```python
import math
import numpy as np
from contextlib import ExitStack
import concourse.bass as bass
import concourse.mybir as mybir
from concourse.bass_utils import run_bass_kernel_spmd

F32, BF16, I32 = mybir.dt.float32, mybir.dt.bfloat16, mybir.dt.int32
AF = mybir.ActivationFunctionType
ALU = mybir.AluOpType

N_LAYERS = 4
P4SUB = 7
RUN = {"p1", "p2", "p3", "p4", "p5", "mem", "out"}
D = 4096
TOK = 2048
NB = 16
EPS = 1e-6
A_IN, B_IN = 6720, 9216
NEG = -30000.0
CHUNKS = {0: [0, 3, 4, 7], 1: [1, 2, 5, 6]}
NSEM_DMA = 20


class T:
    __slots__ = ("w", "r")

    def __init__(self):
        self.w = None
        self.r = {}


def Ts(n):
    return [T() for _ in range(n)]


class Op:
    __slots__ = ("eng", "fn", "deps", "kind", "sig", "ticket", "idx", "snap", "sem", "val", "waits")


class Prog:
    ENGS = ["pe", "act", "dve", "pool", "sp"]

    def __init__(self):
        self.ops = {e: [] for e in self.ENGS}
        self.allops = []
        self.dmah = {e: [] for e in self.ENGS}

    def op(self, eng, fn, reads=(), writes=(), kind="c"):
        o = Op()
        o.eng, o.fn, o.kind, o.sig = eng, fn, kind, False
        deps = []
        for t in reads:
            if t.w is not None:
                deps.append(t.w)
        for t in writes:
            for lst in t.r.values():
                deps.extend(lst)
            if t.w is not None:
                deps.append(t.w)
        if kind != "c":
            h = self.dmah[eng]
            if len(h) >= NSEM_DMA:
                deps.append(h[-NSEM_DMA])
            h.append(o)
        o.deps = deps
        key = eng if kind == "c" else "d" + eng
        for t in reads:
            lst = t.r.get(key)
            if lst is None:
                t.r[key] = [o]
            elif kind == "c":
                lst[0] = o
            else:
                lst.append(o)
                if len(lst) > NSEM_DMA:
                    del lst[0]
        for t in writes:
            t.w = o
            t.r = {}
        o.idx = len(self.ops[eng])
        self.ops[eng].append(o)
        self.allops.append(o)
        return o

    def resolve(self):
        known = {e: {f: -1 for f in self.ENGS} for e in self.ENGS}
        kdma = {e: set() for e in self.ENGS}
        for o in self.allops:
            E = o.eng
            kn = known[E]
            waits = []
            for x in o.deps:
                if x is o:
                    continue
                if x.kind != "c":
                    if id(x) not in kdma[E]:
                        kdma[E].add(id(x))
                        waits.append(x)
                    continue
                F = x.eng
                if F == E and E == "pe":
                    continue
                if x.idx <= kn[F]:
                    continue
                x.sig = True
                waits.append(x)
                kn[F] = x.idx
                for g, v in x.snap.items():
                    if v > kn[g]:
                        kn[g] = v
            o.waits = waits
            if o.kind == "c":
                o.snap = dict(kn)
            else:
                o.snap = None

    def emit(self, nc, es, handles_of_block):
        self.resolve()
        ROT = 20000
        sems = {}
        for e in self.ENGS:
            nsig = sum(1 for o in self.ops[e] if o.kind == "c" and o.sig)
            sems[e] = [es.enter_context(nc.semaphore(f"s_{e}_{i}")) for i in range(nsig // ROT + 1)]
            c = 0
            for o in self.ops[e]:
                if o.kind == "c" and o.sig:
                    o.sem = sems[e][c // ROT]
                    o.val = c % ROT + 1
                    c += 1
            nd = len(self.dmah[e])
            if nd:
                pool = [es.enter_context(nc.semaphore(f"d_{e}_{i}")) for i in range(min(nd, NSEM_DMA))]
                cnt = [0] * len(pool)
                ncc = sum(1 for o in self.dmah[e] if o.kind == "cc")
                ccpool = [es.enter_context(nc.semaphore(f"cc_{e}_{i}")) for i in range(min(ncc, 8))]
                cccnt = [0] * len(ccpool)
                ci = 0
                for i, o in enumerate(self.dmah[e]):
                    if o.kind == "cc":
                        k = ci % len(ccpool)
                        ci += 1
                        cccnt[k] += 1
                        o.sem, o.val = ccpool[k], cccnt[k]
                        continue
                    k = i % len(pool)
                    cnt[k] += 16
                    o.sem, o.val = pool[k], cnt[k]
        block = es.enter_context(nc.Block())

        def run(ename):
            def body(e):
                for o in self.ops[ename]:
                    for x in o.waits:
                        e.wait_ge(x.sem, x.val)
                    ins = o.fn(e)
                    if o.kind == "d":
                        ins.then_inc(o.sem, 16)
                    elif o.kind == "cc":
                        ins.then_inc(o.sem)
                    elif o.sig:
                        ins.then_inc(o.sem, 1)
                for o in self.dmah[ename][-NSEM_DMA:]:
                    e.wait_ge(o.sem, o.val)
            return body

        block.tensor(run("pe"))
        block.scalar(run("act"))
        block.vector(run("dve"))
        block.gpsimd(run("pool"))
        block.sync(run("sp"))


def t5_bucket_np(dist):
    n = np.maximum(dist, 0)
    nf = np.maximum(n, 1).astype(np.float32)
    large = 16 + (np.log(nf / 16) / math.log(128 / 16) * 16).astype(np.int32)
    large = np.minimum(large, 31)
    return np.where(n < 16, n, large)


def build(n_layers):
    nc = bass.Bass("TRN2", target_bir_lowering=False)
    P = Prog()

    def din(name, shape, dt=F32):
        need = {"a_w_in": "p1", "b_w_in": "p1", "a_w_qb": "p3", "a_w_kvb": "p4", "w_mem_kv": "mem", "w_out": "out"}
        for pre, ph in need.items():
            if name.startswith(pre) and ph not in RUN:
                shape = [1, 1]
        return nc.dram_tensor(name, list(shape), dt, kind="ExternalInput").ap()

    def dscr(name, shape, dt):
        return nc.dram_tensor(name, list(shape), dt).ap()

    x_in = din("x", [TOK, D])
    mem_in = din("mem", [256, D])
    pos_in = din("pos", [1, TOK], I32)
    norm_g = din("norm_g", [4, D])
    mem_norm_g = din("mem_norm_g", [4, D])
    final_g = din("final_norm_g", [1, D])
    nA, nB = (n_layers + 1) // 2, n_layers // 2
    w_mem_kv = [din(f"w_mem_kv_{l}", [D, 2048]) for l in range(n_layers)]
    w_out = [din(f"w_out_{l}", [D, D]) for l in range(n_layers)]
    a_w_in = [din(f"a_w_in_{l}", [D, A_IN]) for l in range(nA)]
    a_qg = din("a_q_norm_g", [2, 128, 8])
    a_kvg = din("a_kv_norm_g", [2, 128, 4])
    a_w_qb = [din(f"a_w_qb_{l}", [1024, 4608]) for l in range(nA)]
    a_w_kvb = [din(f"a_w_kvb_{l}", [512, 6144]) for l in range(nA)]
    b_w_in = [din(f"b_w_in_{l}", [D, B_IN]) for l in range(nB)]
    b_sinks = din("b_sinks", [2, 48])
    rel_bias = din("rel_bias", [32, 48])
    c_ident = din("c_ident", [128, 128])
    c_rope = din("c_rope", [128, 2])
    c_mask = din("c_mask", [128, 16, 512])
    c_sel = din("c_sel", [128, 16])
    out_d = nc.dram_tensor("out", [TOK, D], F32, kind="ExternalOutput").ap()

    xres = dscr("xres", [TOK, D], F32)
    projT = dscr("projT", [B_IN, TOK], F32)
    yT_d = dscr("yT_d", [D, TOK], BF16)
    sendKV_f = [dscr(f"sendKV{p}", [128, 512], F32) for p in range(10)]
    gathKV_f = [dscr(f"gathKV{p}", [256, 512], F32) for p in range(10)]
    sendKV = [a.bitcast(BF16).rearrange("(r two) c -> r (two c)", two=2) for a in sendKV_f]
    gathKV = [a.bitcast(BF16).rearrange("(r i two) c -> r i (two c)", r=2, two=2) for a in gathKV_f]
    QnT_d = dscr("QnT_d", [24, 128, TOK], BF16)
    QrT_d = dscr("QrT_d", [12, 128, TOK], BF16)
    KnT_d = dscr("KnT_d", [24, 128, 2 * TOK], BF16)
    V_d = dscr("V_d", [24, 128, 32, 128], BF16)
    cs_d = dscr("cs_d", [2, 128, TOK], F32)
    Gd = dscr("Gd", [48, 384], F32)
    BM_d = dscr("BM_d", [2, 128, 48 * 128], F32)
    sendS = [dscr(f"sendS{p}", [128, 512], F32) for p in range(8)]
    gathS = [dscr(f"gathS{p}", [256, 512], F32) for p in range(8)]
    kvp_d = dscr("kvp_d", [1024, 512], F32)

    xresT, projTT, yTT = Ts(NB), Ts(72), Ts(32)
    sendKVT, gathKVT, csT, GdT, BMT, sendST, gathST = T(), T(), T(), T(), T(), T(), T()
    QnTT, QrTT, KnTT, VTT, kvpT = Ts(24), Ts(12), Ts(24), Ts(24), Ts(8)

    es = ExitStack()
    with es:
        def sb(name, shape, dt):
            return es.enter_context(nc.sbuf_tensor(name, list(shape), dt))

        BIG = sb("BIG", [128, 32768], BF16)
        BIGT = Ts(32)
        BIGF = BIG[:].bitcast(F32)
        WBall = sb("WBall", [128, 24576], BF16)
        WB = [WBall[:, i * 8192:(i + 1) * 8192] for i in range(3)]
        WBT = Ts(3)
        XSall = sb("XSall", [128, 8192], F32)
        XS = [XSall[:, i * 4096:(i + 1) * 4096] for i in range(2)]
        XST = Ts(2)
        AUX = sb("AUX", [128, 4096], F32)
        AUXT = T()
        MASK = sb("MASK", [128, 16, 512], BF16)
        MASKT = T()
        HB = sb("HB", [128, 4096], BF16)
        HBT = T()
        EV = [sb(f"EV{i}", [128, 1024], F32) for i in range(4)]
        EVT = Ts(4)
        PT = [sb(f"PT{i}", [128, 512], BF16) for i in range(3)]
        PTT = Ts(3)
        ident = sb("ident", [128, 128], BF16)
        ones = sb("ones", [128, 128], BF16)
        epsT = sb("epsT", [128, 1], F32)
        crope = sb("crope", [128, 2], F32)
        csel = sb("csel", [128, 16], F32)
        stat = sb("stat", [128, 8], F32)
        statT = Ts(8)
        gq = sb("gq", [128, 12], F32)
        gqT = T()
        RB = sb("RB", [32, 48], F32)
        QG = [sb(f"QG{i}", [64, 768], BF16) for i in range(2)]
        QGT = Ts(2)
        ES = sb("ES", [64, 48], F32)
        EST = T()
        constT = T()
        PS = [es.enter_context(nc.psum_tensor(f"PS{i}", [128, 512], F32)) for i in range(6)]
        PST = Ts(6)
        PSBs = [es.enter_context(nc.psum_tensor(f"PSB{i}", [128, 1024], BF16)) for i in range(2)]
        PSBT = Ts(2)

        cnt = {"ps": 0, "ev": 0, "wb": 0, "pt": 0, "sps": 0}

        def nxt(k, n):
            v = cnt[k] % n
            cnt[k] += 1
            return v

        def dma(q, out, in_, reads, writes, slow=False):
            if slow:
                return P.op(q, lambda e: e.dma_start(out=out, in_=in_, allow_slow_non_contiguous=True), reads, writes, "d")
            return P.op(q, lambda e: e.dma_start(out=out, in_=in_), reads, writes, "d")

        def mm(out, lhsT, rhs, start, stop, reads, writes):
            return P.op("pe", lambda e: e.matmul(out, lhsT=lhsT, rhs=rhs, start=start, stop=stop), reads, writes)

        def act(out, in_, func, reads, writes, scale=1.0, bias=None, accum=None):
            def f(e):
                kw = {}
                if bias is not None:
                    kw["bias"] = bias
                if accum is not None:
                    kw["accum_out"] = accum
                return e.activation(out=out, in_=in_, func=func, scale=scale, **kw)
            return P.op("act", f, reads, writes)

        def rows(r0, r1):
            return projTT[r0 // 128:(r1 - 1) // 128 + 1]

        dma("pool", ident[:], c_ident[:, :], [], [constT])
        dma("sp", crope[:], c_rope[:, :], [], [constT])
        dma("sp", csel[:], c_sel[:, :], [], [constT])
        dma("pool", MASK[:], c_mask[:, :, :], [], [MASKT])
        P.op("dve", lambda e: e.memset(ones[:], 1.0), [], [constT])
        P.op("dve", lambda e: e.memset(epsT[:], EPS), [], [constT])

        posi = XS[0][:].bitcast(I32)
        dma("sp", posi[:, 0:TOK], pos_in[0:1, :].partition_broadcast(128), [], [XST[0]])
        ang = XS[1]
        P.op("dve", lambda e: e.tensor_copy(out=ang[:, 0:TOK], in_=posi[:, 0:TOK]), [XST[0]], [XST[1]])
        P.op("dve", lambda e: e.tensor_scalar(out=ang[:, 0:TOK], in0=ang[:, 0:TOK], scalar1=crope[:, 0:1], scalar2=None, op0=ALU.mult),
             [XST[1], constT], [XST[1]])
        P.op("dve", lambda e: e.tensor_scalar(out=ang[:, TOK:2 * TOK], in0=ang[:, 0:TOK], scalar1=0.25, scalar2=None, op0=ALU.add),
             [XST[1]], [XST[1]])
        x0f = XS[0]
        for (lo, hi) in ((0, TOK), (TOK, 2 * TOK)):
            P.op("dve", lambda e, lo=lo, hi=hi: e.tensor_copy(out=posi[:, TOK:2 * TOK], in_=ang[:, lo:hi]), [XST[1]], [XST[0]])
            P.op("dve", lambda e: e.tensor_copy(out=x0f[:, 0:TOK], in_=posi[:, TOK:2 * TOK]), [XST[0]], [XST[0]])
            P.op("dve", lambda e, lo=lo, hi=hi: e.tensor_tensor(out=ang[:, lo:hi], in0=ang[:, lo:hi], in1=x0f[:, 0:TOK], op=ALU.subtract),
                 [XST[0], XST[1]], [XST[1]])
            P.op("dve", lambda e, lo=lo, hi=hi: e.scalar_tensor_tensor(out=ang[:, lo:hi], in0=ang[:, lo:hi], scalar=0.5, in1=ang[:, lo:hi],
                                                                       op0=ALU.is_gt, op1=ALU.subtract), [XST[1]], [XST[1]])
        SC = -(2.0 * math.pi - 2e-6)
        act(AUX[:, TOK:2 * TOK], ang[:, 0:TOK], AF.Sin, [XST[1]], [AUXT], scale=SC)
        act(AUX[:, 0:TOK], ang[:, TOK:2 * TOK], AF.Sin, [XST[1]], [AUXT], scale=SC)
        P.op("dve", lambda e: e.tensor_scalar(out=AUX[:, TOK:2 * TOK], in0=AUX[:, TOK:2 * TOK], scalar1=crope[:, 1:2], scalar2=None, op0=ALU.mult),
             [AUXT, constT], [AUXT])
        dma("sp", cs_d[0], AUX[:, 0:TOK], [AUXT], [csT])
        dma("sp", cs_d[1], AUX[:, TOK:2 * TOK], [AUXT], [csT])

        def load_gain(g_ap_row):
            dma("sp", AUX[:, :], g_ap_row.partition_broadcast(128), [], [AUXT])

        def norm_block(src_ap, src_reads, xi, dst_fn):
            dma("sp", XS[xi][:], src_ap, src_reads, [XST[xi]])
            act(HB[:], XS[xi][:], AF.Square, [XST[xi]], [HBT, statT[xi]], accum=stat[:, xi:xi + 1])
            act(stat[:, xi:xi + 1], stat[:, xi:xi + 1], AF.Sqrt, [statT[xi], constT], [statT[xi]], scale=1.0 / D, bias=epsT[:, 0:1])
            P.op("dve", lambda e: e.reciprocal(out=stat[:, xi:xi + 1], in_=stat[:, xi:xi + 1]), [statT[xi]], [statT[xi]])
            dst_fn(xi)

        def h_transposed(xi, tb_local, nchunk_tok):
            P.op("dve", lambda e: e.scalar_tensor_tensor(out=HB[:], in0=XS[xi][:], scalar=stat[:, xi:xi + 1], in1=AUX[:, :],
                                                         op0=ALU.mult, op1=ALU.mult), [XST[xi], statT[xi], AUXT], [HBT])
            W = nchunk_tok
            for kg in range(8):
                hb = kg % 2
                for kk in range(4):
                    k = kg * 4 + kk
                    P.op("pe", lambda e, k=k, kk=kk, hb=hb: e.transpose(out=PSBs[hb][:, kk * 128:(kk + 1) * 128],
                                                                        in_=HB[:, k * 128:(k + 1) * 128], identity=ident[:]),
                         [HBT, constT], [PSBT[hb]])
                def cp(e, kg=kg, hb=hb):
                    o = BIG[:].rearrange("p (k t) -> p k t", t=W)[:, kg * 4:(kg + 1) * 4, tb_local * 128:(tb_local + 1) * 128]
                    i = PSBs[hb][:, 0:512].rearrange("p (k t) -> p k t", t=128)
                    return e.tensor_copy(out=o, in_=i)
                wr = [BIGT[(k * W) // 1024] for k in range(kg * 4, kg * 4 + 4)]
                P.op("dve", cp, [PSBT[hb]], wr)

        def load_w(src_ap, ncols):
            wi = nxt("wb", 3)
            o = WB[wi][:, 0:32 * ncols].rearrange("p (k n) -> p k n", n=ncols)
            sv = src_ap.rearrange("(k p) n -> p k n", p=128)
            for kq in range(8):
                dma("pool", o[:, kq * 4:(kq + 1) * 4, :], sv[:, kq * 4:(kq + 1) * 4, :], [], [WBT[wi]])
            return wi, o

        def in_proj(w_ap, E, zstart, half):
            hT = BIG[:].rearrange("p (k t) -> p k t", t=1024)
            nblk = (E + 511) // 512
            for cb in range(nblk):
                c0 = cb * 512
                ncol = min(512, E - c0)
                if cb % 2 == 0:
                    wbuf, wT = WBall[:, 0:16384], WBT[0:2]
                else:
                    wbuf, wT = XSall[:, :].bitcast(BF16), XST[0:2]
                wv = wbuf[:, 0:32 * ncol].rearrange("p (k n) -> p k n", n=ncol)
                sv = w_ap[:, c0:c0 + ncol].rearrange("(k p) n -> p k n", p=128)
                for kq in range(8):
                    dma("pool", wv[:, kq * 4:(kq + 1) * 4, :], sv[:, kq * 4:(kq + 1) * 4, :], [], wT)
                for cc in range(0, ncol, 128):
                    m = min(128, ncol - cc)
                    e0 = c0 + cc
                    for tt in range(2):
                        pi = nxt("ps", 6)
                        for k in range(32):
                            mm(PS[pi][0:m, :], wv[:, k, cc:cc + m], hT[:, k, tt * 512:(tt + 1) * 512], k == 0, k == 31,
                               wT + [BIGT[k]], [PST[pi]])
                        ei = nxt("ev", 4)
                        segs = []
                        if e0 + m <= zstart:
                            segs = [(0, m, AF.Copy)]
                        elif e0 >= zstart:
                            segs = [(0, m, AF.Silu)]
                        else:
                            segs = [(0, zstart - e0, AF.Copy), (zstart - e0, m, AF.Silu)]
                        for (p0, p1, fn) in segs:
                            act(EV[ei][p0:p1, 0:512], PS[pi][p0:p1, :], fn, [PST[pi]], [EVT[ei]])
                        t0 = half * 1024 + tt * 512
                        dma("sp", projT[e0:e0 + m, t0:t0 + 512], EV[ei][0:m, 0:512], [EVT[ei]], rows(e0, e0 + m))

        def out_proj(layer, half, src_is_input):
            yv = BIG[:].rearrange("p (k t) -> p k t", t=1024)
            for kq in range(4):
                dma("sp", yv[:, kq * 8:(kq + 1) * 8, :],
                    yT_d[kq * 1024:(kq + 1) * 1024, half * 1024:(half + 1) * 1024].rearrange("(k p) t -> p k t", p=128),
                    yTT[kq * 8:(kq + 1) * 8], BIGT[kq * 8:(kq + 1) * 8])
            xsrc = x_in if src_is_input else xres
            for db in range(8):
                if db % 2 == 0:
                    wbuf, wT = WBall[:, 0:16384], WBT[0:2]
                else:
                    wbuf, wT = XSall[:, :].bitcast(BF16), XST[0:2]
                wv = wbuf.rearrange("p (k n) -> p k n", n=512)
                sv = w_out[layer][:, db * 512:(db + 1) * 512].rearrange("(k p) n -> p k n", p=128)
                for kq in range(8):
                    dma("pool", wv[:, kq * 4:(kq + 1) * 4, :], sv[:, kq * 4:(kq + 1) * 4, :], [], wT)
                for tb in range(8):
                    pi = nxt("ps", 6)
                    ei = nxt("ev", 4)
                    for k in range(32):
                        mm(PS[pi][:, :], yv[:, k, tb * 128:(tb + 1) * 128], wv[:, k, :], k == 0, k == 31, wT + [BIGT[k]], [PST[pi]])
                    gtb = half * 8 + tb
                    xrd = [] if src_is_input else [xresT[gtb]]
                    dma("sp", EV[ei][:, 0:512], xsrc[gtb * 128:(gtb + 1) * 128, db * 512:(db + 1) * 512], xrd, [EVT[ei]])
                    P.op("dve", lambda e, ei=ei, pi=pi: e.tensor_tensor(out=EV[ei][:, 0:512], in0=PS[pi][:, :], in1=EV[ei][:, 0:512], op=ALU.add),
                         [PST[pi], EVT[ei]], [EVT[ei]])
                    dma("sp", xres[gtb * 128:(gtb + 1) * 128, db * 512:(db + 1) * 512], EV[ei][:, 0:512], [EVT[ei]], [xresT[gtb]])

        def gate_store(o_ps, o_pst, den_ps, den_pst, zrow0, yrow0, t0, npart=128, width=512):
            e1, e2 = nxt("ev", 4), nxt("ev", 4)
            dma("sp", EV[e1][0:npart, 0:width], projT[zrow0:zrow0 + npart, t0:t0 + width], rows(zrow0, zrow0 + npart), [EVT[e1]])
            P.op("dve", lambda e: e.reciprocal(out=EV[e2][0:npart, 0:width], in_=den_ps), den_pst, [EVT[e2]])
            P.op("dve", lambda e: e.tensor_tensor(out=EV[e2][0:npart, 0:width], in0=o_ps, in1=EV[e2][0:npart, 0:width], op=ALU.mult),
                 o_pst + [EVT[e2]], [EVT[e2]])
            yb = EV[e2][0:npart, 512:1024].bitcast(BF16)[:, 0:width]
            P.op("dve", lambda e: e.tensor_tensor(out=yb, in0=EV[e2][0:npart, 0:width], in1=EV[e1][0:npart, 0:width], op=ALU.mult),
                 [EVT[e1], EVT[e2]], [EVT[e2]])
            dma("sp", yT_d[yrow0:yrow0 + npart, t0:t0 + width], yb, [EVT[e2]], yTT[yrow0 // 128:(yrow0 + npart - 1) // 128 + 1])

        def latent_norm(row0, nk, gcol0, tt, dst_fn):
            xi = nxt("xs2", 2) if False else (tt % 2)
            cv = XS[xi][:, 0:nk * 512].rearrange("p (k t) -> p k t", t=512)
            dma("sp", cv, projT[row0:row0 + nk * 128, tt * 512:(tt + 1) * 512].rearrange("(k p) t -> p k t", p=128),
                rows(row0, row0 + nk * 128), [XST[xi]])
            sq = HB[:, 0:nk * 512].rearrange("p (k t) -> p k t", t=512)
            act(sq, cv, AF.Square, [XST[xi]], [HBT])
            pi = nxt("ps", 6)
            for k in range(nk):
                mm(PS[pi][:, :], ones[:], sq[:, k, :], k == 0, k == nk - 1, [HBT, constT], [PST[pi]])
            ei = nxt("ev", 4)
            act(EV[ei][:, 0:512], PS[pi][:, :], AF.Sqrt, [PST[pi], constT], [EVT[ei]], scale=1.0 / (nk * 128), bias=epsT[:, 0:1])
            P.op("dve", lambda e: e.reciprocal(out=EV[ei][:, 0:512], in_=EV[ei][:, 0:512]), [EVT[ei]], [EVT[ei]])
            for k in range(nk):
                dst_fn(k, cv[:, k, :], EV[ei][:, 0:512], gq[:, gcol0 + k:gcol0 + k + 1], [XST[xi], EVT[ei], gqT])

        def mla_layer(layer, j):
            w_in = a_w_in[j]
            XQ0, Z0 = 1600, 2624
            dma("sp", gq[:, 0:8], a_qg[j], [], [gqT])
            dma("sp", gq[:, 8:12], a_kvg[j], [], [gqT])
            for half in range(2):
                if "p1" not in RUN:
                    break
                load_gain(norm_g[layer:layer + 1, :])
                for tbl in range(8):
                    tb = half * 8 + tbl
                    src = x_in if layer == 0 else xres
                    rd = [] if layer == 0 else [xresT[tb]]
                    norm_block(src[tb * 128:(tb + 1) * 128, :], rd, tb % 2, lambda xi, tbl=tbl: h_transposed(xi, tbl, 1024))
                in_proj(w_in, A_IN, Z0, half)
            CQN = BIG[:, 0:16384].rearrange("p (k t) -> p k t", t=TOK)
            CKV = BIG[:, 16384:32768].rearrange("p (k t) -> p k t", t=2 * TOK)
            KR = BIG[:, 24576:28672]
            if "p2" in RUN:
                dma("sp", AUX[:, 0:TOK], cs_d[0], [csT], [AUXT])
                dma("sp", AUX[:, TOK:2 * TOK], cs_d[1], [csT], [AUXT])
                CQN = BIG[:, 0:16384].rearrange("p (k t) -> p k t", t=TOK)
                for tt in range(4):
                    def put_cq(k, src, rstd, g, rd, tt=tt):
                        P.op("dve", lambda e: e.scalar_tensor_tensor(out=CQN[:, k, tt * 512:(tt + 1) * 512], in0=src, scalar=g, in1=rstd,
                                                                     op0=ALU.mult, op1=ALU.mult), rd, [BIGT[(k * TOK + tt * 512) // 1024]])
                    latent_norm(0, 8, 0, tt, put_cq)

                    def put_ckv(k, src, rstd, g, rd, tt=tt):
                        pt = nxt("pt", 3)
                        P.op("dve", lambda e: e.scalar_tensor_tensor(out=PT[pt][:, :], in0=src, scalar=g, in1=rstd,
                                                                     op0=ALU.mult, op1=ALU.mult), rd, [PTT[pt]])
                        for hh in range(2):
                            dma("sp", sendKV[2 * k + hh][:, tt * 512:(tt + 1) * 512], PT[pt][hh * 64:(hh + 1) * 64, :], [PTT[pt]], [sendKVT])
                    latent_norm(1024, 4, 8, tt, put_ckv)
                    e1, e2 = nxt("ev", 4), nxt("ev", 4)
                    tsl = slice(tt * 512, (tt + 1) * 512)
                    for hh in range(2):
                        dma("sp", EV[e1][hh * 64:(hh + 1) * 64, 0:512], projT[1536:1600, tsl], rows(1536, 1600), [EVT[e1]])
                        dma("sp", EV[e2][hh * 64:hh * 64 + 32, 0:512], projT[1568:1600, tsl], rows(1536, 1600), [EVT[e2]])
                        dma("sp", EV[e2][hh * 64 + 32:(hh + 1) * 64, 0:512], projT[1536:1568, tsl], rows(1536, 1600), [EVT[e2]])
                    P.op("dve", lambda e, e1=e1, tsl=tsl: e.tensor_tensor(out=EV[e1][:, 0:512], in0=EV[e1][:, 0:512], in1=AUX[:, tsl], op=ALU.mult),
                         [EVT[e1], AUXT], [EVT[e1]])
                    P.op("dve", lambda e, e2=e2, tt=tt: e.tensor_tensor(out=EV[e2][:, 0:512], in0=EV[e2][:, 0:512],
                                                                       in1=AUX[:, TOK + tt * 512:TOK + (tt + 1) * 512], op=ALU.mult),
                         [EVT[e2], AUXT], [EVT[e2]])
                    pt = nxt("pt", 3)
                    P.op("dve", lambda e, e1=e1, e2=e2, pt=pt: e.tensor_tensor(out=PT[pt][:, :], in0=EV[e1][:, 0:512], in1=EV[e2][:, 0:512], op=ALU.add),
                         [EVT[e1], EVT[e2]], [PTT[pt]])
                    for hh in range(2):
                        dma("sp", sendKV[8 + hh][:, tsl], PT[pt][hh * 64:(hh + 1) * 64, :], [PTT[pt]], [sendKVT])
                for pc in range(10):
                    P.op("pool", lambda e, pc=pc: e.collective_compute("AllGather", ALU.bypass, replica_groups=[[0, 1], [2, 3], [4, 5], [6, 7]],
                                                                     ins=[sendKV_f[pc][:, :]], outs=[gathKV_f[pc][:, :]]),
                         [sendKVT], [gathKVT], "cc")
            if "p3" in RUN:
                for hp in range(12):
                    wi = nxt("wb", 3)
                    wv = WB[wi][:, 0:4096].rearrange("p (k g n) -> p k g n", g=4, n=128)
                    wq = a_w_qb[j].rearrange("(k p) n -> p k n", p=128)
                    for u in range(2):
                        h = hp * 2 + u
                        dma("pool", wv[:, :, u, :], wq[:, :, h * 192:h * 192 + 128], [], [WBT[wi]])
                        dma("pool", wv[:, :, 2, u * 64:(u + 1) * 64], wq[:, :, h * 192 + 128:h * 192 + 192], [], [WBT[wi]])
                        dma("pool", wv[:, :, 3, u * 64:u * 64 + 32], wq[:, :, h * 192 + 160:h * 192 + 192], [], [WBT[wi]])
                        dma("pool", wv[:, :, 3, u * 64 + 32:(u + 1) * 64], wq[:, :, h * 192 + 128:h * 192 + 160], [], [WBT[wi]])
                    stg = [nxt("ev", 4) for _ in range(3)]
                    sv = [EV[s][:].bitcast(BF16) for s in stg]
                    for tt in range(4):
                        tsl = slice(tt * 512, (tt + 1) * 512)
                        pis = [nxt("ps", 6) for _ in range(4)]
                        for g in range(4):
                            for k in range(8):
                                mm(PS[pis[g]][:, :], wv[:, k, g, :], CQN[:, k, tsl], k == 0, k == 7,
                                   [WBT[wi], BIGT[(k * TOK + tt * 512) // 1024]], [PST[pis[g]]])
                        for u in range(2):
                            act(sv[u][:, tsl], PS[pis[u]][:, :], AF.Copy, [PST[pis[u]]], [EVT[stg[u]]])
                        x1, x2 = XS[0][:, 0:512], XS[0][:, 512:1024]
                        P.op("dve", lambda e, pi=pis[2], tsl=tsl: e.tensor_tensor(out=x1, in0=PS[pi][:, :], in1=AUX[:, tsl], op=ALU.mult),
                             [PST[pis[2]], AUXT], [XST[0]])
                        P.op("dve", lambda e, pi=pis[3], tt=tt: e.tensor_tensor(out=x2, in0=PS[pi][:, :], in1=AUX[:, TOK + tt * 512:TOK + (tt + 1) * 512], op=ALU.mult),
                             [PST[pis[3]], AUXT], [XST[0]])
                        P.op("dve", lambda e, tsl=tsl, s2=sv[2]: e.tensor_tensor(out=s2[:, tsl], in0=x1, in1=x2, op=ALU.add), [XST[0]], [EVT[stg[2]]])
                    for u in range(2):
                        dma("sp", QnT_d[hp * 2 + u], sv[u][:, 0:TOK], [EVT[stg[u]]], [QnTT[hp * 2 + u]])
                    dma("sp", QrT_d[hp], sv[2][:, 0:TOK], [EVT[stg[2]]], [QrTT[hp]])
            if "p4" in RUN:
                CKV = BIG[:, 16384:32768].rearrange("p (k t) -> p k t", t=2 * TOK)
                for r in range(2):
                    for k in range(4):
                        for hh in range(2):
                            dma("sp", CKV[hh * 64:(hh + 1) * 64, k, r * TOK:(r + 1) * TOK], gathKV[2 * k + hh][r, :, :], [gathKVT],
                                BIGT[16 + k * 4 + r * 2:16 + k * 4 + r * 2 + 2])
                wkv = a_w_kvb[j].rearrange("(k p) (h n) -> p k h n", p=128, n=256)
                for hg in range(6):
                    wi = nxt("wb", 3)
                    wk = WB[wi][:, 0:2048].rearrange("p (k h n) -> p k h n", h=4, n=128)
                    wvv = WB[wi][:, 2048:4096].rearrange("p (k h n) -> p k h n", h=4, n=128)
                    for k in range(4):
                        dma("pool", wk[:, k], wkv[:, k, hg * 4:(hg + 1) * 4, 0:128], [], [WBT[wi]])
                        dma("pool", wvv[:, k], wkv[:, k, hg * 4:(hg + 1) * 4, 128:256], [], [WBT[wi]])
                    for u in range(4):
                        if not (P4SUB & 2):
                            break
                        h = hg * 4 + u
                        for th in range(2):
                            ei = nxt("ev", 4)
                            stv = EV[ei][:].bitcast(BF16)
                            for t4 in range(4):
                                t0 = th * 2048 + t4 * 512
                                pi = nxt("ps", 6)
                                for k in range(4):
                                    mm(PS[pi][:, :], wk[:, k, u, :], CKV[:, k, t0:t0 + 512], k == 0, k == 3,
                                       [WBT[wi], BIGT[16 + k * 4 + t0 // 1024]], [PST[pi]])
                                act(stv[:, t4 * 512:(t4 + 1) * 512], PS[pi][:, :], AF.Copy, [PST[pi]], [EVT[ei]])
                            dma("sp", KnT_d[h, :, th * 2048:(th + 1) * 2048], stv[:, 0:2048], [EVT[ei]], [KnTT[h]])
                    for kb4 in range(8):
                        if not (P4SUB & 4):
                            break
                        ei = nxt("ev", 4)
                        stv = EV[ei][:].bitcast(BF16).rearrange("p (b n) -> p b n", n=512)
                        for b_ in range(4):
                            kb = kb4 * 4 + b_
                            pi = nxt("ps", 6)
                            for k in range(4):
                                mm(PS[pi][:, :], CKV[:, k, kb * 128:(kb + 1) * 128], WB[wi][:, 2048 + k * 512:2048 + (k + 1) * 512],
                                   k == 0, k == 3, [WBT[wi], BIGT[16 + k * 4 + (kb * 128) // 1024]], [PST[pi]])
                            if True:
                                act(stv[:, b_, :], PS[pi][:, :], AF.Copy, [PST[pi]], [EVT[ei]])
                            else:
                                P.op("dve", lambda e, b_=b_, pi=pi, stv=stv: e.tensor_copy(out=stv[:, b_, :], in_=PS[pi][:, :]), [PST[pi]], [EVT[ei]])
                        for u in range(4):
                            if P4SUB & 16:
                                break
                            dma("sp", V_d[hg * 4 + u, :, kb4 * 4:(kb4 + 1) * 4, :], stv[:, :, u * 128:(u + 1) * 128], [EVT[ei]], [VTT[hg * 4 + u]])
            if "p5" in RUN:
                KR = BIG[:, 24576:28672]
                for r in range(2):
                    for hh in range(2):
                        dma("sp", KR[hh * 64:(hh + 1) * 64, r * TOK:(r + 1) * TOK], gathKV[8 + hh][r, :, :], [gathKVT], BIGT[24 + r * 2:24 + r * 2 + 2])
                scale = 192.0 ** -0.5
                for h in range(24):
                    b2 = h % 2
                    hp = h // 2
                    KN = BIG[:, b2 * 4096:(b2 + 1) * 4096]
                    KNT_ = BIGT[b2 * 4:(b2 + 1) * 4]
                    VH = BIG[:, 8192 + b2 * 4096:8192 + (b2 + 1) * 4096].rearrange("p (b n) -> p b n", n=128)
                    VHT_ = BIGT[8 + b2 * 4:8 + (b2 + 1) * 4]
                    QN = BIG[:, 16384 + b2 * 2048:16384 + (b2 + 1) * 2048]
                    QNT_ = BIGT[16 + b2 * 2:16 + (b2 + 1) * 2]
                    QR = BIG[:, 20480 + (hp % 2) * 2048:20480 + (hp % 2 + 1) * 2048]
                    QRT_ = BIGT[20 + (hp % 2) * 2:20 + (hp % 2 + 1) * 2]
                    dma("sp", KN, KnT_d[h], [KnTT[h]], KNT_)
                    dma("sp", VH, V_d[h], [VTT[h]], VHT_)
                    dma("sp", QN, QnT_d[h], [QnTT[h]], QNT_)
                    if b2 == 0:
                        dma("sp", QR, QrT_d[hp], [QrTT[hp]], QRT_)
                    ro = b2 * 64
                    for J in range(4):
                        qs = slice(J * 512, (J + 1) * 512)
                        tiles = [(c, r, kb) for c in range(J + 1) for r in range(2) for kb in range(4)]
                        gidx = h * 4 + J
                        opi, dpi = (2, 3) if gidx % 2 == 0 else (4, 5)
                        pend = None
                        n = len(tiles)

                        def qk(ti):
                            c, r, kb = tiles[ti]
                            kbi = r * 16 + c * 4 + kb
                            ks = slice(kbi * 128, (kbi + 1) * 128)
                            pi = nxt("sps", 2)
                            kt = [KNT_[(kbi * 128) // 1024]]
                            last_is_mask = (c == J)
                            mm(PS[pi][:, :], KN[:, ks], QN[:, qs], True, False, kt + [QNT_[J // 2]], [PST[pi]])
                            mm(PS[pi][:, :], KR[ro:ro + 64, ks], QR[ro:ro + 64, qs], False, not last_is_mask,
                               [BIGT[24 + (kbi * 128) // 1024], QRT_[J // 2]], [PST[pi]])
                            if last_is_mask:
                                mm(PS[pi][:, :], ident[:], MASK[:, (J % 2) * 8 + r * 4 + kb, :], False, True, [constT, MASKT], [PST[pi]])
                            pt = nxt("pt", 3)
                            act(PT[pt][:, :], PS[pi][:, :], AF.Exp, [PST[pi]], [PTT[pt]], scale=scale)
                            return (pt, kbi)

                        def pv(ti, st):
                            pt, kbi = st
                            mm(PS[opi][:, :], VH[:, kbi, :], PT[pt][:, :], ti == 0, ti == n - 1, [VHT_[(kbi * 128) // 1024], PTT[pt]], [PST[opi]])
                            mm(PS[dpi][:, :], ones[:], PT[pt][:, :], ti == 0, ti == n - 1, [constT, PTT[pt]], [PST[dpi]])

                        for ti in range(n):
                            st = qk(ti)
                            if pend is not None:
                                pv(ti - 1, pend)
                            pend = st
                        pv(n - 1, pend)
                        gate_store(PS[opi][:, :], [PST[opi]], PS[dpi][:, :], [PST[dpi]], Z0 + h * 128, h * 128, J * 512)
            if "mem" in RUN:
                mem_attention(layer, XQ0, Z0 + 3072)
            if "out" in RUN:
                for half in range(2):
                    out_proj(layer, half, layer == 0)

        def mem_attention(layer, XQ0, ZM0):
            load_gain(mem_norm_g[layer:layer + 1, :])
            MN = BIG[:, 0:8192].rearrange("p (k t) -> p k t", t=256)
            for mb in range(2):
                def tr(xi, mb=mb):
                    P.op("dve", lambda e: e.scalar_tensor_tensor(out=HB[:], in0=XS[xi][:], scalar=stat[:, xi:xi + 1], in1=AUX[:, :],
                                                                 op0=ALU.mult, op1=ALU.mult), [XST[xi], statT[xi], AUXT], [HBT])
                    for kg in range(8):
                        hb = kg % 2
                        for kk in range(4):
                            k = kg * 4 + kk
                            P.op("pe", lambda e, k=k, kk=kk, hb=hb: e.transpose(out=PSBs[hb][:, kk * 128:(kk + 1) * 128],
                                                                                in_=HB[:, k * 128:(k + 1) * 128], identity=ident[:]),
                                 [HBT, constT], [PSBT[hb]])
                        P.op("dve", lambda e, kg=kg, hb=hb: e.tensor_copy(out=MN[:, kg * 4:(kg + 1) * 4, mb * 128:(mb + 1) * 128],
                                                                         in_=PSBs[hb][:, 0:512].rearrange("p (k t) -> p k t", t=128)),
                             [PSBT[hb]], [BIGT[kg]])
                norm_block(mem_in[mb * 128:(mb + 1) * 128, :], [], mb, tr)
            MKT = BIG[:, 8192:10240].rearrange("p (c m) -> p c m", m=256)
            MV = BIG[:, 10240:12288].rearrange("p (c n) -> p c n", n=1024)
            for cb in range(8):
                wi, wv = load_w(w_mem_kv[layer][:, cb * 256:(cb + 1) * 256], 256)
                if cb < 4:
                    for cc in range(2):
                        c = cb * 2 + cc
                        pi = nxt("ps", 6)
                        for k in range(32):
                            mm(PS[pi][:, 0:256], wv[:, k, cc * 128:(cc + 1) * 128], MN[:, k, :], k == 0, k == 31, [WBT[wi], BIGT[k // 4]], [PST[pi]])
                        act(MKT[:, c, :], PS[pi][:, 0:256], AF.Copy, [PST[pi]], [BIGT[8 + c // 4]])
                else:
                    n0 = (cb - 4) * 256
                    for mc in range(2):
                        pi = nxt("ps", 6)
                        for k in range(32):
                            mm(PS[pi][:, 0:256], MN[:, k, mc * 128:(mc + 1) * 128], wv[:, k, :], k == 0, k == 31, [WBT[wi], BIGT[k // 4]], [PST[pi]])
                        act(MV[:, mc, n0:n0 + 256], PS[pi][:, 0:256], AF.Copy, [PST[pi]], [BIGT[10 + mc]])
            XQ = BIG[:, 16384:32768].rearrange("p (c t) -> p c t", t=TOK)
            for c in range(8):
                dma("pool", XQ[:, c, :], projT[XQ0 + c * 128:XQ0 + (c + 1) * 128, :], rows(XQ0 + c * 128, XQ0 + (c + 1) * 128),
                    BIGT[16 + c * 2:16 + c * 2 + 2])
            for hx in range(4):
                for tt in range(4):
                    tsl = slice(tt * 512, (tt + 1) * 512)
                    pts = []
                    for mc in range(2):
                        pi = nxt("ps", 6)
                        for dc in range(2):
                            c = hx * 2 + dc
                            mm(PS[pi][:, :], MKT[:, c, mc * 128:(mc + 1) * 128], XQ[:, c, tsl], dc == 0, dc == 1,
                               [BIGT[8 + c // 4], BIGT[16 + c * 2 + tt // 2]], [PST[pi]])
                        pt = nxt("pt", 3)
                        act(PT[pt][:, :], PS[pi][:, :], AF.Exp, [PST[pi]], [PTT[pt]], scale=1.0 / 16.0)
                        pts.append(pt)
                    dpi = nxt("ps", 6)
                    for mc in range(2):
                        mm(PS[dpi][:, :], ones[:], PT[pts[mc]][:, :], mc == 0, mc == 1, [constT, PTT[pts[mc]]], [PST[dpi]])
                    for dvc in range(2):
                        opi = nxt("ps", 6)
                        for mc in range(2):
                            mm(PS[opi][:, :], MV[:, mc, hx * 256 + dvc * 128:hx * 256 + (dvc + 1) * 128], PT[pts[mc]][:, :], mc == 0, mc == 1,
                               [BIGT[10 + mc], PTT[pts[mc]]], [PST[opi]])
                        gate_store(PS[opi][:, :], [PST[opi]], PS[dpi][:, :], [PST[dpi]], ZM0 + hx * 256 + dvc * 128,
                                   3072 + hx * 256 + dvc * 128, tt * 512)

        swa_state = {"bm": False}

        def swa_setup():
            dist = np.arange(128)
            bucket = t5_bucket_np(dist)
            gfill = XS[0][0:48, 0:384]
            P.op("dve", lambda e: e.memset(gfill, NEG), [], [XST[0]])
            dma("sp", Gd[:, :], gfill, [XST[0]], [GdT])
            dma("sp", RB[:], rel_bias[:, :], [], [constT])
            for d in range(128):
                b = int(bucket[d])
                dma("sp", Gd[:, 127 + d:128 + d].rearrange("h o -> o h"), RB[b:b + 1, :], [constT], [GdT], slow=True)
            BMv = BIGF[:, 0:6144].rearrange("p (h q) -> p h q", q=128)
            for which in range(2):
                for k in range(128):
                    off = (127 - k) if which == 0 else (255 - k)
                    dma("sp", BMv[k:k + 1, :, :], Gd[:, off:off + 128].rearrange("(o h) q -> o h q", o=1), [GdT], BIGT[0:12])
                dma("sp", BM_d[which], BIGF[:, 0:6144], BIGT[0:12], [BMT])

        def swa_layer(layer, j):
            w_in = b_w_in[j]
            K0, V0, XQ0, Z0 = 3072, 3584, 4096, 5120
            if not swa_state["bm"]:
                swa_setup()
                swa_state["bm"] = True
            for half in range(2):
                load_gain(norm_g[layer:layer + 1, :])
                for tbl in range(8):
                    tb = half * 8 + tbl
                    norm_block(xres[tb * 128:(tb + 1) * 128, :], [xresT[tb]], tb % 2, lambda xi, tbl=tbl: h_transposed(xi, tbl, 1024))
                in_proj(w_in, B_IN, Z0, half)
            for i in range(4):
                ei = nxt("ev", 4)
                for kq in range(8):
                    dma("sp", EV[ei][:, kq * 128:(kq + 1) * 128], projT[K0 + kq * 128:K0 + (kq + 1) * 128, (4 * i + 3) * 128:(4 * i + 4) * 128],
                        rows(K0 + kq * 128, K0 + (kq + 1) * 128), [EVT[ei]])
                for kq in range(8):
                    dma("sp", sendS[kq][:, i * 128:(i + 1) * 128], EV[ei][:, kq * 128:(kq + 1) * 128], [EVT[ei]], [sendST])
            for pc in range(8):
                P.op("pool", lambda e, pc=pc: e.collective_compute("AllGather", ALU.bypass, replica_groups=[[0, 1], [2, 3], [4, 5], [6, 7]],
                                                                 ins=[sendS[pc][:, :]], outs=[gathS[pc][:, :]]),
                     [sendST], [gathST], "cc")
            for i in range(4):
                e_o, e_a, e_b = nxt("ev", 4), nxt("ev", 4), nxt("ev", 4)
                if i > 0:
                    for kq in range(8):
                        dma("sp", EV[e_o][:, kq * 128:(kq + 1) * 128], projT[K0 + kq * 128:K0 + (kq + 1) * 128, (4 * i - 1) * 128:(4 * i) * 128],
                            rows(K0 + kq * 128, K0 + (kq + 1) * 128), [EVT[e_o]])
                else:
                    P.op("dve", lambda e, e_o=e_o: e.memset(EV[e_o][:, :], 0.0), [], [EVT[e_o]])
                P.op("dve", lambda e, e_o=e_o, i=i: e.tensor_scalar(out=EV[e_o][:, :], in0=EV[e_o][:, :], scalar1=csel[:, 3 * i:3 * i + 1], scalar2=None, op0=ALU.mult),
                     [EVT[e_o], constT], [EVT[e_o]])
                for r, e_r in ((0, e_a), (1, e_b)):
                    for kq in range(8):
                        dma("sp", EV[e_r][:, kq * 128:(kq + 1) * 128], gathS[kq][r * 128:(r + 1) * 128, i * 128:(i + 1) * 128], [gathST], [EVT[e_r]])
                    P.op("dve", lambda e, e_o=e_o, e_r=e_r, i=i, r=r: e.scalar_tensor_tensor(out=EV[e_o][:, :], in0=EV[e_r][:, :], scalar=csel[:, 3 * i + 1 + r:3 * i + 2 + r],
                                                                                          in1=EV[e_o][:, :], op0=ALU.mult, op1=ALU.add),
                         [EVT[e_o], EVT[e_r], constT], [EVT[e_o]])
                for kq in range(8):
                    dma("sp", kvp_d[kq * 128:(kq + 1) * 128, i * 128:(i + 1) * 128], EV[e_o][:, kq * 128:(kq + 1) * 128], [EVT[e_o]], [kvpT[kq]])
            BMv = BIGF[:, 0:12288].rearrange("p (w h q) -> p w h q", w=2, q=128)
            for which in range(2):
                dma("sp", BIGF[:, which * 6144:(which + 1) * 6144], BM_d[which], [BMT], BIGT[which * 12:(which + 1) * 12])
            dma("sp", ES[:, :], b_sinks[j:j + 1, :].partition_broadcast(64), [], [EST])
            act(ES[:, :], ES[:, :], AF.Exp, [EST], [EST])
            sc = 0.125
            KC = BIG[0:64, 24576:26624]
            VCt = BIG[0:64, 26624:28672]
            KB = BIG[0:64, 28672:29184]
            VBt = BIG[0:64, 29696:30208]
            VC = HB[:, 0:1024].rearrange("p (b n) -> p b n", n=64)
            VB = HB[:, 1024:1280].rearrange("p (b n) -> p b n", n=64)
            for g in range(8):
                dma("pool", KC, projT[K0 + g * 64:K0 + (g + 1) * 64, :], rows(K0 + g * 64, K0 + (g + 1) * 64), BIGT[24:26])
                dma("pool", VCt, projT[V0 + g * 64:V0 + (g + 1) * 64, :], rows(V0 + g * 64, V0 + (g + 1) * 64), BIGT[26:28])
                dma("pool", KB, kvp_d[g * 64:(g + 1) * 64, :], [kvpT[g // 2]], [BIGT[28]])
                dma("pool", VBt, kvp_d[512 + g * 64:512 + (g + 1) * 64, :], [kvpT[4 + g // 2]], [BIGT[29]])
                for hb in range(2):
                    for bb in range(8):
                        b_ = hb * 8 + bb
                        P.op("pe", lambda e, b_=b_, bb=bb, hb=hb: e.transpose(out=PSBs[hb][:, bb * 64:(bb + 1) * 64],
                                                                           in_=VCt[:, b_ * 128:(b_ + 1) * 128], identity=ident[0:64, 0:64]),
                             [BIGT[26 + b_ // 8], constT], [PSBT[hb]])
                    P.op("dve", lambda e, hb=hb: e.tensor_copy(out=HB[:, hb * 512:(hb + 1) * 512], in_=PSBs[hb][:, 0:512]), [PSBT[hb]], [HBT])
                for bb in range(4):
                    P.op("pe", lambda e, bb=bb: e.transpose(out=PSBs[0][:, bb * 64:(bb + 1) * 64], in_=VBt[:, bb * 128:(bb + 1) * 128],
                                                          identity=ident[0:64, 0:64]), [BIGT[29], constT], [PSBT[0]])
                P.op("dve", lambda e: e.tensor_copy(out=HB[:, 1024:1280], in_=PSBs[0][:, 0:256]), [PSBT[0]], [HBT])
                for jb in range(NB):
                    qsl = slice(jb * 128, (jb + 1) * 128)
                    if jb % 4 == 0:
                        i = jb // 4
                        kprev, kprevT = KB[:, i * 128:(i + 1) * 128], BIGT[28]
                        vprev = VB[:, i, :]
                    else:
                        kprev, kprevT = KC[:, (jb - 1) * 128:jb * 128], BIGT[24 + (jb - 1) // 8]
                        vprev = VC[:, jb - 1, :]
                    qi = jb % 2
                    QGv = QG[qi][:, :].rearrange("p (h q) -> p h q", q=128)
                    dma("pool", QGv, projT[g * 384:(g + 1) * 384, qsl].rearrange("(h p) q -> p h q", p=64), rows(g * 384, (g + 1) * 384), [QGT[qi]])
                    for hf in range(2):
                        h0 = g * 6 + hf * 3
                        qrhs = QGv[:, hf * 3:hf * 3 + 3, :]
                        p_prev, p_cur = nxt("ps", 6), nxt("ps", 6)
                        mm(PS[p_prev][:, 0:384], kprev, qrhs, True, True, [kprevT, QGT[qi]], [PST[p_prev]])
                        mm(PS[p_cur][:, 0:384], KC[:, qsl], qrhs, True, True, [BIGT[24 + jb // 8], QGT[qi]], [PST[p_cur]])
                        pts = []
                        for which, pp in ((1, p_prev), (0, p_cur)):
                            e_t = nxt("ev", 4)
                            P.op("dve", lambda e, pp=pp, e_t=e_t, which=which, h0=h0: e.scalar_tensor_tensor(
                                out=EV[e_t][:, 0:384].rearrange("p (h q) -> p h q", q=128), in0=PS[pp][:, 0:384].rearrange("p (h q) -> p h q", q=128),
                                scalar=sc, in1=BMv[:, which, h0:h0 + 3, :], op0=ALU.mult, op1=ALU.add),
                                [PST[pp]] + BIGT[which * 12:(which + 1) * 12], [EVT[e_t]])
                            pt = nxt("pt", 3)
                            act(PT[pt][:, 0:384], EV[e_t][:, 0:384], AF.Exp, [EVT[e_t]], [PTT[pt]])
                            pts.append(pt)
                        opi, dpi = nxt("ps", 6), nxt("ps", 6)
                        mm(PS[opi][0:64, 0:384], vprev, PT[pts[0]][:, 0:384], True, False, [HBT, PTT[pts[0]]], [PST[opi]])
                        mm(PS[opi][0:64, 0:384], VC[:, jb, :], PT[pts[1]][:, 0:384], False, True, [HBT, PTT[pts[1]]], [PST[opi]])
                        mm(PS[dpi][0:64, 0:384], ONP_l[jb], PT[pts[0]][:, 0:384], True, False, [constT, PTT[pts[0]]], [PST[dpi]])
                        mm(PS[dpi][0:64, 0:384], ones[:, 0:64], PT[pts[1]][:, 0:384], False, True, [constT, PTT[pts[1]]], [PST[dpi]])
                        e_d = nxt("ev", 4)
                        for u in range(3):
                            P.op("dve", lambda e, u=u, e_d=e_d, dpi=dpi, h0=h0: e.tensor_scalar(
                                out=EV[e_d][0:64, u * 128:(u + 1) * 128], in0=PS[dpi][0:64, u * 128:(u + 1) * 128],
                                scalar1=ES[:, h0 + u:h0 + u + 1], scalar2=None, op0=ALU.add), [PST[dpi], EST], [EVT[e_d]])
                        P.op("dve", lambda e, e_d=e_d: e.reciprocal(out=EV[e_d][0:64, 0:384], in_=EV[e_d][0:64, 0:384]), [EVT[e_d]], [EVT[e_d]])
                        P.op("dve", lambda e, e_d=e_d, opi=opi: e.tensor_tensor(out=EV[e_d][0:64, 0:384], in0=PS[opi][0:64, 0:384], in1=EV[e_d][0:64, 0:384], op=ALU.mult),
                             [PST[opi], EVT[e_d]], [EVT[e_d]])
                        e_z = nxt("ev", 4)
                        zv = EV[e_z][0:64, 0:384].rearrange("p (h q) -> p h q", q=128)
                        dma("sp", zv, projT[Z0 + h0 * 64:Z0 + (h0 + 3) * 64, qsl].rearrange("(h p) q -> p h q", p=64), rows(Z0 + h0 * 64, Z0 + (h0 + 3) * 64), [EVT[e_z]])
                        yb = EV[e_d][0:64, 512:1024].bitcast(BF16)[:, 0:384]
                        P.op("dve", lambda e, e_d=e_d, e_z=e_z, yb=yb: e.tensor_tensor(out=yb, in0=EV[e_d][0:64, 0:384], in1=EV[e_z][0:64, 0:384], op=ALU.mult),
                             [EVT[e_d], EVT[e_z]], [EVT[e_d]])
                        dma("sp", yT_d[h0 * 64:(h0 + 3) * 64, qsl].rearrange("(h p) q -> p h q", p=64), yb.rearrange("p (h q) -> p h q", q=128),
                            [EVT[e_d]], yTT[(h0 * 64) // 128:((h0 + 3) * 64 - 1) // 128 + 1])
            mem_attention(layer, XQ0, Z0 + 3072)
            for half in range(2):
                out_proj(layer, half, False)

        onp0 = sb("onp0", [128, 64], BF16)
        P.op("dve", lambda e: e.memset(onp0[:], 1.0), [], [constT])
        P.op("dve", lambda e: e.tensor_scalar(out=onp0[:], in0=onp0[:], scalar1=csel[:, 12:13], scalar2=None, op0=ALU.mult), [constT], [constT])
        ONP_l = [onp0[:, :]] + [ones[:, 0:64]] * (NB - 1)

        for layer in range(n_layers):
            if layer % 2 == 0:
                mla_layer(layer, layer // 2)
            else:
                swa_layer(layer, layer // 2)
        load_gain(final_g[0:1, :])
        for tb in range(NB):
            def fin(xi, tb=tb):
                P.op("dve", lambda e: e.scalar_tensor_tensor(out=XS[xi][:], in0=XS[xi][:], scalar=stat[:, xi:xi + 1], in1=AUX[:, :],
                                                             op0=ALU.mult, op1=ALU.mult), [XST[xi], statT[xi], AUXT], [XST[xi]])
                dma("sp", out_d[tb * 128:(tb + 1) * 128, :], XS[xi][:], [XST[xi]], [])
            use_res = n_layers > 0 and "out" in RUN
            src = xres if use_res else x_in
            norm_block(src[tb * 128:(tb + 1) * 128, :], [xresT[tb]] if use_res else [], tb % 2, fin)

        P.emit(nc, es, None)
    return nc


def _tok_idx(s):
    return np.concatenate([np.arange(c * 512, (c + 1) * 512) for c in CHUNKS[s]])


def _const_tables(s):
    ident = np.eye(128, dtype=np.float32)
    inv = (10000.0 ** (-np.arange(0, 64, 2, dtype=np.float32) / 64)).astype(np.float32)
    rope = np.zeros((128, 2), np.float32)
    for p in range(128):
        rope[p, 0] = np.float32(inv[p % 32]) / np.float32(2 * math.pi)
        rope[p, 1] = -1.0 if (p % 64) < 32 else 1.0
    mask = np.zeros((128, 16, 512), np.float32)
    kk = np.arange(128)[:, None]
    qq = np.arange(512)[None, :]
    for par in range(2):
        gq = CHUNKS[s][par]
        for r in range(2):
            gk = CHUNKS[r][par]
            for kb in range(4):
                if gk < gq:
                    m = np.zeros((128, 512), np.float32)
                elif gk > gq:
                    m = np.full((128, 512), NEG, np.float32)
                else:
                    m = np.where(kb * 128 + kk <= qq, 0.0, NEG).astype(np.float32)
                mask[:, par * 8 + r * 4 + kb, :] = m
    sel = np.zeros((128, 8), np.float32)
    return ident, rope, mask, sel


def _sel_table(s):
    sel = np.zeros((128, 16), np.float32)
    for i in range(4):
        g = CHUNKS[s][i]
        if g == 0:
            continue
        prev = g - 1
        if prev in CHUNKS[s]:
            sel[:, 3 * i] = 1.0
        else:
            sel[:, 3 * i + 1 + (1 - s)] = 1.0
    sel[:, 12] = 0.0 if CHUNKS[s][0] == 0 else 1.0
    return sel


def kernel(**inputs):
    x = np.asarray(inputs["x"], np.float32)
    mem = np.asarray(inputs["mem"], np.float32)
    pos = np.asarray(inputs["positions"], np.int32)
    nc = build(N_LAYERS)
    shared = {k: np.ascontiguousarray(np.asarray(inputs[k], np.float32)) for k in
              ["norm_g", "mem_norm_g", "a_q_norm_g", "a_kv_norm_g", "b_sinks", "rel_bias"]}
    nA, nB = (N_LAYERS + 1) // 2, N_LAYERS // 2
    for l in range(N_LAYERS):
        shared[f"w_mem_kv_{l}"] = np.ascontiguousarray(np.asarray(inputs["w_mem_kv"][l], np.float32))
        shared[f"w_out_{l}"] = np.ascontiguousarray(np.asarray(inputs["w_out"][l], np.float32))
    for l in range(nA):
        shared[f"a_w_in_{l}"] = np.ascontiguousarray(np.asarray(inputs["a_w_in"][l], np.float32))
        shared[f"a_w_qb_{l}"] = np.ascontiguousarray(np.asarray(inputs["a_w_qb"][l], np.float32))
        shared[f"a_w_kvb_{l}"] = np.ascontiguousarray(np.asarray(inputs["a_w_kvb"][l], np.float32))
    for l in range(nB):
        shared[f"b_w_in_{l}"] = np.ascontiguousarray(np.asarray(inputs["b_w_in"][l], np.float32))
    shared["a_q_norm_g"] = np.ascontiguousarray(shared["a_q_norm_g"].reshape(2, 8, 128).transpose(0, 2, 1))
    shared["a_kv_norm_g"] = np.ascontiguousarray(shared["a_kv_norm_g"].reshape(2, 4, 128).transpose(0, 2, 1))
    shared["final_norm_g"] = np.ascontiguousarray(np.asarray(inputs["final_norm_g"], np.float32).reshape(1, D))
    need = {"a_w_in": "p1", "b_w_in": "p1", "a_w_qb": "p3", "a_w_kvb": "p4", "w_mem_kv": "mem", "w_out": "out"}
    for k in list(shared):
        for pre, ph in need.items():
            if k.startswith(pre) and ph not in RUN:
                shared[k] = np.zeros((1, 1), np.float32)
    in_maps = []
    for c in range(8):
        b, s = c // 2, c % 2
        idx = _tok_idx(s)
        ident, rope, mask, _ = _const_tables(s)
        m = dict(shared)
        m["x"] = np.ascontiguousarray(x[b][idx])
        m["mem"] = np.ascontiguousarray(mem[b])
        m["pos"] = np.ascontiguousarray(pos[b][idx].reshape(1, TOK))
        m["c_ident"] = ident
        m["c_rope"] = rope
        m["c_mask"] = mask
        m["c_sel"] = _sel_table(s)
        in_maps.append(m)
    res = run_bass_kernel_spmd(nc, in_maps, core_ids=list(range(8)))
    out = np.zeros((4, 4096, D), np.float32)
    for c in range(8):
        b, s = c // 2, c % 2
        out[b, _tok_idx(s)] = res.results[c]["out"]
    return out
```

```python
import math
import numpy as np
from contextlib import ExitStack
import concourse.bass as bass
import concourse.mybir as mybir
from concourse.bass_utils import run_bass_kernel_spmd

F32, BF16, I32 = mybir.dt.float32, mybir.dt.bfloat16, mybir.dt.int32
AF = mybir.ActivationFunctionType
ALU = mybir.AluOpType

N_LAYERS = 4
P4SUB = 7
RUN = {"p1", "p2", "p3", "p4", "p5", "mem", "out"}
D = 4096
TOK = 2048
NB = 16
EPS = 1e-6
A_IN, B_IN = 6720, 9216
NEG = -30000.0
CHUNKS = {0: [0, 3, 4, 7], 1: [1, 2, 5, 6]}
NSEM_DMA = 20


class T:
    __slots__ = ("w", "r")

    def __init__(self):
        self.w = None
        self.r = {}


def Ts(n):
    return [T() for _ in range(n)]


class Op:
    __slots__ = ("eng", "fn", "deps", "kind", "sig", "ticket", "idx", "snap", "sem", "val", "waits")


class Prog:
    ENGS = ["pe", "act", "dve", "pool", "sp"]

    def __init__(self):
        self.ops = {e: [] for e in self.ENGS}
        self.allops = []
        self.dmah = {e: [] for e in self.ENGS}

    def op(self, eng, fn, reads=(), writes=(), kind="c"):
        o = Op()
        o.eng, o.fn, o.kind, o.sig = eng, fn, kind, False
        deps = []
        for t in reads:
            if t.w is not None:
                deps.append(t.w)
        for t in writes:
            for lst in t.r.values():
                deps.extend(lst)
            if t.w is not None:
                deps.append(t.w)
        if kind != "c":
            h = self.dmah[eng]
            if len(h) >= NSEM_DMA:
                deps.append(h[-NSEM_DMA])
            h.append(o)
        o.deps = deps
        key = eng if kind == "c" else "d" + eng
        for t in reads:
            lst = t.r.get(key)
            if lst is None:
                t.r[key] = [o]
            elif kind == "c":
                lst[0] = o
            else:
                lst.append(o)
                if len(lst) > NSEM_DMA:
                    del lst[0]
        for t in writes:
            t.w = o
            t.r = {}
        o.idx = len(self.ops[eng])
        self.ops[eng].append(o)
        self.allops.append(o)
        return o

    def resolve(self):
        known = {e: {f: -1 for f in self.ENGS} for e in self.ENGS}
        kdma = {e: set() for e in self.ENGS}
        for o in self.allops:
            E = o.eng
            kn = known[E]
            waits = []
            for x in o.deps:
                if x is o:
                    continue
                if x.kind != "c":
                    if id(x) not in kdma[E]:
                        kdma[E].add(id(x))
                        waits.append(x)
                    continue
                F = x.eng
                if F == E and E == "pe":
                    continue
                if x.idx <= kn[F]:
                    continue
                x.sig = True
                waits.append(x)
                kn[F] = x.idx
                for g, v in x.snap.items():
                    if v > kn[g]:
                        kn[g] = v
            o.waits = waits
            if o.kind == "c":
                o.snap = dict(kn)
            else:
                o.snap = None

    def emit(self, nc, es, handles_of_block):
        self.resolve()
        ROT = 20000
        sems = {}
        for e in self.ENGS:
            nsig = sum(1 for o in self.ops[e] if o.kind == "c" and o.sig)
            sems[e] = [es.enter_context(nc.semaphore(f"s_{e}_{i}")) for i in range(nsig // ROT + 1)]
            c = 0
            for o in self.ops[e]:
                if o.kind == "c" and o.sig:
                    o.sem = sems[e][c // ROT]
                    o.val = c % ROT + 1
                    c += 1
            nd = len(self.dmah[e])
            if nd:
                pool = [es.enter_context(nc.semaphore(f"d_{e}_{i}")) for i in range(min(nd, NSEM_DMA))]
                cnt = [0] * len(pool)
                ncc = sum(1 for o in self.dmah[e] if o.kind == "cc")
                ccpool = [es.enter_context(nc.semaphore(f"cc_{e}_{i}")) for i in range(min(ncc, 8))]
                cccnt = [0] * len(ccpool)
                ci = 0
                for i, o in enumerate(self.dmah[e]):
                    if o.kind == "cc":
                        k = ci % len(ccpool)
                        ci += 1
                        cccnt[k] += 1
                        o.sem, o.val = ccpool[k], cccnt[k]
                        continue
                    k = i % len(pool)
                    cnt[k] += 16
                    o.sem, o.val = pool[k], cnt[k]
        block = es.enter_context(nc.Block())

        def run(ename):
            def body(e):
                for o in self.ops[ename]:
                    for x in o.waits:
                        e.wait_ge(x.sem, x.val)
                    ins = o.fn(e)
                    if o.kind == "d":
                        ins.then_inc(o.sem, 16)
                    elif o.kind == "cc":
                        ins.then_inc(o.sem)
                    elif o.sig:
                        ins.then_inc(o.sem, 1)
                for o in self.dmah[ename][-NSEM_DMA:]:
                    e.wait_ge(o.sem, o.val)
            return body

        block.tensor(run("pe"))
        block.scalar(run("act"))
        block.vector(run("dve"))
        block.gpsimd(run("pool"))
        block.sync(run("sp"))


def t5_bucket_np(dist):
    n = np.maximum(dist, 0)
    nf = np.maximum(n, 1).astype(np.float32)
    large = 16 + (np.log(nf / 16) / math.log(128 / 16) * 16).astype(np.int32)
    large = np.minimum(large, 31)
    return np.where(n < 16, n, large)


def build(n_layers):
    nc = bass.Bass("TRN2", target_bir_lowering=False)
    P = Prog()

    def din(name, shape, dt=F32):
        need = {"a_w_in": "p1", "b_w_in": "p1", "a_w_qb": "p3", "a_w_kvb": "p4", "w_mem_kv": "mem", "w_out": "out"}
        for pre, ph in need.items():
            if name.startswith(pre) and ph not in RUN:
                shape = [1, 1]
        return nc.dram_tensor(name, list(shape), dt, kind="ExternalInput").ap()

    def dscr(name, shape, dt):
        return nc.dram_tensor(name, list(shape), dt).ap()

    x_in = din("x", [TOK, D])
    mem_in = din("mem", [256, D])
    pos_in = din("pos", [1, TOK], I32)
    norm_g = din("norm_g", [4, D])
    mem_norm_g = din("mem_norm_g", [4, D])
    final_g = din("final_norm_g", [1, D])
    nA, nB = (n_layers + 1) // 2, n_layers // 2
    w_mem_kv = [din(f"w_mem_kv_{l}", [D, 2048]) for l in range(n_layers)]
    w_out = [din(f"w_out_{l}", [D, D]) for l in range(n_layers)]
    a_w_in = [din(f"a_w_in_{l}", [D, A_IN]) for l in range(nA)]
    a_qg = din("a_q_norm_g", [2, 128, 8])
    a_kvg = din("a_kv_norm_g", [2, 128, 4])
    a_w_qb = [din(f"a_w_qb_{l}", [1024, 4608]) for l in range(nA)]
    a_w_kvb = [din(f"a_w_kvb_{l}", [512, 6144]) for l in range(nA)]
    b_w_in = [din(f"b_w_in_{l}", [D, B_IN]) for l in range(nB)]
    b_sinks = din("b_sinks", [2, 48])
    rel_bias = din("rel_bias", [32, 48])
    c_ident = din("c_ident", [128, 128])
    c_rope = din("c_rope", [128, 2])
    c_mask = din("c_mask", [128, 16, 512])
    c_sel = din("c_sel", [128, 16])
    out_d = nc.dram_tensor("out", [TOK, D], F32, kind="ExternalOutput").ap()

    xres = dscr("xres", [TOK, D], F32)
    projT = dscr("projT", [B_IN, TOK], F32)
    yT_d = dscr("yT_d", [D, TOK], BF16)
    sendKV_f = [dscr(f"sendKV{p}", [128, 512], F32) for p in range(10)]
    gathKV_f = [dscr(f"gathKV{p}", [256, 512], F32) for p in range(10)]
    sendKV = [a.bitcast(BF16).rearrange("(r two) c -> r (two c)", two=2) for a in sendKV_f]
    gathKV = [a.bitcast(BF16).rearrange("(r i two) c -> r i (two c)", r=2, two=2) for a in gathKV_f]
    QnT_d = dscr("QnT_d", [24, 128, TOK], BF16)
    QrT_d = dscr("QrT_d", [12, 128, TOK], BF16)
    KnT_d = dscr("KnT_d", [24, 128, 2 * TOK], BF16)
    V_d = dscr("V_d", [24, 128, 32, 128], BF16)
    cs_d = dscr("cs_d", [2, 128, TOK], F32)
    Gd = dscr("Gd", [48, 384], F32)
    BM_d = dscr("BM_d", [2, 128, 48 * 128], F32)
    sendS = [dscr(f"sendS{p}", [128, 512], F32) for p in range(8)]
    gathS = [dscr(f"gathS{p}", [256, 512], F32) for p in range(8)]
    kvp_d = dscr("kvp_d", [1024, 512], F32)

    xresT, projTT, yTT = Ts(NB), Ts(72), Ts(32)
    sendKVT, gathKVT, csT, GdT, BMT, sendST, gathST = T(), T(), T(), T(), T(), T(), T()
    QnTT, QrTT, KnTT, VTT, kvpT = Ts(24), Ts(12), Ts(24), Ts(24), Ts(8)

    es = ExitStack()
    with es:
        def sb(name, shape, dt):
            return es.enter_context(nc.sbuf_tensor(name, list(shape), dt))

        BIG = sb("BIG", [128, 32768], BF16)
        BIGT = Ts(32)
        BIGF = BIG[:].bitcast(F32)
        WBall = sb("WBall", [128, 24576], BF16)
        WB = [WBall[:, i * 8192:(i + 1) * 8192] for i in range(3)]
        WBT = Ts(3)
        XSall = sb("XSall", [128, 8192], F32)
        XS = [XSall[:, i * 4096:(i + 1) * 4096] for i in range(2)]
        XST = Ts(2)
        AUX = sb("AUX", [128, 4096], F32)
        AUXT = T()
        MASK = sb("MASK", [128, 16, 512], BF16)
        MASKT = T()
        HB = sb("HB", [128, 4096], BF16)
        HBT = T()
        EV = [sb(f"EV{i}", [128, 1024], F32) for i in range(4)]
        EVT = Ts(4)
        PT = [sb(f"PT{i}", [128, 512], BF16) for i in range(3)]
        PTT = Ts(3)
        ident = sb("ident", [128, 128], BF16)
        ones = sb("ones", [128, 128], BF16)
        epsT = sb("epsT", [128, 1], F32)
        crope = sb("crope", [128, 2], F32)
        csel = sb("csel", [128, 16], F32)
        stat = sb("stat", [128, 8], F32)
        statT = Ts(8)
        gq = sb("gq", [128, 12], F32)
        gqT = T()
        RB = sb("RB", [32, 48], F32)
        QG = [sb(f"QG{i}", [64, 768], BF16) for i in range(2)]
        QGT = Ts(2)
        ES = sb("ES", [64, 48], F32)
        EST = T()
        ESRT = T()
        constT = T()
        PS = [es.enter_context(nc.psum_tensor(f"PS{i}", [128, 512], F32)) for i in range(6)]
        PST = Ts(6)
        PSBs = [es.enter_context(nc.psum_tensor(f"PSB{i}", [128, 1024], BF16)) for i in range(2)]
        PSBT = Ts(2)

        cnt = {"ps": 0, "ev": 0, "wb": 0, "pt": 0, "sps": 0}

        def nxt(k, n):
            v = cnt[k] % n
            cnt[k] += 1
            return v

        def dma(q, out, in_, reads, writes, slow=False):
            if slow:
                return P.op(q, lambda e: e.dma_start(out=out, in_=in_, allow_slow_non_contiguous=True), reads, writes, "d")
            return P.op(q, lambda e: e.dma_start(out=out, in_=in_), reads, writes, "d")

        def mm(out, lhsT, rhs, start, stop, reads, writes):
            return P.op("pe", lambda e: e.matmul(out, lhsT=lhsT, rhs=rhs, start=start, stop=stop), reads, writes)

        def act(out, in_, func, reads, writes, scale=1.0, bias=None, accum=None):
            def f(e):
                kw = {}
                if bias is not None:
                    kw["bias"] = bias
                if accum is not None:
                    kw["accum_out"] = accum
                return e.activation(out=out, in_=in_, func=func, scale=scale, **kw)
            return P.op("act", f, reads, writes)

        def rows(r0, r1):
            return projTT[r0 // 128:(r1 - 1) // 128 + 1]

        dma("pool", ident[:], c_ident[:, :], [], [constT])
        dma("sp", crope[:], c_rope[:, :], [], [constT])
        dma("sp", csel[:], c_sel[:, :], [], [constT])
        dma("pool", MASK[:], c_mask[:, :, :], [], [MASKT])
        P.op("dve", lambda e: e.memset(ones[:], 1.0), [], [constT])
        P.op("dve", lambda e: e.memset(epsT[:], EPS), [], [constT])

        posi = XS[0][:].bitcast(I32)
        dma("sp", posi[:, 0:TOK], pos_in[0:1, :].partition_broadcast(128), [], [XST[0]])
        ang = XS[1]
        P.op("dve", lambda e: e.tensor_copy(out=ang[:, 0:TOK], in_=posi[:, 0:TOK]), [XST[0]], [XST[1]])
        P.op("dve", lambda e: e.tensor_scalar(out=ang[:, 0:TOK], in0=ang[:, 0:TOK], scalar1=crope[:, 0:1], scalar2=None, op0=ALU.mult),
             [XST[1], constT], [XST[1]])
        P.op("dve", lambda e: e.tensor_scalar(out=ang[:, TOK:2 * TOK], in0=ang[:, 0:TOK], scalar1=0.25, scalar2=None, op0=ALU.add),
             [XST[1]], [XST[1]])
        x0f = XS[0]
        for (lo, hi) in ((0, TOK), (TOK, 2 * TOK)):
            P.op("dve", lambda e, lo=lo, hi=hi: e.tensor_copy(out=posi[:, TOK:2 * TOK], in_=ang[:, lo:hi]), [XST[1]], [XST[0]])
            P.op("dve", lambda e: e.tensor_copy(out=x0f[:, 0:TOK], in_=posi[:, TOK:2 * TOK]), [XST[0]], [XST[0]])
            P.op("dve", lambda e, lo=lo, hi=hi: e.tensor_tensor(out=ang[:, lo:hi], in0=ang[:, lo:hi], in1=x0f[:, 0:TOK], op=ALU.subtract),
                 [XST[0], XST[1]], [XST[1]])
            P.op("dve", lambda e, lo=lo, hi=hi: e.scalar_tensor_tensor(out=ang[:, lo:hi], in0=ang[:, lo:hi], scalar=0.5, in1=ang[:, lo:hi],
                                                                       op0=ALU.is_gt, op1=ALU.subtract), [XST[1]], [XST[1]])
        SC = -(2.0 * math.pi - 2e-6)
        act(AUX[:, TOK:2 * TOK], ang[:, 0:TOK], AF.Sin, [XST[1]], [AUXT], scale=SC)
        act(AUX[:, 0:TOK], ang[:, TOK:2 * TOK], AF.Sin, [XST[1]], [AUXT], scale=SC)
        P.op("dve", lambda e: e.tensor_scalar(out=AUX[:, TOK:2 * TOK], in0=AUX[:, TOK:2 * TOK], scalar1=crope[:, 1:2], scalar2=None, op0=ALU.mult),
             [AUXT, constT], [AUXT])
        dma("sp", cs_d[0], AUX[:, 0:TOK], [AUXT], [csT])
        dma("sp", cs_d[1], AUX[:, TOK:2 * TOK], [AUXT], [csT])

        def load_gain(g_ap_row):
            dma("sp", AUX[:, :], g_ap_row.partition_broadcast(128), [], [AUXT])

        def norm_block(src_ap, src_reads, xi, dst_fn):
            dma("sp", XS[xi][:], src_ap, src_reads, [XST[xi]])
            act(HB[:], XS[xi][:], AF.Square, [XST[xi]], [HBT, statT[xi]], accum=stat[:, xi:xi + 1])
            act(stat[:, xi:xi + 1], stat[:, xi:xi + 1], AF.Sqrt, [statT[xi], constT], [statT[xi]], scale=1.0 / D, bias=epsT[:, 0:1])
            P.op("dve", lambda e: e.reciprocal(out=stat[:, xi:xi + 1], in_=stat[:, xi:xi + 1]), [statT[xi]], [statT[xi]])
            dst_fn(xi)

        def h_transposed(xi, tb_local, nchunk_tok):
            P.op("dve", lambda e: e.scalar_tensor_tensor(out=HB[:], in0=XS[xi][:], scalar=stat[:, xi:xi + 1], in1=AUX[:, :],
                                                         op0=ALU.mult, op1=ALU.mult), [XST[xi], statT[xi], AUXT], [HBT])
            W = nchunk_tok
            for kg in range(8):
                hb = kg % 2
                for kk in range(4):
                    k = kg * 4 + kk
                    P.op("pe", lambda e, k=k, kk=kk, hb=hb: e.transpose(out=PSBs[hb][:, kk * 128:(kk + 1) * 128],
                                                                        in_=HB[:, k * 128:(k + 1) * 128], identity=ident[:]),
                         [HBT, constT], [PSBT[hb]])
                def cp(e, kg=kg, hb=hb):
                    o = BIG[:].rearrange("p (k t) -> p k t", t=W)[:, kg * 4:(kg + 1) * 4, tb_local * 128:(tb_local + 1) * 128]
                    i = PSBs[hb][:, 0:512].rearrange("p (k t) -> p k t", t=128)
                    return e.tensor_copy(out=o, in_=i)
                wr = [BIGT[(k * W) // 1024] for k in range(kg * 4, kg * 4 + 4)]
                P.op("dve", cp, [PSBT[hb]], wr)

        def load_w(src_ap, ncols):
            wi = nxt("wb", 3)
            o = WB[wi][:, 0:32 * ncols].rearrange("p (k n) -> p k n", n=ncols)
            sv = src_ap.rearrange("(k p) n -> p k n", p=128)
            for kq in range(8):
                dma("pool", o[:, kq * 4:(kq + 1) * 4, :], sv[:, kq * 4:(kq + 1) * 4, :], [], [WBT[wi]])
            return wi, o

        def in_proj(w_ap, E, zstart, half):
            hT = BIG[:].rearrange("p (k t) -> p k t", t=1024)
            nblk = (E + 511) // 512
            for cb in range(nblk):
                c0 = cb * 512
                ncol = min(512, E - c0)
                if cb % 2 == 0:
                    wbuf, wT = WBall[:, 0:16384], WBT[0:2]
                else:
                    wbuf, wT = XSall[:, :].bitcast(BF16), XST[0:2]
                wv = wbuf[:, 0:32 * ncol].rearrange("p (k n) -> p k n", n=ncol)
                sv = w_ap[:, c0:c0 + ncol].rearrange("(k p) n -> p k n", p=128)
                for kq in range(8):
                    dma("pool", wv[:, kq * 4:(kq + 1) * 4, :], sv[:, kq * 4:(kq + 1) * 4, :], [], wT)
                for cc in range(0, ncol, 128):
                    m = min(128, ncol - cc)
                    e0 = c0 + cc
                    for tt in range(2):
                        pi = nxt("ps", 6)
                        for k in range(32):
                            mm(PS[pi][0:m, :], wv[:, k, cc:cc + m], hT[:, k, tt * 512:(tt + 1) * 512], k == 0, k == 31,
                               wT + [BIGT[k]], [PST[pi]])
                        ei = nxt("ev", 4)
                        segs = []
                        if e0 + m <= zstart:
                            segs = [(0, m, AF.Copy)]
                        elif e0 >= zstart:
                            segs = [(0, m, AF.Silu)]
                        else:
                            segs = [(0, zstart - e0, AF.Copy), (zstart - e0, m, AF.Silu)]
                        for (p0, p1, fn) in segs:
                            act(EV[ei][p0:p1, 0:512], PS[pi][p0:p1, :], fn, [PST[pi]], [EVT[ei]])
                        t0 = half * 1024 + tt * 512
                        dma("sp", projT[e0:e0 + m, t0:t0 + 512], EV[ei][0:m, 0:512], [EVT[ei]], rows(e0, e0 + m))

        def out_proj(layer, half, src_is_input):
            yv = BIG[:].rearrange("p (k t) -> p k t", t=1024)
            for kq in range(4):
                dma("sp", yv[:, kq * 8:(kq + 1) * 8, :],
                    yT_d[kq * 1024:(kq + 1) * 1024, half * 1024:(half + 1) * 1024].rearrange("(k p) t -> p k t", p=128),
                    yTT[kq * 8:(kq + 1) * 8], BIGT[kq * 8:(kq + 1) * 8])
            xsrc = x_in if src_is_input else xres
            for db in range(8):
                if db % 2 == 0:
                    wbuf, wT = WBall[:, 0:16384], WBT[0:2]
                else:
                    wbuf, wT = XSall[:, :].bitcast(BF16), XST[0:2]
                wv = wbuf.rearrange("p (k n) -> p k n", n=512)
                sv = w_out[layer][:, db * 512:(db + 1) * 512].rearrange("(k p) n -> p k n", p=128)
                for kq in range(8):
                    dma("pool", wv[:, kq * 4:(kq + 1) * 4, :], sv[:, kq * 4:(kq + 1) * 4, :], [], wT)
                for tb in range(8):
                    pi = nxt("ps", 6)
                    ei = nxt("ev", 4)
                    for k in range(32):
                        mm(PS[pi][:, :], yv[:, k, tb * 128:(tb + 1) * 128], wv[:, k, :], k == 0, k == 31, wT + [BIGT[k]], [PST[pi]])
                    gtb = half * 8 + tb
                    xrd = [] if src_is_input else [xresT[gtb]]
                    dma("sp", EV[ei][:, 0:512], xsrc[gtb * 128:(gtb + 1) * 128, db * 512:(db + 1) * 512], xrd, [EVT[ei]])
                    P.op("dve", lambda e, ei=ei, pi=pi: e.tensor_tensor(out=EV[ei][:, 0:512], in0=PS[pi][:, :], in1=EV[ei][:, 0:512], op=ALU.add),
                         [PST[pi], EVT[ei]], [EVT[ei]])
                    dma("sp", xres[gtb * 128:(gtb + 1) * 128, db * 512:(db + 1) * 512], EV[ei][:, 0:512], [EVT[ei]], [xresT[gtb]])

        def gate_store(o_ps, o_pst, den_ps, den_pst, zrow0, yrow0, t0, npart=128, width=512):
            e1, e2 = nxt("ev", 4), nxt("ev", 4)
            dma("sp", EV[e1][0:npart, 0:width], projT[zrow0:zrow0 + npart, t0:t0 + width], rows(zrow0, zrow0 + npart), [EVT[e1]])
            P.op("dve", lambda e: e.reciprocal(out=EV[e2][0:npart, 0:width], in_=den_ps), den_pst, [EVT[e2]])
            P.op("dve", lambda e: e.tensor_tensor(out=EV[e2][0:npart, 0:width], in0=o_ps, in1=EV[e2][0:npart, 0:width], op=ALU.mult),
                 o_pst + [EVT[e2]], [EVT[e2]])
            yb = EV[e2][0:npart, 512:1024].bitcast(BF16)[:, 0:width]
            P.op("dve", lambda e: e.tensor_tensor(out=yb, in0=EV[e2][0:npart, 0:width], in1=EV[e1][0:npart, 0:width], op=ALU.mult),
                 [EVT[e1], EVT[e2]], [EVT[e2]])
            dma("sp", yT_d[yrow0:yrow0 + npart, t0:t0 + width], yb, [EVT[e2]], yTT[yrow0 // 128:(yrow0 + npart - 1) // 128 + 1])

        def latent_norm(row0, nk, gcol0, tt, dst_fn):
            xi = nxt("xs2", 2) if False else (tt % 2)
            cv = XS[xi][:, 0:nk * 512].rearrange("p (k t) -> p k t", t=512)
            dma("sp", cv, projT[row0:row0 + nk * 128, tt * 512:(tt + 1) * 512].rearrange("(k p) t -> p k t", p=128),
                rows(row0, row0 + nk * 128), [XST[xi]])
            sq = HB[:, 0:nk * 512].rearrange("p (k t) -> p k t", t=512)
            act(sq, cv, AF.Square, [XST[xi]], [HBT])
            pi = nxt("ps", 6)
            for k in range(nk):
                mm(PS[pi][:, :], ones[:], sq[:, k, :], k == 0, k == nk - 1, [HBT, constT], [PST[pi]])
            ei = nxt("ev", 4)
            act(EV[ei][:, 0:512], PS[pi][:, :], AF.Sqrt, [PST[pi], constT], [EVT[ei]], scale=1.0 / (nk * 128), bias=epsT[:, 0:1])
            P.op("dve", lambda e: e.reciprocal(out=EV[ei][:, 0:512], in_=EV[ei][:, 0:512]), [EVT[ei]], [EVT[ei]])
            for k in range(nk):
                dst_fn(k, cv[:, k, :], EV[ei][:, 0:512], gq[:, gcol0 + k:gcol0 + k + 1], [XST[xi], EVT[ei], gqT])

        def mla_layer(layer, j):
            w_in = a_w_in[j]
            XQ0, Z0 = 1600, 2624
            dma("sp", gq[:, 0:8], a_qg[j], [], [gqT])
            dma("sp", gq[:, 8:12], a_kvg[j], [], [gqT])
            for half in range(2):
                if "p1" not in RUN:
                    break
                load_gain(norm_g[layer:layer + 1, :])
                for tbl in range(8):
                    tb = half * 8 + tbl
                    src = x_in if layer == 0 else xres
                    rd = [] if layer == 0 else [xresT[tb]]
                    norm_block(src[tb * 128:(tb + 1) * 128, :], rd, tb % 2, lambda xi, tbl=tbl: h_transposed(xi, tbl, 1024))
                in_proj(w_in, A_IN, Z0, half)
            CQN = BIG[:, 0:16384].rearrange("p (k t) -> p k t", t=TOK)
            CKV = BIG[:, 16384:32768].rearrange("p (k t) -> p k t", t=2 * TOK)
            KR = BIG[:, 24576:28672]
            if "p2" in RUN:
                dma("sp", AUX[:, 0:TOK], cs_d[0], [csT], [AUXT])
                dma("sp", AUX[:, TOK:2 * TOK], cs_d[1], [csT], [AUXT])
                CQN = BIG[:, 0:16384].rearrange("p (k t) -> p k t", t=TOK)
                for tt in range(4):
                    def put_cq(k, src, rstd, g, rd, tt=tt):
                        P.op("dve", lambda e: e.scalar_tensor_tensor(out=CQN[:, k, tt * 512:(tt + 1) * 512], in0=src, scalar=g, in1=rstd,
                                                                     op0=ALU.mult, op1=ALU.mult), rd, [BIGT[(k * TOK + tt * 512) // 1024]])
                    latent_norm(0, 8, 0, tt, put_cq)

                    def put_ckv(k, src, rstd, g, rd, tt=tt):
                        pt = nxt("pt", 3)
                        P.op("dve", lambda e: e.scalar_tensor_tensor(out=PT[pt][:, :], in0=src, scalar=g, in1=rstd,
                                                                     op0=ALU.mult, op1=ALU.mult), rd, [PTT[pt]])
                        for hh in range(2):
                            dma("sp", sendKV[2 * k + hh][:, tt * 512:(tt + 1) * 512], PT[pt][hh * 64:(hh + 1) * 64, :], [PTT[pt]], [sendKVT])
                    latent_norm(1024, 4, 8, tt, put_ckv)
                    e1, e2 = nxt("ev", 4), nxt("ev", 4)
                    tsl = slice(tt * 512, (tt + 1) * 512)
                    for hh in range(2):
                        dma("sp", EV[e1][hh * 64:(hh + 1) * 64, 0:512], projT[1536:1600, tsl], rows(1536, 1600), [EVT[e1]])
                        dma("sp", EV[e2][hh * 64:hh * 64 + 32, 0:512], projT[1568:1600, tsl], rows(1536, 1600), [EVT[e2]])
                        dma("sp", EV[e2][hh * 64 + 32:(hh + 1) * 64, 0:512], projT[1536:1568, tsl], rows(1536, 1600), [EVT[e2]])
                    P.op("dve", lambda e, e1=e1, tsl=tsl: e.tensor_tensor(out=EV[e1][:, 0:512], in0=EV[e1][:, 0:512], in1=AUX[:, tsl], op=ALU.mult),
                         [EVT[e1], AUXT], [EVT[e1]])
                    P.op("dve", lambda e, e2=e2, tt=tt: e.tensor_tensor(out=EV[e2][:, 0:512], in0=EV[e2][:, 0:512],
                                                                       in1=AUX[:, TOK + tt * 512:TOK + (tt + 1) * 512], op=ALU.mult),
                         [EVT[e2], AUXT], [EVT[e2]])
                    pt = nxt("pt", 3)
                    P.op("dve", lambda e, e1=e1, e2=e2, pt=pt: e.tensor_tensor(out=PT[pt][:, :], in0=EV[e1][:, 0:512], in1=EV[e2][:, 0:512], op=ALU.add),
                         [EVT[e1], EVT[e2]], [PTT[pt]])
                    for hh in range(2):
                        dma("sp", sendKV[8 + hh][:, tsl], PT[pt][hh * 64:(hh + 1) * 64, :], [PTT[pt]], [sendKVT])
                for pc in range(10):
                    P.op("pool", lambda e, pc=pc: e.collective_compute("AllGather", ALU.bypass, replica_groups=[[0, 1], [2, 3], [4, 5], [6, 7]],
                                                                     ins=[sendKV_f[pc][:, :]], outs=[gathKV_f[pc][:, :]]),
                         [sendKVT], [gathKVT], "cc")
            if "p3" in RUN:
                for hp in range(12):
                    wi = nxt("wb", 3)
                    wv = WB[wi][:, 0:4096].rearrange("p (k g n) -> p k g n", g=4, n=128)
                    wq = a_w_qb[j].rearrange("(k p) n -> p k n", p=128)
                    for u in range(2):
                        h = hp * 2 + u
                        dma("pool", wv[:, :, u, :], wq[:, :, h * 192:h * 192 + 128], [], [WBT[wi]])
                        dma("pool", wv[:, :, 2, u * 64:(u + 1) * 64], wq[:, :, h * 192 + 128:h * 192 + 192], [], [WBT[wi]])
                        dma("pool", wv[:, :, 3, u * 64:u * 64 + 32], wq[:, :, h * 192 + 160:h * 192 + 192], [], [WBT[wi]])
                        dma("pool", wv[:, :, 3, u * 64 + 32:(u + 1) * 64], wq[:, :, h * 192 + 128:h * 192 + 160], [], [WBT[wi]])
                    stg = [nxt("ev", 4) for _ in range(3)]
                    sv = [EV[s][:].bitcast(BF16) for s in stg]
                    for tt in range(4):
                        tsl = slice(tt * 512, (tt + 1) * 512)
                        pis = [nxt("ps", 6) for _ in range(4)]
                        for g in range(4):
                            for k in range(8):
                                mm(PS[pis[g]][:, :], wv[:, k, g, :], CQN[:, k, tsl], k == 0, k == 7,
                                   [WBT[wi], BIGT[(k * TOK + tt * 512) // 1024]], [PST[pis[g]]])
                        for u in range(2):
                            act(sv[u][:, tsl], PS[pis[u]][:, :], AF.Copy, [PST[pis[u]]], [EVT[stg[u]]])
                        x1, x2 = XS[0][:, 0:512], XS[0][:, 512:1024]
                        P.op("dve", lambda e, pi=pis[2], tsl=tsl: e.tensor_tensor(out=x1, in0=PS[pi][:, :], in1=AUX[:, tsl], op=ALU.mult),
                             [PST[pis[2]], AUXT], [XST[0]])
                        P.op("dve", lambda e, pi=pis[3], tt=tt: e.tensor_tensor(out=x2, in0=PS[pi][:, :], in1=AUX[:, TOK + tt * 512:TOK + (tt + 1) * 512], op=ALU.mult),
                             [PST[pis[3]], AUXT], [XST[0]])
                        P.op("dve", lambda e, tsl=tsl, s2=sv[2]: e.tensor_tensor(out=s2[:, tsl], in0=x1, in1=x2, op=ALU.add), [XST[0]], [EVT[stg[2]]])
                    for u in range(2):
                        dma("sp", QnT_d[hp * 2 + u], sv[u][:, 0:TOK], [EVT[stg[u]]], [QnTT[hp * 2 + u]])
                    dma("sp", QrT_d[hp], sv[2][:, 0:TOK], [EVT[stg[2]]], [QrTT[hp]])
            if "p4" in RUN:
                CKV = BIG[:, 16384:32768].rearrange("p (k t) -> p k t", t=2 * TOK)
                for r in range(2):
                    for k in range(4):
                        for hh in range(2):
                            dma("sp", CKV[hh * 64:(hh + 1) * 64, k, r * TOK:(r + 1) * TOK], gathKV[2 * k + hh][r, :, :], [gathKVT],
                                BIGT[16 + k * 4 + r * 2:16 + k * 4 + r * 2 + 2])
                wkv = a_w_kvb[j].rearrange("(k p) (h n) -> p k h n", p=128, n=256)
                for hg in range(6):
                    wi = nxt("wb", 3)
                    wk = WB[wi][:, 0:2048].rearrange("p (k h n) -> p k h n", h=4, n=128)
                    wvv = WB[wi][:, 2048:4096].rearrange("p (k h n) -> p k h n", h=4, n=128)
                    for k in range(4):
                        dma("pool", wk[:, k], wkv[:, k, hg * 4:(hg + 1) * 4, 0:128], [], [WBT[wi]])
                        dma("pool", wvv[:, k], wkv[:, k, hg * 4:(hg + 1) * 4, 128:256], [], [WBT[wi]])
                    for u in range(4):
                        if not (P4SUB & 2):
                            break
                        h = hg * 4 + u
                        for th in range(2):
                            ei = nxt("ev", 4)
                            stv = EV[ei][:].bitcast(BF16)
                            for t4 in range(4):
                                t0 = th * 2048 + t4 * 512
                                pi = nxt("ps", 6)
                                for k in range(4):
                                    mm(PS[pi][:, :], wk[:, k, u, :], CKV[:, k, t0:t0 + 512], k == 0, k == 3,
                                       [WBT[wi], BIGT[16 + k * 4 + t0 // 1024]], [PST[pi]])
                                act(stv[:, t4 * 512:(t4 + 1) * 512], PS[pi][:, :], AF.Copy, [PST[pi]], [EVT[ei]])
                            dma("sp", KnT_d[h, :, th * 2048:(th + 1) * 2048], stv[:, 0:2048], [EVT[ei]], [KnTT[h]])
                    for kb4 in range(8):
                        if not (P4SUB & 4):
                            break
                        ei = nxt("ev", 4)
                        stv = EV[ei][:].bitcast(BF16).rearrange("p (b n) -> p b n", n=512)
                        for b_ in range(4):
                            kb = kb4 * 4 + b_
                            pi = nxt("ps", 6)
                            for k in range(4):
                                mm(PS[pi][:, :], CKV[:, k, kb * 128:(kb + 1) * 128], WB[wi][:, 2048 + k * 512:2048 + (k + 1) * 512],
                                   k == 0, k == 3, [WBT[wi], BIGT[16 + k * 4 + (kb * 128) // 1024]], [PST[pi]])
                            if True:
                                act(stv[:, b_, :], PS[pi][:, :], AF.Copy, [PST[pi]], [EVT[ei]])
                            else:
                                P.op("dve", lambda e, b_=b_, pi=pi, stv=stv: e.tensor_copy(out=stv[:, b_, :], in_=PS[pi][:, :]), [PST[pi]], [EVT[ei]])
                        for u in range(4):
                            if P4SUB & 16:
                                break
                            dma("sp", V_d[hg * 4 + u, :, kb4 * 4:(kb4 + 1) * 4, :], stv[:, :, u * 128:(u + 1) * 128], [EVT[ei]], [VTT[hg * 4 + u]])
            if "p5" in RUN:
                KR = BIG[:, 24576:28672]
                for r in range(2):
                    for hh in range(2):
                        dma("sp", KR[hh * 64:(hh + 1) * 64, r * TOK:(r + 1) * TOK], gathKV[8 + hh][r, :, :], [gathKVT], BIGT[24 + r * 2:24 + r * 2 + 2])
                scale = 192.0 ** -0.5
                for h in range(24):
                    b2 = h % 2
                    hp = h // 2
                    KN = BIG[:, b2 * 4096:(b2 + 1) * 4096]
                    KNT_ = BIGT[b2 * 4:(b2 + 1) * 4]
                    VH = BIG[:, 8192 + b2 * 4096:8192 + (b2 + 1) * 4096].rearrange("p (b n) -> p b n", n=128)
                    VHT_ = BIGT[8 + b2 * 4:8 + (b2 + 1) * 4]
                    QN = BIG[:, 16384 + b2 * 2048:16384 + (b2 + 1) * 2048]
                    QNT_ = BIGT[16 + b2 * 2:16 + (b2 + 1) * 2]
                    QR = BIG[:, 20480 + (hp % 2) * 2048:20480 + (hp % 2 + 1) * 2048]
                    QRT_ = BIGT[20 + (hp % 2) * 2:20 + (hp % 2 + 1) * 2]
                    dma("sp", KN, KnT_d[h], [KnTT[h]], KNT_)
                    dma("sp", VH, V_d[h], [VTT[h]], VHT_)
                    dma("sp", QN, QnT_d[h], [QnTT[h]], QNT_)
                    if b2 == 0:
                        dma("sp", QR, QrT_d[hp], [QrTT[hp]], QRT_)
                    ro = b2 * 64
                    for J in range(4):
                        qs = slice(J * 512, (J + 1) * 512)
                        tiles = [(c, r, kb) for c in range(J + 1) for r in range(2) for kb in range(4)]
                        gidx = h * 4 + J
                        opi, dpi = (2, 3) if gidx % 2 == 0 else (4, 5)
                        pend = None
                        n = len(tiles)

                        def qk(ti):
                            c, r, kb = tiles[ti]
                            kbi = r * 16 + c * 4 + kb
                            ks = slice(kbi * 128, (kbi + 1) * 128)
                            pi = nxt("sps", 2)
                            kt = [KNT_[(kbi * 128) // 1024]]
                            last_is_mask = (c == J)
                            mm(PS[pi][:, :], KN[:, ks], QN[:, qs], True, False, kt + [QNT_[J // 2]], [PST[pi]])
                            mm(PS[pi][:, :], KR[ro:ro + 64, ks], QR[ro:ro + 64, qs], False, not last_is_mask,
                               [BIGT[24 + (kbi * 128) // 1024], QRT_[J // 2]], [PST[pi]])
                            if last_is_mask:
                                mm(PS[pi][:, :], ident[:], MASK[:, (J % 2) * 8 + r * 4 + kb, :], False, True, [constT, MASKT], [PST[pi]])
                            pt = nxt("pt", 3)
                            act(PT[pt][:, :], PS[pi][:, :], AF.Exp, [PST[pi]], [PTT[pt]], scale=scale)
                            return (pt, kbi)

                        def pv(ti, st):
                            pt, kbi = st
                            mm(PS[opi][:, :], VH[:, kbi, :], PT[pt][:, :], ti == 0, ti == n - 1, [VHT_[(kbi * 128) // 1024], PTT[pt]], [PST[opi]])
                            mm(PS[dpi][:, :], ones[:], PT[pt][:, :], ti == 0, ti == n - 1, [constT, PTT[pt]], [PST[dpi]])

                        for ti in range(n):
                            st = qk(ti)
                            if pend is not None:
                                pv(ti - 1, pend)
                            pend = st
                        pv(n - 1, pend)
                        gate_store(PS[opi][:, :], [PST[opi]], PS[dpi][:, :], [PST[dpi]], Z0 + h * 128, h * 128, J * 512)
            if "mem" in RUN:
                mem_attention(layer, XQ0, Z0 + 3072)
            if "out" in RUN:
                for half in range(2):
                    out_proj(layer, half, layer == 0)

        def mem_attention(layer, XQ0, ZM0):
            load_gain(mem_norm_g[layer:layer + 1, :])
            MN = BIG[:, 0:8192].rearrange("p (k t) -> p k t", t=256)
            for mb in range(2):
                def tr(xi, mb=mb):
                    P.op("dve", lambda e: e.scalar_tensor_tensor(out=HB[:], in0=XS[xi][:], scalar=stat[:, xi:xi + 1], in1=AUX[:, :],
                                                                 op0=ALU.mult, op1=ALU.mult), [XST[xi], statT[xi], AUXT], [HBT])
                    for kg in range(8):
                        hb = kg % 2
                        for kk in range(4):
                            k = kg * 4 + kk
                            P.op("pe", lambda e, k=k, kk=kk, hb=hb: e.transpose(out=PSBs[hb][:, kk * 128:(kk + 1) * 128],
                                                                                in_=HB[:, k * 128:(k + 1) * 128], identity=ident[:]),
                                 [HBT, constT], [PSBT[hb]])
                        P.op("dve", lambda e, kg=kg, hb=hb: e.tensor_copy(out=MN[:, kg * 4:(kg + 1) * 4, mb * 128:(mb + 1) * 128],
                                                                         in_=PSBs[hb][:, 0:512].rearrange("p (k t) -> p k t", t=128)),
                             [PSBT[hb]], [BIGT[kg]])
                norm_block(mem_in[mb * 128:(mb + 1) * 128, :], [], mb, tr)
            MKT = BIG[:, 8192:10240].rearrange("p (c m) -> p c m", m=256)
            MV = BIG[:, 10240:12288].rearrange("p (c n) -> p c n", n=1024)
            for cb in range(8):
                wi, wv = load_w(w_mem_kv[layer][:, cb * 256:(cb + 1) * 256], 256)
                if cb < 4:
                    for cc in range(2):
                        c = cb * 2 + cc
                        pi = nxt("ps", 6)
                        for k in range(32):
                            mm(PS[pi][:, 0:256], wv[:, k, cc * 128:(cc + 1) * 128], MN[:, k, :], k == 0, k == 31, [WBT[wi], BIGT[k // 4]], [PST[pi]])
                        act(MKT[:, c, :], PS[pi][:, 0:256], AF.Copy, [PST[pi]], [BIGT[8 + c // 4]])
                else:
                    n0 = (cb - 4) * 256
                    for mc in range(2):
                        pi = nxt("ps", 6)
                        for k in range(32):
                            mm(PS[pi][:, 0:256], MN[:, k, mc * 128:(mc + 1) * 128], wv[:, k, :], k == 0, k == 31, [WBT[wi], BIGT[k // 4]], [PST[pi]])
                        act(MV[:, mc, n0:n0 + 256], PS[pi][:, 0:256], AF.Copy, [PST[pi]], [BIGT[10 + mc]])
            XQ = BIG[:, 16384:32768].rearrange("p (c t) -> p c t", t=TOK)
            for c in range(8):
                dma("pool", XQ[:, c, :], projT[XQ0 + c * 128:XQ0 + (c + 1) * 128, :], rows(XQ0 + c * 128, XQ0 + (c + 1) * 128),
                    BIGT[16 + c * 2:16 + c * 2 + 2])
            for hx in range(4):
                for tt in range(4):
                    tsl = slice(tt * 512, (tt + 1) * 512)
                    pts = []
                    for mc in range(2):
                        pi = nxt("ps", 6)
                        for dc in range(2):
                            c = hx * 2 + dc
                            mm(PS[pi][:, :], MKT[:, c, mc * 128:(mc + 1) * 128], XQ[:, c, tsl], dc == 0, dc == 1,
                               [BIGT[8 + c // 4], BIGT[16 + c * 2 + tt // 2]], [PST[pi]])
                        pt = nxt("pt", 3)
                        act(PT[pt][:, :], PS[pi][:, :], AF.Exp, [PST[pi]], [PTT[pt]], scale=1.0 / 16.0)
                        pts.append(pt)
                    dpi = nxt("ps", 6)
                    for mc in range(2):
                        mm(PS[dpi][:, :], ones[:], PT[pts[mc]][:, :], mc == 0, mc == 1, [constT, PTT[pts[mc]]], [PST[dpi]])
                    for dvc in range(2):
                        opi = nxt("ps", 6)
                        for mc in range(2):
                            mm(PS[opi][:, :], MV[:, mc, hx * 256 + dvc * 128:hx * 256 + (dvc + 1) * 128], PT[pts[mc]][:, :], mc == 0, mc == 1,
                               [BIGT[10 + mc], PTT[pts[mc]]], [PST[opi]])
                        gate_store(PS[opi][:, :], [PST[opi]], PS[dpi][:, :], [PST[dpi]], ZM0 + hx * 256 + dvc * 128,
                                   3072 + hx * 256 + dvc * 128, tt * 512)

        swa_state = {"bm": False}

        def swa_setup():
            dist = np.arange(128)
            bucket = t5_bucket_np(dist)
            gfill = XS[0][0:48, 0:384]
            P.op("dve", lambda e: e.memset(gfill, NEG), [], [XST[0]])
            dma("sp", Gd[:, :], gfill, [XST[0]], [GdT])
            dma("sp", RB[:], rel_bias[:, :], [], [constT])
            for d in range(128):
                b = int(bucket[d])
                dma("sp", Gd[:, 127 + d:128 + d].rearrange("h o -> o h"), RB[b:b + 1, :], [constT], [GdT], slow=True)
            BMv = BIGF[:, 0:6144].rearrange("p (h q) -> p h q", q=128)
            for which in range(2):
                for k in range(128):
                    off = (127 - k) if which == 0 else (255 - k)
                    dma("sp", BMv[k:k + 1, :, :], Gd[:, off:off + 128].rearrange("(o h) q -> o h q", o=1), [GdT], BIGT[0:12])
                dma("sp", BM_d[which], BIGF[:, 0:6144], BIGT[0:12], [BMT])

        def swa_layer(layer, j):
            w_in = b_w_in[j]
            K0, V0, XQ0, Z0 = 3072, 3584, 4096, 5120
            if not swa_state["bm"]:
                swa_setup()
                swa_state["bm"] = True
            for half in range(2):
                load_gain(norm_g[layer:layer + 1, :])
                for tbl in range(8):
                    tb = half * 8 + tbl
                    norm_block(xres[tb * 128:(tb + 1) * 128, :], [xresT[tb]], tb % 2, lambda xi, tbl=tbl: h_transposed(xi, tbl, 1024))
                in_proj(w_in, B_IN, Z0, half)
            for i in range(4):
                ei = nxt("ev", 4)
                for kq in range(8):
                    dma("sp", EV[ei][:, kq * 128:(kq + 1) * 128], projT[K0 + kq * 128:K0 + (kq + 1) * 128, (4 * i + 3) * 128:(4 * i + 4) * 128],
                        rows(K0 + kq * 128, K0 + (kq + 1) * 128), [EVT[ei]])
                for kq in range(8):
                    dma("sp", sendS[kq][:, i * 128:(i + 1) * 128], EV[ei][:, kq * 128:(kq + 1) * 128], [EVT[ei]], [sendST])
            for pc in range(8):
                P.op("pool", lambda e, pc=pc: e.collective_compute("AllGather", ALU.bypass, replica_groups=[[0, 1], [2, 3], [4, 5], [6, 7]],
                                                                 ins=[sendS[pc][:, :]], outs=[gathS[pc][:, :]]),
                     [sendST], [gathST], "cc")
            for i in range(4):
                e_o, e_a, e_b = nxt("ev", 4), nxt("ev", 4), nxt("ev", 4)
                if i > 0:
                    for kq in range(8):
                        dma("sp", EV[e_o][:, kq * 128:(kq + 1) * 128], projT[K0 + kq * 128:K0 + (kq + 1) * 128, (4 * i - 1) * 128:(4 * i) * 128],
                            rows(K0 + kq * 128, K0 + (kq + 1) * 128), [EVT[e_o]])
                else:
                    P.op("dve", lambda e, e_o=e_o: e.memset(EV[e_o][:, :], 0.0), [], [EVT[e_o]])
                P.op("dve", lambda e, e_o=e_o, i=i: e.tensor_scalar(out=EV[e_o][:, :], in0=EV[e_o][:, :], scalar1=csel[:, 3 * i:3 * i + 1], scalar2=None, op0=ALU.mult),
                     [EVT[e_o], constT], [EVT[e_o]])
                for r, e_r in ((0, e_a), (1, e_b)):
                    for kq in range(8):
                        dma("sp", EV[e_r][:, kq * 128:(kq + 1) * 128], gathS[kq][r * 128:(r + 1) * 128, i * 128:(i + 1) * 128], [gathST], [EVT[e_r]])
                    P.op("dve", lambda e, e_o=e_o, e_r=e_r, i=i, r=r: e.scalar_tensor_tensor(out=EV[e_o][:, :], in0=EV[e_r][:, :], scalar=csel[:, 3 * i + 1 + r:3 * i + 2 + r],
                                                                                          in1=EV[e_o][:, :], op0=ALU.mult, op1=ALU.add),
                         [EVT[e_o], EVT[e_r], constT], [EVT[e_o]])
                for kq in range(8):
                    dma("sp", kvp_d[kq * 128:(kq + 1) * 128, i * 128:(i + 1) * 128], EV[e_o][:, kq * 128:(kq + 1) * 128], [EVT[e_o]], [kvpT[kq]])
            BMv = BIGF[:, 0:12288].rearrange("p (w h q) -> p w h q", w=2, q=128)
            for which in range(2):
                dma("sp", BIGF[:, which * 6144:(which + 1) * 6144], BM_d[which], [BMT], BIGT[which * 12:(which + 1) * 12])
            dma("sp", ES[:, :], b_sinks[j:j + 1, :].partition_broadcast(64), [], [EST])
            act(ES[:, :], ES[:, :], AF.Exp, [EST], [EST])
            sc = 0.125
            KC = BIG[0:64, 24576:26624]
            VCt = BIG[0:64, 26624:28672]
            KB = BIG[0:64, 28672:29184]
            VBt = BIG[0:64, 29696:30208]
            VC = HB[:, 0:1024].rearrange("p (b n) -> p b n", n=64)
            VB = HB[:, 1024:1280].rearrange("p (b n) -> p b n", n=64)
            for g in range(8):
                dma("pool", KC, projT[K0 + g * 64:K0 + (g + 1) * 64, :], rows(K0 + g * 64, K0 + (g + 1) * 64), BIGT[24:26])
                dma("pool", VCt, projT[V0 + g * 64:V0 + (g + 1) * 64, :], rows(V0 + g * 64, V0 + (g + 1) * 64), BIGT[26:28])
                dma("pool", KB, kvp_d[g * 64:(g + 1) * 64, :], [kvpT[g // 2]], [BIGT[28]])
                dma("pool", VBt, kvp_d[512 + g * 64:512 + (g + 1) * 64, :], [kvpT[4 + g // 2]], [BIGT[29]])
                for hb in range(2):
                    for bb in range(8):
                        b_ = hb * 8 + bb
                        P.op("pe", lambda e, b_=b_, bb=bb, hb=hb: e.transpose(out=PSBs[hb][:, bb * 64:(bb + 1) * 64],
                                                                           in_=VCt[:, b_ * 128:(b_ + 1) * 128], identity=ident[0:64, 0:64]),
                             [BIGT[26 + b_ // 8], constT], [PSBT[hb]])
                    P.op("dve", lambda e, hb=hb: e.tensor_copy(out=HB[:, hb * 512:(hb + 1) * 512], in_=PSBs[hb][:, 0:512]), [PSBT[hb]], [HBT])
                for bb in range(4):
                    P.op("pe", lambda e, bb=bb: e.transpose(out=PSBs[0][:, bb * 64:(bb + 1) * 64], in_=VBt[:, bb * 128:(bb + 1) * 128],
                                                          identity=ident[0:64, 0:64]), [BIGT[29], constT], [PSBT[0]])
                P.op("dve", lambda e: e.tensor_copy(out=HB[:, 1024:1280], in_=PSBs[0][:, 0:256]), [PSBT[0]], [HBT])
                for u in range(6):
                    P.op("dve", lambda e, u=u, g=g: e.tensor_scalar(out=HB[0:1, 2048 + u * 128:2048 + (u + 1) * 128], in0=ones[0:1, 0:128],
                                                                   scalar1=ES[0:1, g * 6 + u:g * 6 + u + 1], scalar2=None, op0=ALU.mult),
                         [constT, EST, ESRT], [ESRT])
                for jb in range(NB):
                    qsl = slice(jb * 128, (jb + 1) * 128)
                    if jb % 4 == 0:
                        i = jb // 4
                        kprev, kprevT = KB[:, i * 128:(i + 1) * 128], BIGT[28]
                        vprev = VB[:, i, :]
                    else:
                        kprev, kprevT = KC[:, (jb - 1) * 128:jb * 128], BIGT[24 + (jb - 1) // 8]
                        vprev = VC[:, jb - 1, :]
                    qi = jb % 2
                    QGv = QG[qi][:, :].rearrange("p (h q) -> p h q", q=128)
                    dma("pool", QGv, projT[g * 384:(g + 1) * 384, qsl].rearrange("(h p) q -> p h q", p=64), rows(g * 384, (g + 1) * 384), [QGT[qi]])
                    for hf in range(2):
                        h0 = g * 6 + hf * 3
                        qrhs = QGv[:, hf * 3:hf * 3 + 3, :]
                        p_prev, p_cur = nxt("ps", 6), nxt("ps", 6)
                        mm(PS[p_prev][:, 0:384], kprev, qrhs, True, True, [kprevT, QGT[qi]], [PST[p_prev]])
                        mm(PS[p_cur][:, 0:384], KC[:, qsl], qrhs, True, True, [BIGT[24 + jb // 8], QGT[qi]], [PST[p_cur]])
                        pts = []
                        for which, pp in ((1, p_prev), (0, p_cur)):
                            e_t = nxt("ev", 4)
                            P.op("dve", lambda e, pp=pp, e_t=e_t, which=which, h0=h0: e.scalar_tensor_tensor(
                                out=EV[e_t][:, 0:384].rearrange("p (h q) -> p h q", q=128), in0=PS[pp][:, 0:384].rearrange("p (h q) -> p h q", q=128),
                                scalar=sc, in1=BMv[:, which, h0:h0 + 3, :], op0=ALU.mult, op1=ALU.add),
                                [PST[pp]] + BIGT[which * 12:(which + 1) * 12], [EVT[e_t]])
                            pt = nxt("pt", 3)
                            act(PT[pt][:, 0:384], EV[e_t][:, 0:384], AF.Exp, [EVT[e_t]], [PTT[pt]])
                            pts.append(pt)
                        opi, dpi = nxt("ps", 6), nxt("ps", 6)
                        mm(PS[opi][0:64, 0:384], vprev, PT[pts[0]][:, 0:384], True, False, [HBT, PTT[pts[0]]], [PST[opi]])
                        mm(PS[opi][0:64, 0:384], VC[:, jb, :], PT[pts[1]][:, 0:384], False, True, [HBT, PTT[pts[1]]], [PST[opi]])
                        mm(PS[dpi][0:64, 0:384], ONP_l[jb], PT[pts[0]][:, 0:384], True, False, [constT, PTT[pts[0]]], [PST[dpi]])
                        mm(PS[dpi][0:64, 0:384], ones[:, 0:64], PT[pts[1]][:, 0:384], False, False, [constT, PTT[pts[1]]], [PST[dpi]])
                        mm(PS[dpi][0:64, 0:384], ones[0:1, 0:64], HB[0:1, 2048 + hf * 384:2048 + (hf + 1) * 384], False, True, [constT, ESRT], [PST[dpi]])
                        e_d = nxt("ev", 4)
                        P.op("dve", lambda e, e_d=e_d, dpi=dpi: e.reciprocal(out=EV[e_d][0:64, 0:384], in_=PS[dpi][0:64, 0:384]), [PST[dpi]], [EVT[e_d]])
                        P.op("dve", lambda e, e_d=e_d, opi=opi: e.tensor_tensor(out=EV[e_d][0:64, 0:384], in0=PS[opi][0:64, 0:384], in1=EV[e_d][0:64, 0:384], op=ALU.mult),
                             [PST[opi], EVT[e_d]], [EVT[e_d]])
                        e_z = nxt("ev", 4)
                        zv = EV[e_z][0:64, 0:384].rearrange("p (h q) -> p h q", q=128)
                        dma("sp", zv, projT[Z0 + h0 * 64:Z0 + (h0 + 3) * 64, qsl].rearrange("(h p) q -> p h q", p=64), rows(Z0 + h0 * 64, Z0 + (h0 + 3) * 64), [EVT[e_z]])
                        yb = EV[e_d][0:64, 512:1024].bitcast(BF16)[:, 0:384]
                        P.op("dve", lambda e, e_d=e_d, e_z=e_z, yb=yb: e.tensor_tensor(out=yb, in0=EV[e_d][0:64, 0:384], in1=EV[e_z][0:64, 0:384], op=ALU.mult),
                             [EVT[e_d], EVT[e_z]], [EVT[e_d]])
                        dma("sp", yT_d[h0 * 64:(h0 + 3) * 64, qsl].rearrange("(h p) q -> p h q", p=64), yb.rearrange("p (h q) -> p h q", q=128),
                            [EVT[e_d]], yTT[(h0 * 64) // 128:((h0 + 3) * 64 - 1) // 128 + 1])
            mem_attention(layer, XQ0, Z0 + 3072)
            for half in range(2):
                out_proj(layer, half, False)

        onp0 = sb("onp0", [128, 64], BF16)
        P.op("dve", lambda e: e.memset(onp0[:], 1.0), [], [constT])
        P.op("dve", lambda e: e.tensor_scalar(out=onp0[:], in0=onp0[:], scalar1=csel[:, 12:13], scalar2=None, op0=ALU.mult), [constT], [constT])
        ONP_l = [onp0[:, :]] + [ones[:, 0:64]] * (NB - 1)

        for layer in range(n_layers):
            if layer % 2 == 0:
                mla_layer(layer, layer // 2)
            else:
                swa_layer(layer, layer // 2)
        load_gain(final_g[0:1, :])
        for tb in range(NB):
            def fin(xi, tb=tb):
                P.op("dve", lambda e: e.scalar_tensor_tensor(out=XS[xi][:], in0=XS[xi][:], scalar=stat[:, xi:xi + 1], in1=AUX[:, :],
                                                             op0=ALU.mult, op1=ALU.mult), [XST[xi], statT[xi], AUXT], [XST[xi]])
                dma("sp", out_d[tb * 128:(tb + 1) * 128, :], XS[xi][:], [XST[xi]], [])
            use_res = n_layers > 0 and "out" in RUN
            src = xres if use_res else x_in
            norm_block(src[tb * 128:(tb + 1) * 128, :], [xresT[tb]] if use_res else [], tb % 2, fin)

        P.emit(nc, es, None)
    return nc


def _tok_idx(s):
    return np.concatenate([np.arange(c * 512, (c + 1) * 512) for c in CHUNKS[s]])


def _const_tables(s):
    ident = np.eye(128, dtype=np.float32)
    inv = (10000.0 ** (-np.arange(0, 64, 2, dtype=np.float32) / 64)).astype(np.float32)
    rope = np.zeros((128, 2), np.float32)
    for p in range(128):
        rope[p, 0] = np.float32(inv[p % 32]) / np.float32(2 * math.pi)
        rope[p, 1] = -1.0 if (p % 64) < 32 else 1.0
    mask = np.zeros((128, 16, 512), np.float32)
    kk = np.arange(128)[:, None]
    qq = np.arange(512)[None, :]
    for par in range(2):
        gq = CHUNKS[s][par]
        for r in range(2):
            gk = CHUNKS[r][par]
            for kb in range(4):
                if gk < gq:
                    m = np.zeros((128, 512), np.float32)
                elif gk > gq:
                    m = np.full((128, 512), NEG, np.float32)
                else:
                    m = np.where(kb * 128 + kk <= qq, 0.0, NEG).astype(np.float32)
                mask[:, par * 8 + r * 4 + kb, :] = m
    sel = np.zeros((128, 8), np.float32)
    return ident, rope, mask, sel


def _sel_table(s):
    sel = np.zeros((128, 16), np.float32)
    for i in range(4):
        g = CHUNKS[s][i]
        if g == 0:
            continue
        prev = g - 1
        if prev in CHUNKS[s]:
            sel[:, 3 * i] = 1.0
        else:
            sel[:, 3 * i + 1 + (1 - s)] = 1.0
    sel[:, 12] = 0.0 if CHUNKS[s][0] == 0 else 1.0
    return sel


def kernel(**inputs):
    x = np.asarray(inputs["x"], np.float32)
    mem = np.asarray(inputs["mem"], np.float32)
    pos = np.asarray(inputs["positions"], np.int32)
    nc = build(N_LAYERS)
    shared = {k: np.ascontiguousarray(np.asarray(inputs[k], np.float32)) for k in
              ["norm_g", "mem_norm_g", "a_q_norm_g", "a_kv_norm_g", "b_sinks", "rel_bias"]}
    nA, nB = (N_LAYERS + 1) // 2, N_LAYERS // 2
    for l in range(N_LAYERS):
        shared[f"w_mem_kv_{l}"] = np.ascontiguousarray(np.asarray(inputs["w_mem_kv"][l], np.float32))
        shared[f"w_out_{l}"] = np.ascontiguousarray(np.asarray(inputs["w_out"][l], np.float32))
    for l in range(nA):
        shared[f"a_w_in_{l}"] = np.ascontiguousarray(np.asarray(inputs["a_w_in"][l], np.float32))
        shared[f"a_w_qb_{l}"] = np.ascontiguousarray(np.asarray(inputs["a_w_qb"][l], np.float32))
        shared[f"a_w_kvb_{l}"] = np.ascontiguousarray(np.asarray(inputs["a_w_kvb"][l], np.float32))
    for l in range(nB):
        shared[f"b_w_in_{l}"] = np.ascontiguousarray(np.asarray(inputs["b_w_in"][l], np.float32))
    shared["a_q_norm_g"] = np.ascontiguousarray(shared["a_q_norm_g"].reshape(2, 8, 128).transpose(0, 2, 1))
    shared["a_kv_norm_g"] = np.ascontiguousarray(shared["a_kv_norm_g"].reshape(2, 4, 128).transpose(0, 2, 1))
    shared["final_norm_g"] = np.ascontiguousarray(np.asarray(inputs["final_norm_g"], np.float32).reshape(1, D))
    need = {"a_w_in": "p1", "b_w_in": "p1", "a_w_qb": "p3", "a_w_kvb": "p4", "w_mem_kv": "mem", "w_out": "out"}
    for k in list(shared):
        for pre, ph in need.items():
            if k.startswith(pre) and ph not in RUN:
                shared[k] = np.zeros((1, 1), np.float32)
    in_maps = []
    for c in range(8):
        b, s = c // 2, c % 2
        idx = _tok_idx(s)
        ident, rope, mask, _ = _const_tables(s)
        m = dict(shared)
        m["x"] = np.ascontiguousarray(x[b][idx])
        m["mem"] = np.ascontiguousarray(mem[b])
        m["pos"] = np.ascontiguousarray(pos[b][idx].reshape(1, TOK))
        m["c_ident"] = ident
        m["c_rope"] = rope
        m["c_mask"] = mask
        m["c_sel"] = _sel_table(s)
        in_maps.append(m)
    res = run_bass_kernel_spmd(nc, in_maps, core_ids=list(range(8)))
    out = np.zeros((4, 4096, D), np.float32)
    for c in range(8):
        b, s = c // 2, c % 2
        out[b, _tok_idx(s)] = res.results[c]["out"]
    return out
```

```python
import math
import numpy as np
from contextlib import ExitStack
import concourse.bass as bass
import concourse.mybir as mybir
from concourse.bass_utils import run_bass_kernel_spmd

F32, BF16, I32 = mybir.dt.float32, mybir.dt.bfloat16, mybir.dt.int32
AF = mybir.ActivationFunctionType
ALU = mybir.AluOpType

N_LAYERS = 4
P4SUB = 7
RUN = {"p1", "p2", "p3", "p4", "p5", "mem", "out"}
D = 4096
TOK = 2048
NB = 16
EPS = 1e-6
A_IN, B_IN = 6720, 9216
NEG = -30000.0
CHUNKS = {0: [0, 3, 4, 7], 1: [1, 2, 5, 6]}
NSEM_DMA = 20


class T:
    __slots__ = ("w", "r")

    def __init__(self):
        self.w = None
        self.r = {}


def Ts(n):
    return [T() for _ in range(n)]


class Op:
    __slots__ = ("eng", "fn", "deps", "kind", "sig", "ticket", "idx", "snap", "sem", "val", "waits")


class Prog:
    ENGS = ["pe", "act", "dve", "pool", "sp"]

    def __init__(self):
        self.ops = {e: [] for e in self.ENGS}
        self.allops = []
        self.dmah = {e: [] for e in self.ENGS}

    def op(self, eng, fn, reads=(), writes=(), kind="c"):
        o = Op()
        o.eng, o.fn, o.kind, o.sig = eng, fn, kind, False
        deps = []
        for t in reads:
            if t.w is not None:
                deps.append(t.w)
        for t in writes:
            for lst in t.r.values():
                deps.extend(lst)
            if t.w is not None:
                deps.append(t.w)
        if kind != "c":
            h = self.dmah[eng]
            if len(h) >= NSEM_DMA:
                deps.append(h[-NSEM_DMA])
            h.append(o)
        o.deps = deps
        key = eng if kind == "c" else "d" + eng
        for t in reads:
            lst = t.r.get(key)
            if lst is None:
                t.r[key] = [o]
            elif kind == "c":
                lst[0] = o
            else:
                lst.append(o)
                if len(lst) > NSEM_DMA:
                    del lst[0]
        for t in writes:
            t.w = o
            t.r = {}
        o.idx = len(self.ops[eng])
        self.ops[eng].append(o)
        self.allops.append(o)
        return o

    def resolve(self):
        known = {e: {f: -1 for f in self.ENGS} for e in self.ENGS}
        kdma = {e: set() for e in self.ENGS}
        for o in self.allops:
            E = o.eng
            kn = known[E]
            waits = []
            for x in o.deps:
                if x is o:
                    continue
                if x.kind != "c":
                    if id(x) not in kdma[E]:
                        kdma[E].add(id(x))
                        waits.append(x)
                    continue
                F = x.eng
                if F == E and E == "pe":
                    continue
                if x.idx <= kn[F]:
                    continue
                x.sig = True
                waits.append(x)
                kn[F] = x.idx
                for g, v in x.snap.items():
                    if v > kn[g]:
                        kn[g] = v
            o.waits = waits
            if o.kind == "c":
                o.snap = dict(kn)
            else:
                o.snap = None

    def emit(self, nc, es, handles_of_block):
        self.resolve()
        ROT = 20000
        sems = {}
        for e in self.ENGS:
            nsig = sum(1 for o in self.ops[e] if o.kind == "c" and o.sig)
            sems[e] = [es.enter_context(nc.semaphore(f"s_{e}_{i}")) for i in range(nsig // ROT + 1)]
            c = 0
            for o in self.ops[e]:
                if o.kind == "c" and o.sig:
                    o.sem = sems[e][c // ROT]
                    o.val = c % ROT + 1
                    c += 1
            nd = len(self.dmah[e])
            if nd:
                pool = [es.enter_context(nc.semaphore(f"d_{e}_{i}")) for i in range(min(nd, NSEM_DMA))]
                cnt = [0] * len(pool)
                ncc = sum(1 for o in self.dmah[e] if o.kind == "cc")
                ccpool = [es.enter_context(nc.semaphore(f"cc_{e}_{i}")) for i in range(min(ncc, 8))]
                cccnt = [0] * len(ccpool)
                ci = 0
                for i, o in enumerate(self.dmah[e]):
                    if o.kind == "cc":
                        k = ci % len(ccpool)
                        ci += 1
                        cccnt[k] += 1
                        o.sem, o.val = ccpool[k], cccnt[k]
                        continue
                    k = i % len(pool)
                    cnt[k] += 16
                    o.sem, o.val = pool[k], cnt[k]
        block = es.enter_context(nc.Block())

        def run(ename):
            def body(e):
                for o in self.ops[ename]:
                    for x in o.waits:
                        e.wait_ge(x.sem, x.val)
                    ins = o.fn(e)
                    if o.kind == "d":
                        ins.then_inc(o.sem, 16)
                    elif o.kind == "cc":
                        ins.then_inc(o.sem)
                    elif o.sig:
                        ins.then_inc(o.sem, 1)
                for o in self.dmah[ename][-NSEM_DMA:]:
                    e.wait_ge(o.sem, o.val)
            return body

        block.tensor(run("pe"))
        block.scalar(run("act"))
        block.vector(run("dve"))
        block.gpsimd(run("pool"))
        block.sync(run("sp"))


def t5_bucket_np(dist):
    n = np.maximum(dist, 0)
    nf = np.maximum(n, 1).astype(np.float32)
    large = 16 + (np.log(nf / 16) / math.log(128 / 16) * 16).astype(np.int32)
    large = np.minimum(large, 31)
    return np.where(n < 16, n, large)


def build(n_layers):
    nc = bass.Bass("TRN2", target_bir_lowering=False)
    P = Prog()

    def din(name, shape, dt=F32):
        need = {"a_w_in": "p1", "b_w_in": "p1", "a_w_qb": "p3", "a_w_kvb": "p4", "w_mem_kv": "mem", "w_out": "out"}
        for pre, ph in need.items():
            if name.startswith(pre) and ph not in RUN:
                shape = [1, 1]
        return nc.dram_tensor(name, list(shape), dt, kind="ExternalInput").ap()

    def dscr(name, shape, dt):
        return nc.dram_tensor(name, list(shape), dt).ap()

    x_in = din("x", [TOK, D])
    mem_in = din("mem", [256, D])
    pos_in = din("pos", [1, TOK], I32)
    norm_g = din("norm_g", [4, D])
    mem_norm_g = din("mem_norm_g", [4, D])
    final_g = din("final_norm_g", [1, D])
    nA, nB = (n_layers + 1) // 2, n_layers // 2
    w_mem_kv = [din(f"w_mem_kv_{l}", [D, 2048]) for l in range(n_layers)]
    w_out = [din(f"w_out_{l}", [D, D]) for l in range(n_layers)]
    a_w_in = [din(f"a_w_in_{l}", [D, A_IN]) for l in range(nA)]
    a_qg = din("a_q_norm_g", [2, 128, 8])
    a_kvg = din("a_kv_norm_g", [2, 128, 4])
    a_w_qb = [din(f"a_w_qb_{l}", [1024, 4608]) for l in range(nA)]
    a_w_kvb = [din(f"a_w_kvb_{l}", [512, 6144]) for l in range(nA)]
    b_w_in = [din(f"b_w_in_{l}", [D, B_IN]) for l in range(nB)]
    b_sinks = din("b_sinks", [2, 48])
    rel_bias = din("rel_bias", [32, 48])
    c_ident = din("c_ident", [128, 128])
    c_rope = din("c_rope", [128, 2])
    c_mask = din("c_mask", [128, 16, 512])
    c_sel = din("c_sel", [128, 16])
    out_d = nc.dram_tensor("out", [TOK, D], F32, kind="ExternalOutput").ap()

    xres = dscr("xres", [TOK, D], F32)
    projT = dscr("projT", [B_IN, TOK], F32)
    yT_d = dscr("yT_d", [D, TOK], BF16)
    sendKV_f = [dscr(f"sendKV{p}", [128, 512], F32) for p in range(10)]
    gathKV_f = [dscr(f"gathKV{p}", [256, 512], F32) for p in range(10)]
    sendKV = [a.bitcast(BF16).rearrange("(r two) c -> r (two c)", two=2) for a in sendKV_f]
    gathKV = [a.bitcast(BF16).rearrange("(r i two) c -> r i (two c)", r=2, two=2) for a in gathKV_f]
    QnT_d = dscr("QnT_d", [24, 128, TOK], BF16)
    QrT_d = dscr("QrT_d", [12, 128, TOK], BF16)
    KnT_d = dscr("KnT_d", [24, 128, 2 * TOK], BF16)
    V_d = dscr("V_d", [24, 128, 32, 128], BF16)
    cs_d = dscr("cs_d", [2, 128, TOK], F32)
    Gd = dscr("Gd", [48, 384], F32)
    BM_d = dscr("BM_d", [2, 128, 48 * 128], F32)
    sendS = [dscr(f"sendS{p}", [128, 512], F32) for p in range(8)]
    gathS = [dscr(f"gathS{p}", [256, 512], F32) for p in range(8)]
    kvp_d = dscr("kvp_d", [1024, 512], F32)

    xresT, projTT, yTT = Ts(NB), Ts(72), Ts(32)
    sendKVT, gathKVT, csT, GdT, BMT, sendST, gathST = T(), T(), T(), T(), T(), T(), T()
    QnTT, QrTT, KnTT, VTT, kvpT = Ts(24), Ts(12), Ts(24), Ts(24), Ts(8)

    es = ExitStack()
    with es:
        def sb(name, shape, dt):
            return es.enter_context(nc.sbuf_tensor(name, list(shape), dt))

        BIG = sb("BIG", [128, 32768], BF16)
        BIGT = Ts(32)
        BIGF = BIG[:].bitcast(F32)
        WBall = sb("WBall", [128, 24576], BF16)
        WB = [WBall[:, i * 8192:(i + 1) * 8192] for i in range(3)]
        WBT = Ts(3)
        XSall = sb("XSall", [128, 8192], F32)
        XS = [XSall[:, i * 4096:(i + 1) * 4096] for i in range(2)]
        XST = Ts(2)
        AUX = sb("AUX", [128, 4096], F32)
        AUXT = T()
        MASK = sb("MASK", [128, 16, 512], BF16)
        MASKT = T()
        HB = sb("HB", [128, 4096], BF16)
        HBT = T()
        EV = [sb(f"EV{i}", [128, 1024], F32) for i in range(4)]
        EVT = Ts(4)
        PT = [sb(f"PT{i}", [128, 512], BF16) for i in range(3)]
        PTT = Ts(3)
        ident = sb("ident", [128, 128], BF16)
        ones = sb("ones", [128, 128], BF16)
        epsT = sb("epsT", [128, 1], F32)
        crope = sb("crope", [128, 2], F32)
        csel = sb("csel", [128, 16], F32)
        stat = sb("stat", [128, 8], F32)
        statT = Ts(8)
        gq = sb("gq", [128, 12], F32)
        gqT = T()
        RB = sb("RB", [32, 48], F32)
        QG = [sb(f"QG{i}", [64, 768], BF16) for i in range(2)]
        QGT = Ts(2)
        ES = sb("ES", [64, 48], F32)
        EST = T()
        ESRT = T()
        YBT = Ts(5)
        constT = T()
        PS = [es.enter_context(nc.psum_tensor(f"PS{i}", [128, 512], F32)) for i in range(6)]
        PST = Ts(6)
        PSBs = [es.enter_context(nc.psum_tensor(f"PSB{i}", [128, 1024], BF16)) for i in range(2)]
        PSBT = Ts(2)

        cnt = {"ps": 0, "ev": 0, "wb": 0, "pt": 0, "sps": 0, "yb": 0}

        def nxt(k, n):
            v = cnt[k] % n
            cnt[k] += 1
            return v

        def dma(q, out, in_, reads, writes, slow=False):
            if slow:
                return P.op(q, lambda e: e.dma_start(out=out, in_=in_, allow_slow_non_contiguous=True), reads, writes, "d")
            return P.op(q, lambda e: e.dma_start(out=out, in_=in_), reads, writes, "d")

        def mm(out, lhsT, rhs, start, stop, reads, writes):
            return P.op("pe", lambda e: e.matmul(out, lhsT=lhsT, rhs=rhs, start=start, stop=stop), reads, writes)

        def act(out, in_, func, reads, writes, scale=1.0, bias=None, accum=None):
            def f(e):
                kw = {}
                if bias is not None:
                    kw["bias"] = bias
                if accum is not None:
                    kw["accum_out"] = accum
                return e.activation(out=out, in_=in_, func=func, scale=scale, **kw)
            return P.op("act", f, reads, writes)

        def rows(r0, r1):
            return projTT[r0 // 128:(r1 - 1) // 128 + 1]

        dma("pool", ident[:], c_ident[:, :], [], [constT])
        dma("sp", crope[:], c_rope[:, :], [], [constT])
        dma("sp", csel[:], c_sel[:, :], [], [constT])
        dma("pool", MASK[:], c_mask[:, :, :], [], [MASKT])
        P.op("dve", lambda e: e.memset(ones[:], 1.0), [], [constT])
        P.op("dve", lambda e: e.memset(epsT[:], EPS), [], [constT])

        posi = XS[0][:].bitcast(I32)
        dma("sp", posi[:, 0:TOK], pos_in[0:1, :].partition_broadcast(128), [], [XST[0]])
        ang = XS[1]
        P.op("dve", lambda e: e.tensor_copy(out=ang[:, 0:TOK], in_=posi[:, 0:TOK]), [XST[0]], [XST[1]])
        P.op("dve", lambda e: e.tensor_scalar(out=ang[:, 0:TOK], in0=ang[:, 0:TOK], scalar1=crope[:, 0:1], scalar2=None, op0=ALU.mult),
             [XST[1], constT], [XST[1]])
        P.op("dve", lambda e: e.tensor_scalar(out=ang[:, TOK:2 * TOK], in0=ang[:, 0:TOK], scalar1=0.25, scalar2=None, op0=ALU.add),
             [XST[1]], [XST[1]])
        x0f = XS[0]
        for (lo, hi) in ((0, TOK), (TOK, 2 * TOK)):
            P.op("dve", lambda e, lo=lo, hi=hi: e.tensor_copy(out=posi[:, TOK:2 * TOK], in_=ang[:, lo:hi]), [XST[1]], [XST[0]])
            P.op("dve", lambda e: e.tensor_copy(out=x0f[:, 0:TOK], in_=posi[:, TOK:2 * TOK]), [XST[0]], [XST[0]])
            P.op("dve", lambda e, lo=lo, hi=hi: e.tensor_tensor(out=ang[:, lo:hi], in0=ang[:, lo:hi], in1=x0f[:, 0:TOK], op=ALU.subtract),
                 [XST[0], XST[1]], [XST[1]])
            P.op("dve", lambda e, lo=lo, hi=hi: e.scalar_tensor_tensor(out=ang[:, lo:hi], in0=ang[:, lo:hi], scalar=0.5, in1=ang[:, lo:hi],
                                                                       op0=ALU.is_gt, op1=ALU.subtract), [XST[1]], [XST[1]])
        SC = -(2.0 * math.pi - 2e-6)
        act(AUX[:, TOK:2 * TOK], ang[:, 0:TOK], AF.Sin, [XST[1]], [AUXT], scale=SC)
        act(AUX[:, 0:TOK], ang[:, TOK:2 * TOK], AF.Sin, [XST[1]], [AUXT], scale=SC)
        P.op("dve", lambda e: e.tensor_scalar(out=AUX[:, TOK:2 * TOK], in0=AUX[:, TOK:2 * TOK], scalar1=crope[:, 1:2], scalar2=None, op0=ALU.mult),
             [AUXT, constT], [AUXT])
        dma("sp", cs_d[0], AUX[:, 0:TOK], [AUXT], [csT])
        dma("sp", cs_d[1], AUX[:, TOK:2 * TOK], [AUXT], [csT])

        def load_gain(g_ap_row):
            dma("sp", AUX[:, :], g_ap_row.partition_broadcast(128), [], [AUXT])

        def norm_block(src_ap, src_reads, xi, dst_fn):
            dma("sp", XS[xi][:], src_ap, src_reads, [XST[xi]])
            act(HB[:], XS[xi][:], AF.Square, [XST[xi]], [HBT, statT[xi]], accum=stat[:, xi:xi + 1])
            act(stat[:, xi:xi + 1], stat[:, xi:xi + 1], AF.Sqrt, [statT[xi], constT], [statT[xi]], scale=1.0 / D, bias=epsT[:, 0:1])
            P.op("dve", lambda e: e.reciprocal(out=stat[:, xi:xi + 1], in_=stat[:, xi:xi + 1]), [statT[xi]], [statT[xi]])
            dst_fn(xi)

        def h_transposed(xi, tb_local, nchunk_tok):
            P.op("dve", lambda e: e.scalar_tensor_tensor(out=HB[:], in0=XS[xi][:], scalar=stat[:, xi:xi + 1], in1=AUX[:, :],
                                                         op0=ALU.mult, op1=ALU.mult), [XST[xi], statT[xi], AUXT], [HBT])
            W = nchunk_tok
            for kg in range(8):
                hb = kg % 2
                for kk in range(4):
                    k = kg * 4 + kk
                    P.op("pe", lambda e, k=k, kk=kk, hb=hb: e.transpose(out=PSBs[hb][:, kk * 128:(kk + 1) * 128],
                                                                        in_=HB[:, k * 128:(k + 1) * 128], identity=ident[:]),
                         [HBT, constT], [PSBT[hb]])
                def cp(e, kg=kg, hb=hb):
                    o = BIG[:].rearrange("p (k t) -> p k t", t=W)[:, kg * 4:(kg + 1) * 4, tb_local * 128:(tb_local + 1) * 128]
                    i = PSBs[hb][:, 0:512].rearrange("p (k t) -> p k t", t=128)
                    return e.tensor_copy(out=o, in_=i)
                wr = [BIGT[(k * W) // 1024] for k in range(kg * 4, kg * 4 + 4)]
                P.op("dve", cp, [PSBT[hb]], wr)

        def load_w(src_ap, ncols):
            wi = nxt("wb", 3)
            o = WB[wi][:, 0:32 * ncols].rearrange("p (k n) -> p k n", n=ncols)
            sv = src_ap.rearrange("(k p) n -> p k n", p=128)
            for kq in range(8):
                dma("pool", o[:, kq * 4:(kq + 1) * 4, :], sv[:, kq * 4:(kq + 1) * 4, :], [], [WBT[wi]])
            return wi, o

        def in_proj(w_ap, E, zstart, half):
            hT = BIG[:].rearrange("p (k t) -> p k t", t=1024)
            nblk = (E + 511) // 512
            for cb in range(nblk):
                c0 = cb * 512
                ncol = min(512, E - c0)
                if cb % 2 == 0:
                    wbuf, wT = WBall[:, 0:16384], WBT[0:2]
                else:
                    wbuf, wT = XSall[:, :].bitcast(BF16), XST[0:2]
                wv = wbuf[:, 0:32 * ncol].rearrange("p (k n) -> p k n", n=ncol)
                sv = w_ap[:, c0:c0 + ncol].rearrange("(k p) n -> p k n", p=128)
                for kq in range(8):
                    dma("pool", wv[:, kq * 4:(kq + 1) * 4, :], sv[:, kq * 4:(kq + 1) * 4, :], [], wT)
                for cc in range(0, ncol, 128):
                    m = min(128, ncol - cc)
                    e0 = c0 + cc
                    for tt in range(2):
                        pi = nxt("ps", 6)
                        for k in range(32):
                            mm(PS[pi][0:m, :], wv[:, k, cc:cc + m], hT[:, k, tt * 512:(tt + 1) * 512], k == 0, k == 31,
                               wT + [BIGT[k]], [PST[pi]])
                        ei = nxt("ev", 4)
                        segs = []
                        if e0 + m <= zstart:
                            segs = [(0, m, AF.Copy)]
                        elif e0 >= zstart:
                            segs = [(0, m, AF.Silu)]
                        else:
                            segs = [(0, zstart - e0, AF.Copy), (zstart - e0, m, AF.Silu)]
                        for (p0, p1, fn) in segs:
                            act(EV[ei][p0:p1, 0:512], PS[pi][p0:p1, :], fn, [PST[pi]], [EVT[ei]])
                        t0 = half * 1024 + tt * 512
                        dma("sp", projT[e0:e0 + m, t0:t0 + 512], EV[ei][0:m, 0:512], [EVT[ei]], rows(e0, e0 + m))

        def out_proj(layer, half, src_is_input):
            yv = BIG[:].rearrange("p (k t) -> p k t", t=1024)
            for kq in range(4):
                dma("sp", yv[:, kq * 8:(kq + 1) * 8, :],
                    yT_d[kq * 1024:(kq + 1) * 1024, half * 1024:(half + 1) * 1024].rearrange("(k p) t -> p k t", p=128),
                    yTT[kq * 8:(kq + 1) * 8], BIGT[kq * 8:(kq + 1) * 8])
            xsrc = x_in if src_is_input else xres
            for db in range(8):
                if db % 2 == 0:
                    wbuf, wT = WBall[:, 0:16384], WBT[0:2]
                else:
                    wbuf, wT = XSall[:, :].bitcast(BF16), XST[0:2]
                wv = wbuf.rearrange("p (k n) -> p k n", n=512)
                sv = w_out[layer][:, db * 512:(db + 1) * 512].rearrange("(k p) n -> p k n", p=128)
                for kq in range(8):
                    dma("pool", wv[:, kq * 4:(kq + 1) * 4, :], sv[:, kq * 4:(kq + 1) * 4, :], [], wT)
                for tb in range(8):
                    pi = nxt("ps", 6)
                    ei = nxt("ev", 4)
                    for k in range(32):
                        mm(PS[pi][:, :], yv[:, k, tb * 128:(tb + 1) * 128], wv[:, k, :], k == 0, k == 31, wT + [BIGT[k]], [PST[pi]])
                    gtb = half * 8 + tb
                    xrd = [] if src_is_input else [xresT[gtb]]
                    dma("sp", EV[ei][:, 0:512], xsrc[gtb * 128:(gtb + 1) * 128, db * 512:(db + 1) * 512], xrd, [EVT[ei]])
                    P.op("dve", lambda e, ei=ei, pi=pi: e.tensor_tensor(out=EV[ei][:, 0:512], in0=PS[pi][:, :], in1=EV[ei][:, 0:512], op=ALU.add),
                         [PST[pi], EVT[ei]], [EVT[ei]])
                    dma("sp", xres[gtb * 128:(gtb + 1) * 128, db * 512:(db + 1) * 512], EV[ei][:, 0:512], [EVT[ei]], [xresT[gtb]])

        def gate_store(o_ps, o_pst, den_ps, den_pst, zrow0, yrow0, t0, npart=128, width=512):
            e1, e2 = nxt("ev", 4), nxt("ev", 4)
            dma("sp", EV[e1][0:npart, 0:width], projT[zrow0:zrow0 + npart, t0:t0 + width], rows(zrow0, zrow0 + npart), [EVT[e1]])
            P.op("dve", lambda e: e.reciprocal(out=EV[e2][0:npart, 0:width], in_=den_ps), den_pst, [EVT[e2]])
            P.op("dve", lambda e: e.tensor_tensor(out=EV[e2][0:npart, 0:width], in0=o_ps, in1=EV[e2][0:npart, 0:width], op=ALU.mult),
                 o_pst + [EVT[e2]], [EVT[e2]])
            yb = EV[e2][0:npart, 512:1024].bitcast(BF16)[:, 0:width]
            P.op("dve", lambda e: e.tensor_tensor(out=yb, in0=EV[e2][0:npart, 0:width], in1=EV[e1][0:npart, 0:width], op=ALU.mult),
                 [EVT[e1], EVT[e2]], [EVT[e2]])
            dma("sp", yT_d[yrow0:yrow0 + npart, t0:t0 + width], yb, [EVT[e2]], yTT[yrow0 // 128:(yrow0 + npart - 1) // 128 + 1])

        def latent_norm(row0, nk, gcol0, tt, dst_fn):
            xi = nxt("xs2", 2) if False else (tt % 2)
            cv = XS[xi][:, 0:nk * 512].rearrange("p (k t) -> p k t", t=512)
            dma("sp", cv, projT[row0:row0 + nk * 128, tt * 512:(tt + 1) * 512].rearrange("(k p) t -> p k t", p=128),
                rows(row0, row0 + nk * 128), [XST[xi]])
            sq = HB[:, 0:nk * 512].rearrange("p (k t) -> p k t", t=512)
            act(sq, cv, AF.Square, [XST[xi]], [HBT])
            pi = nxt("ps", 6)
            for k in range(nk):
                mm(PS[pi][:, :], ones[:], sq[:, k, :], k == 0, k == nk - 1, [HBT, constT], [PST[pi]])
            ei = nxt("ev", 4)
            act(EV[ei][:, 0:512], PS[pi][:, :], AF.Sqrt, [PST[pi], constT], [EVT[ei]], scale=1.0 / (nk * 128), bias=epsT[:, 0:1])
            P.op("dve", lambda e: e.reciprocal(out=EV[ei][:, 0:512], in_=EV[ei][:, 0:512]), [EVT[ei]], [EVT[ei]])
            for k in range(nk):
                dst_fn(k, cv[:, k, :], EV[ei][:, 0:512], gq[:, gcol0 + k:gcol0 + k + 1], [XST[xi], EVT[ei], gqT])

        def mla_layer(layer, j):
            w_in = a_w_in[j]
            XQ0, Z0 = 1600, 2624
            dma("sp", gq[:, 0:8], a_qg[j], [], [gqT])
            dma("sp", gq[:, 8:12], a_kvg[j], [], [gqT])
            for half in range(2):
                if "p1" not in RUN:
                    break
                load_gain(norm_g[layer:layer + 1, :])
                for tbl in range(8):
                    tb = half * 8 + tbl
                    src = x_in if layer == 0 else xres
                    rd = [] if layer == 0 else [xresT[tb]]
                    norm_block(src[tb * 128:(tb + 1) * 128, :], rd, tb % 2, lambda xi, tbl=tbl: h_transposed(xi, tbl, 1024))
                in_proj(w_in, A_IN, Z0, half)
            CQN = BIG[:, 0:16384].rearrange("p (k t) -> p k t", t=TOK)
            CKV = BIG[:, 16384:32768].rearrange("p (k t) -> p k t", t=2 * TOK)
            KR = BIG[:, 24576:28672]
            if "p2" in RUN:
                dma("sp", AUX[:, 0:TOK], cs_d[0], [csT], [AUXT])
                dma("sp", AUX[:, TOK:2 * TOK], cs_d[1], [csT], [AUXT])
                CQN = BIG[:, 0:16384].rearrange("p (k t) -> p k t", t=TOK)
                for tt in range(4):
                    def put_cq(k, src, rstd, g, rd, tt=tt):
                        P.op("dve", lambda e: e.scalar_tensor_tensor(out=CQN[:, k, tt * 512:(tt + 1) * 512], in0=src, scalar=g, in1=rstd,
                                                                     op0=ALU.mult, op1=ALU.mult), rd, [BIGT[(k * TOK + tt * 512) // 1024]])
                    latent_norm(0, 8, 0, tt, put_cq)

                    def put_ckv(k, src, rstd, g, rd, tt=tt):
                        pt = nxt("pt", 3)
                        P.op("dve", lambda e: e.scalar_tensor_tensor(out=PT[pt][:, :], in0=src, scalar=g, in1=rstd,
                                                                     op0=ALU.mult, op1=ALU.mult), rd, [PTT[pt]])
                        for hh in range(2):
                            dma("sp", sendKV[2 * k + hh][:, tt * 512:(tt + 1) * 512], PT[pt][hh * 64:(hh + 1) * 64, :], [PTT[pt]], [sendKVT])
                    latent_norm(1024, 4, 8, tt, put_ckv)
                    e1, e2 = nxt("ev", 4), nxt("ev", 4)
                    tsl = slice(tt * 512, (tt + 1) * 512)
                    for hh in range(2):
                        dma("sp", EV[e1][hh * 64:(hh + 1) * 64, 0:512], projT[1536:1600, tsl], rows(1536, 1600), [EVT[e1]])
                        dma("sp", EV[e2][hh * 64:hh * 64 + 32, 0:512], projT[1568:1600, tsl], rows(1536, 1600), [EVT[e2]])
                        dma("sp", EV[e2][hh * 64 + 32:(hh + 1) * 64, 0:512], projT[1536:1568, tsl], rows(1536, 1600), [EVT[e2]])
                    P.op("dve", lambda e, e1=e1, tsl=tsl: e.tensor_tensor(out=EV[e1][:, 0:512], in0=EV[e1][:, 0:512], in1=AUX[:, tsl], op=ALU.mult),
                         [EVT[e1], AUXT], [EVT[e1]])
                    P.op("dve", lambda e, e2=e2, tt=tt: e.tensor_tensor(out=EV[e2][:, 0:512], in0=EV[e2][:, 0:512],
                                                                       in1=AUX[:, TOK + tt * 512:TOK + (tt + 1) * 512], op=ALU.mult),
                         [EVT[e2], AUXT], [EVT[e2]])
                    pt = nxt("pt", 3)
                    P.op("dve", lambda e, e1=e1, e2=e2, pt=pt: e.tensor_tensor(out=PT[pt][:, :], in0=EV[e1][:, 0:512], in1=EV[e2][:, 0:512], op=ALU.add),
                         [EVT[e1], EVT[e2]], [PTT[pt]])
                    for hh in range(2):
                        dma("sp", sendKV[8 + hh][:, tsl], PT[pt][hh * 64:(hh + 1) * 64, :], [PTT[pt]], [sendKVT])
                for pc in range(10):
                    P.op("pool", lambda e, pc=pc: e.collective_compute("AllGather", ALU.bypass, replica_groups=[[0, 1], [2, 3], [4, 5], [6, 7]],
                                                                     ins=[sendKV_f[pc][:, :]], outs=[gathKV_f[pc][:, :]]),
                         [sendKVT], [gathKVT], "cc")
            if "p3" in RUN:
                for hp in range(12):
                    wi = nxt("wb", 3)
                    wv = WB[wi][:, 0:4096].rearrange("p (k g n) -> p k g n", g=4, n=128)
                    wq = a_w_qb[j].rearrange("(k p) n -> p k n", p=128)
                    for u in range(2):
                        h = hp * 2 + u
                        dma("pool", wv[:, :, u, :], wq[:, :, h * 192:h * 192 + 128], [], [WBT[wi]])
                        dma("pool", wv[:, :, 2, u * 64:(u + 1) * 64], wq[:, :, h * 192 + 128:h * 192 + 192], [], [WBT[wi]])
                        dma("pool", wv[:, :, 3, u * 64:u * 64 + 32], wq[:, :, h * 192 + 160:h * 192 + 192], [], [WBT[wi]])
                        dma("pool", wv[:, :, 3, u * 64 + 32:(u + 1) * 64], wq[:, :, h * 192 + 128:h * 192 + 160], [], [WBT[wi]])
                    stg = [nxt("ev", 4) for _ in range(3)]
                    sv = [EV[s][:].bitcast(BF16) for s in stg]
                    for tt in range(4):
                        tsl = slice(tt * 512, (tt + 1) * 512)
                        pis = [nxt("ps", 6) for _ in range(4)]
                        for g in range(4):
                            for k in range(8):
                                mm(PS[pis[g]][:, :], wv[:, k, g, :], CQN[:, k, tsl], k == 0, k == 7,
                                   [WBT[wi], BIGT[(k * TOK + tt * 512) // 1024]], [PST[pis[g]]])
                        for u in range(2):
                            act(sv[u][:, tsl], PS[pis[u]][:, :], AF.Copy, [PST[pis[u]]], [EVT[stg[u]]])
                        x1, x2 = XS[0][:, 0:512], XS[0][:, 512:1024]
                        P.op("dve", lambda e, pi=pis[2], tsl=tsl: e.tensor_tensor(out=x1, in0=PS[pi][:, :], in1=AUX[:, tsl], op=ALU.mult),
                             [PST[pis[2]], AUXT], [XST[0]])
                        P.op("dve", lambda e, pi=pis[3], tt=tt: e.tensor_tensor(out=x2, in0=PS[pi][:, :], in1=AUX[:, TOK + tt * 512:TOK + (tt + 1) * 512], op=ALU.mult),
                             [PST[pis[3]], AUXT], [XST[0]])
                        P.op("dve", lambda e, tsl=tsl, s2=sv[2]: e.tensor_tensor(out=s2[:, tsl], in0=x1, in1=x2, op=ALU.add), [XST[0]], [EVT[stg[2]]])
                    for u in range(2):
                        dma("sp", QnT_d[hp * 2 + u], sv[u][:, 0:TOK], [EVT[stg[u]]], [QnTT[hp * 2 + u]])
                    dma("sp", QrT_d[hp], sv[2][:, 0:TOK], [EVT[stg[2]]], [QrTT[hp]])
            if "p4" in RUN:
                CKV = BIG[:, 16384:32768].rearrange("p (k t) -> p k t", t=2 * TOK)
                for r in range(2):
                    for k in range(4):
                        for hh in range(2):
                            dma("sp", CKV[hh * 64:(hh + 1) * 64, k, r * TOK:(r + 1) * TOK], gathKV[2 * k + hh][r, :, :], [gathKVT],
                                BIGT[16 + k * 4 + r * 2:16 + k * 4 + r * 2 + 2])
                wkv = a_w_kvb[j].rearrange("(k p) (h n) -> p k h n", p=128, n=256)
                for hg in range(6):
                    wi = nxt("wb", 3)
                    wk = WB[wi][:, 0:2048].rearrange("p (k h n) -> p k h n", h=4, n=128)
                    wvv = WB[wi][:, 2048:4096].rearrange("p (k h n) -> p k h n", h=4, n=128)
                    for k in range(4):
                        dma("pool", wk[:, k], wkv[:, k, hg * 4:(hg + 1) * 4, 0:128], [], [WBT[wi]])
                        dma("pool", wvv[:, k], wkv[:, k, hg * 4:(hg + 1) * 4, 128:256], [], [WBT[wi]])
                    for u in range(4):
                        if not (P4SUB & 2):
                            break
                        h = hg * 4 + u
                        for th in range(2):
                            ei = nxt("ev", 4)
                            stv = EV[ei][:].bitcast(BF16)
                            for t4 in range(4):
                                t0 = th * 2048 + t4 * 512
                                pi = nxt("ps", 6)
                                for k in range(4):
                                    mm(PS[pi][:, :], wk[:, k, u, :], CKV[:, k, t0:t0 + 512], k == 0, k == 3,
                                       [WBT[wi], BIGT[16 + k * 4 + t0 // 1024]], [PST[pi]])
                                act(stv[:, t4 * 512:(t4 + 1) * 512], PS[pi][:, :], AF.Copy, [PST[pi]], [EVT[ei]])
                            dma("sp", KnT_d[h, :, th * 2048:(th + 1) * 2048], stv[:, 0:2048], [EVT[ei]], [KnTT[h]])
                    for kb4 in range(8):
                        if not (P4SUB & 4):
                            break
                        ei = nxt("ev", 4)
                        stv = EV[ei][:].bitcast(BF16).rearrange("p (b n) -> p b n", n=512)
                        for b_ in range(4):
                            kb = kb4 * 4 + b_
                            pi = nxt("ps", 6)
                            for k in range(4):
                                mm(PS[pi][:, :], CKV[:, k, kb * 128:(kb + 1) * 128], WB[wi][:, 2048 + k * 512:2048 + (k + 1) * 512],
                                   k == 0, k == 3, [WBT[wi], BIGT[16 + k * 4 + (kb * 128) // 1024]], [PST[pi]])
                            if True:
                                act(stv[:, b_, :], PS[pi][:, :], AF.Copy, [PST[pi]], [EVT[ei]])
                            else:
                                P.op("dve", lambda e, b_=b_, pi=pi, stv=stv: e.tensor_copy(out=stv[:, b_, :], in_=PS[pi][:, :]), [PST[pi]], [EVT[ei]])
                        for u in range(4):
                            if P4SUB & 16:
                                break
                            dma("sp", V_d[hg * 4 + u, :, kb4 * 4:(kb4 + 1) * 4, :], stv[:, :, u * 128:(u + 1) * 128], [EVT[ei]], [VTT[hg * 4 + u]])
            if "p5" in RUN:
                KR = BIG[:, 24576:28672]
                for r in range(2):
                    for hh in range(2):
                        dma("sp", KR[hh * 64:(hh + 1) * 64, r * TOK:(r + 1) * TOK], gathKV[8 + hh][r, :, :], [gathKVT], BIGT[24 + r * 2:24 + r * 2 + 2])
                scale = 192.0 ** -0.5
                for h in range(24):
                    b2 = h % 2
                    hp = h // 2
                    KN = BIG[:, b2 * 4096:(b2 + 1) * 4096]
                    KNT_ = BIGT[b2 * 4:(b2 + 1) * 4]
                    VH = BIG[:, 8192 + b2 * 4096:8192 + (b2 + 1) * 4096].rearrange("p (b n) -> p b n", n=128)
                    VHT_ = BIGT[8 + b2 * 4:8 + (b2 + 1) * 4]
                    QN = BIG[:, 16384 + b2 * 2048:16384 + (b2 + 1) * 2048]
                    QNT_ = BIGT[16 + b2 * 2:16 + (b2 + 1) * 2]
                    QR = BIG[:, 20480 + (hp % 2) * 2048:20480 + (hp % 2 + 1) * 2048]
                    QRT_ = BIGT[20 + (hp % 2) * 2:20 + (hp % 2 + 1) * 2]
                    dma("sp", KN, KnT_d[h], [KnTT[h]], KNT_)
                    dma("sp", VH, V_d[h], [VTT[h]], VHT_)
                    dma("sp", QN, QnT_d[h], [QnTT[h]], QNT_)
                    if b2 == 0:
                        dma("sp", QR, QrT_d[hp], [QrTT[hp]], QRT_)
                    ro = b2 * 64
                    for J in range(4):
                        qs = slice(J * 512, (J + 1) * 512)
                        tiles = [(c, r, kb) for c in range(J + 1) for r in range(2) for kb in range(4)]
                        gidx = h * 4 + J
                        opi, dpi = (2, 3) if gidx % 2 == 0 else (4, 5)
                        pend = None
                        n = len(tiles)

                        def qk(ti):
                            c, r, kb = tiles[ti]
                            kbi = r * 16 + c * 4 + kb
                            ks = slice(kbi * 128, (kbi + 1) * 128)
                            pi = nxt("sps", 2)
                            kt = [KNT_[(kbi * 128) // 1024]]
                            last_is_mask = (c == J)
                            mm(PS[pi][:, :], KN[:, ks], QN[:, qs], True, False, kt + [QNT_[J // 2]], [PST[pi]])
                            mm(PS[pi][:, :], KR[ro:ro + 64, ks], QR[ro:ro + 64, qs], False, not last_is_mask,
                               [BIGT[24 + (kbi * 128) // 1024], QRT_[J // 2]], [PST[pi]])
                            if last_is_mask:
                                mm(PS[pi][:, :], ident[:], MASK[:, (J % 2) * 8 + r * 4 + kb, :], False, True, [constT, MASKT], [PST[pi]])
                            pt = nxt("pt", 3)
                            act(PT[pt][:, :], PS[pi][:, :], AF.Exp, [PST[pi]], [PTT[pt]], scale=scale)
                            return (pt, kbi)

                        def pv(ti, st):
                            pt, kbi = st
                            mm(PS[opi][:, :], VH[:, kbi, :], PT[pt][:, :], ti == 0, ti == n - 1, [VHT_[(kbi * 128) // 1024], PTT[pt]], [PST[opi]])
                            mm(PS[dpi][:, :], ones[:], PT[pt][:, :], ti == 0, ti == n - 1, [constT, PTT[pt]], [PST[dpi]])

                        for ti in range(n):
                            st = qk(ti)
                            if pend is not None:
                                pv(ti - 1, pend)
                            pend = st
                        pv(n - 1, pend)
                        gate_store(PS[opi][:, :], [PST[opi]], PS[dpi][:, :], [PST[dpi]], Z0 + h * 128, h * 128, J * 512)
            if "mem" in RUN:
                mem_attention(layer, XQ0, Z0 + 3072)
            if "out" in RUN:
                for half in range(2):
                    out_proj(layer, half, layer == 0)

        def mem_attention(layer, XQ0, ZM0):
            load_gain(mem_norm_g[layer:layer + 1, :])
            MN = BIG[:, 0:8192].rearrange("p (k t) -> p k t", t=256)
            for mb in range(2):
                def tr(xi, mb=mb):
                    P.op("dve", lambda e: e.scalar_tensor_tensor(out=HB[:], in0=XS[xi][:], scalar=stat[:, xi:xi + 1], in1=AUX[:, :],
                                                                 op0=ALU.mult, op1=ALU.mult), [XST[xi], statT[xi], AUXT], [HBT])
                    for kg in range(8):
                        hb = kg % 2
                        for kk in range(4):
                            k = kg * 4 + kk
                            P.op("pe", lambda e, k=k, kk=kk, hb=hb: e.transpose(out=PSBs[hb][:, kk * 128:(kk + 1) * 128],
                                                                                in_=HB[:, k * 128:(k + 1) * 128], identity=ident[:]),
                                 [HBT, constT], [PSBT[hb]])
                        P.op("dve", lambda e, kg=kg, hb=hb: e.tensor_copy(out=MN[:, kg * 4:(kg + 1) * 4, mb * 128:(mb + 1) * 128],
                                                                         in_=PSBs[hb][:, 0:512].rearrange("p (k t) -> p k t", t=128)),
                             [PSBT[hb]], [BIGT[kg]])
                norm_block(mem_in[mb * 128:(mb + 1) * 128, :], [], mb, tr)
            MKT = BIG[:, 8192:10240].rearrange("p (c m) -> p c m", m=256)
            MV = BIG[:, 10240:12288].rearrange("p (c n) -> p c n", n=1024)
            for cb in range(8):
                wi, wv = load_w(w_mem_kv[layer][:, cb * 256:(cb + 1) * 256], 256)
                if cb < 4:
                    for cc in range(2):
                        c = cb * 2 + cc
                        pi = nxt("ps", 6)
                        for k in range(32):
                            mm(PS[pi][:, 0:256], wv[:, k, cc * 128:(cc + 1) * 128], MN[:, k, :], k == 0, k == 31, [WBT[wi], BIGT[k // 4]], [PST[pi]])
                        act(MKT[:, c, :], PS[pi][:, 0:256], AF.Copy, [PST[pi]], [BIGT[8 + c // 4]])
                else:
                    n0 = (cb - 4) * 256
                    for mc in range(2):
                        pi = nxt("ps", 6)
                        for k in range(32):
                            mm(PS[pi][:, 0:256], MN[:, k, mc * 128:(mc + 1) * 128], wv[:, k, :], k == 0, k == 31, [WBT[wi], BIGT[k // 4]], [PST[pi]])
                        act(MV[:, mc, n0:n0 + 256], PS[pi][:, 0:256], AF.Copy, [PST[pi]], [BIGT[10 + mc]])
            XQ = BIG[:, 16384:32768].rearrange("p (c t) -> p c t", t=TOK)
            for c in range(8):
                dma("pool", XQ[:, c, :], projT[XQ0 + c * 128:XQ0 + (c + 1) * 128, :], rows(XQ0 + c * 128, XQ0 + (c + 1) * 128),
                    BIGT[16 + c * 2:16 + c * 2 + 2])
            for hx in range(4):
                for tt in range(4):
                    tsl = slice(tt * 512, (tt + 1) * 512)
                    pts = []
                    for mc in range(2):
                        pi = nxt("ps", 6)
                        for dc in range(2):
                            c = hx * 2 + dc
                            mm(PS[pi][:, :], MKT[:, c, mc * 128:(mc + 1) * 128], XQ[:, c, tsl], dc == 0, dc == 1,
                               [BIGT[8 + c // 4], BIGT[16 + c * 2 + tt // 2]], [PST[pi]])
                        pt = nxt("pt", 3)
                        act(PT[pt][:, :], PS[pi][:, :], AF.Exp, [PST[pi]], [PTT[pt]], scale=1.0 / 16.0)
                        pts.append(pt)
                    dpi = nxt("ps", 6)
                    for mc in range(2):
                        mm(PS[dpi][:, :], ones[:], PT[pts[mc]][:, :], mc == 0, mc == 1, [constT, PTT[pts[mc]]], [PST[dpi]])
                    for dvc in range(2):
                        opi = nxt("ps", 6)
                        for mc in range(2):
                            mm(PS[opi][:, :], MV[:, mc, hx * 256 + dvc * 128:hx * 256 + (dvc + 1) * 128], PT[pts[mc]][:, :], mc == 0, mc == 1,
                               [BIGT[10 + mc], PTT[pts[mc]]], [PST[opi]])
                        gate_store(PS[opi][:, :], [PST[opi]], PS[dpi][:, :], [PST[dpi]], ZM0 + hx * 256 + dvc * 128,
                                   3072 + hx * 256 + dvc * 128, tt * 512)

        swa_state = {"bm": False}

        def swa_setup():
            dist = np.arange(128)
            bucket = t5_bucket_np(dist)
            gfill = XS[0][0:48, 0:384]
            P.op("dve", lambda e: e.memset(gfill, NEG), [], [XST[0]])
            dma("sp", Gd[:, :], gfill, [XST[0]], [GdT])
            dma("sp", RB[:], rel_bias[:, :], [], [constT])
            for d in range(128):
                b = int(bucket[d])
                dma("sp", Gd[:, 127 + d:128 + d].rearrange("h o -> o h"), RB[b:b + 1, :], [constT], [GdT], slow=True)
            BMv = BIGF[:, 0:6144].rearrange("p (h q) -> p h q", q=128)
            for which in range(2):
                for k in range(128):
                    off = (127 - k) if which == 0 else (255 - k)
                    dma("sp", BMv[k:k + 1, :, :], Gd[:, off:off + 128].rearrange("(o h) q -> o h q", o=1), [GdT], BIGT[0:12])
                dma("sp", BM_d[which], BIGF[:, 0:6144], BIGT[0:12], [BMT])

        def swa_layer(layer, j):
            w_in = b_w_in[j]
            K0, V0, XQ0, Z0 = 3072, 3584, 4096, 5120
            if not swa_state["bm"]:
                swa_setup()
                swa_state["bm"] = True
            for half in range(2):
                load_gain(norm_g[layer:layer + 1, :])
                for tbl in range(8):
                    tb = half * 8 + tbl
                    norm_block(xres[tb * 128:(tb + 1) * 128, :], [xresT[tb]], tb % 2, lambda xi, tbl=tbl: h_transposed(xi, tbl, 1024))
                in_proj(w_in, B_IN, Z0, half)
            for i in range(4):
                ei = nxt("ev", 4)
                for kq in range(8):
                    dma("sp", EV[ei][:, kq * 128:(kq + 1) * 128], projT[K0 + kq * 128:K0 + (kq + 1) * 128, (4 * i + 3) * 128:(4 * i + 4) * 128],
                        rows(K0 + kq * 128, K0 + (kq + 1) * 128), [EVT[ei]])
                for kq in range(8):
                    dma("sp", sendS[kq][:, i * 128:(i + 1) * 128], EV[ei][:, kq * 128:(kq + 1) * 128], [EVT[ei]], [sendST])
            for pc in range(8):
                P.op("pool", lambda e, pc=pc: e.collective_compute("AllGather", ALU.bypass, replica_groups=[[0, 1], [2, 3], [4, 5], [6, 7]],
                                                                 ins=[sendS[pc][:, :]], outs=[gathS[pc][:, :]]),
                     [sendST], [gathST], "cc")
            for i in range(4):
                e_o, e_a, e_b = nxt("ev", 4), nxt("ev", 4), nxt("ev", 4)
                if i > 0:
                    for kq in range(8):
                        dma("sp", EV[e_o][:, kq * 128:(kq + 1) * 128], projT[K0 + kq * 128:K0 + (kq + 1) * 128, (4 * i - 1) * 128:(4 * i) * 128],
                            rows(K0 + kq * 128, K0 + (kq + 1) * 128), [EVT[e_o]])
                else:
                    P.op("dve", lambda e, e_o=e_o: e.memset(EV[e_o][:, :], 0.0), [], [EVT[e_o]])
                P.op("dve", lambda e, e_o=e_o, i=i: e.tensor_scalar(out=EV[e_o][:, :], in0=EV[e_o][:, :], scalar1=csel[:, 3 * i:3 * i + 1], scalar2=None, op0=ALU.mult),
                     [EVT[e_o], constT], [EVT[e_o]])
                for r, e_r in ((0, e_a), (1, e_b)):
                    for kq in range(8):
                        dma("sp", EV[e_r][:, kq * 128:(kq + 1) * 128], gathS[kq][r * 128:(r + 1) * 128, i * 128:(i + 1) * 128], [gathST], [EVT[e_r]])
                    P.op("dve", lambda e, e_o=e_o, e_r=e_r, i=i, r=r: e.scalar_tensor_tensor(out=EV[e_o][:, :], in0=EV[e_r][:, :], scalar=csel[:, 3 * i + 1 + r:3 * i + 2 + r],
                                                                                          in1=EV[e_o][:, :], op0=ALU.mult, op1=ALU.add),
                         [EVT[e_o], EVT[e_r], constT], [EVT[e_o]])
                for kq in range(8):
                    dma("sp", kvp_d[kq * 128:(kq + 1) * 128, i * 128:(i + 1) * 128], EV[e_o][:, kq * 128:(kq + 1) * 128], [EVT[e_o]], [kvpT[kq]])
            BMv = BIGF[:, 0:12288].rearrange("p (w h q) -> p w h q", w=2, q=128)
            for which in range(2):
                dma("sp", BIGF[:, which * 6144:(which + 1) * 6144], BM_d[which], [BMT], BIGT[which * 12:(which + 1) * 12])
            dma("sp", ES[:, :], b_sinks[j:j + 1, :].partition_broadcast(64), [], [EST])
            act(ES[:, :], ES[:, :], AF.Exp, [EST], [EST])
            sc = 0.125
            KC = BIG[0:64, 24576:26624]
            VCt = BIG[0:64, 26624:28672]
            KB = BIG[0:64, 28672:29184]
            VBt = BIG[0:64, 29696:30208]
            VC = HB[:, 0:1024].rearrange("p (b n) -> p b n", n=64)
            VB = HB[:, 1024:1280].rearrange("p (b n) -> p b n", n=64)
            for g in range(8):
                dma("pool", KC, projT[K0 + g * 64:K0 + (g + 1) * 64, :], rows(K0 + g * 64, K0 + (g + 1) * 64), BIGT[24:26])
                dma("pool", VCt, projT[V0 + g * 64:V0 + (g + 1) * 64, :], rows(V0 + g * 64, V0 + (g + 1) * 64), BIGT[26:28])
                dma("pool", KB, kvp_d[g * 64:(g + 1) * 64, :], [kvpT[g // 2]], [BIGT[28]])
                dma("pool", VBt, kvp_d[512 + g * 64:512 + (g + 1) * 64, :], [kvpT[4 + g // 2]], [BIGT[29]])
                for hb in range(2):
                    for bb in range(8):
                        b_ = hb * 8 + bb
                        P.op("pe", lambda e, b_=b_, bb=bb, hb=hb: e.transpose(out=PSBs[hb][:, bb * 64:(bb + 1) * 64],
                                                                           in_=VCt[:, b_ * 128:(b_ + 1) * 128], identity=ident[0:64, 0:64]),
                             [BIGT[26 + b_ // 8], constT], [PSBT[hb]])
                    P.op("dve", lambda e, hb=hb: e.tensor_copy(out=HB[:, hb * 512:(hb + 1) * 512], in_=PSBs[hb][:, 0:512]), [PSBT[hb]], [HBT])
                for bb in range(4):
                    P.op("pe", lambda e, bb=bb: e.transpose(out=PSBs[0][:, bb * 64:(bb + 1) * 64], in_=VBt[:, bb * 128:(bb + 1) * 128],
                                                          identity=ident[0:64, 0:64]), [BIGT[29], constT], [PSBT[0]])
                P.op("dve", lambda e: e.tensor_copy(out=HB[:, 1024:1280], in_=PSBs[0][:, 0:256]), [PSBT[0]], [HBT])
                for u in range(6):
                    P.op("dve", lambda e, u=u, g=g: e.tensor_scalar(out=HB[0:1, 2048 + u * 128:2048 + (u + 1) * 128], in0=ones[0:1, 0:128],
                                                                   scalar1=ES[0:1, g * 6 + u:g * 6 + u + 1], scalar2=None, op0=ALU.mult),
                         [constT, EST, ESRT], [ESRT])
                for jb in range(NB):
                    qsl = slice(jb * 128, (jb + 1) * 128)
                    if jb % 4 == 0:
                        i = jb // 4
                        kprev, kprevT = KB[:, i * 128:(i + 1) * 128], BIGT[28]
                        vprev = VB[:, i, :]
                    else:
                        kprev, kprevT = KC[:, (jb - 1) * 128:jb * 128], BIGT[24 + (jb - 1) // 8]
                        vprev = VC[:, jb - 1, :]
                    qi = jb % 2
                    QGv = QG[qi][:, :].rearrange("p (h q) -> p h q", q=128)
                    dma("pool", QGv, projT[g * 384:(g + 1) * 384, qsl].rearrange("(h p) q -> p h q", p=64), rows(g * 384, (g + 1) * 384), [QGT[qi]])
                    for hf in range(2):
                        h0 = g * 6 + hf * 3
                        qrhs = QGv[:, hf * 3:hf * 3 + 3, :]
                        p_prev, p_cur = nxt("ps", 6), nxt("ps", 6)
                        mm(PS[p_prev][:, 0:384], kprev, qrhs, True, True, [kprevT, QGT[qi]], [PST[p_prev]])
                        mm(PS[p_cur][:, 0:384], KC[:, qsl], qrhs, True, True, [BIGT[24 + jb // 8], QGT[qi]], [PST[p_cur]])
                        pts = []
                        for which, pp in ((1, p_prev), (0, p_cur)):
                            e_t = nxt("ev", 4)
                            P.op("dve", lambda e, pp=pp, e_t=e_t, which=which, h0=h0: e.scalar_tensor_tensor(
                                out=EV[e_t][:, 0:384].rearrange("p (h q) -> p h q", q=128), in0=PS[pp][:, 0:384].rearrange("p (h q) -> p h q", q=128),
                                scalar=sc, in1=BMv[:, which, h0:h0 + 3, :], op0=ALU.mult, op1=ALU.add),
                                [PST[pp]] + BIGT[which * 12:(which + 1) * 12], [EVT[e_t]])
                            pt = nxt("pt", 3)
                            act(PT[pt][:, 0:384], EV[e_t][:, 0:384], AF.Exp, [EVT[e_t]], [PTT[pt]])
                            pts.append(pt)
                        opi, dpi = nxt("ps", 6), nxt("ps", 6)
                        mm(PS[opi][0:64, 0:384], vprev, PT[pts[0]][:, 0:384], True, False, [HBT, PTT[pts[0]]], [PST[opi]])
                        mm(PS[opi][0:64, 0:384], VC[:, jb, :], PT[pts[1]][:, 0:384], False, True, [HBT, PTT[pts[1]]], [PST[opi]])
                        mm(PS[dpi][0:64, 0:384], ONP_l[jb], PT[pts[0]][:, 0:384], True, False, [constT, PTT[pts[0]]], [PST[dpi]])
                        mm(PS[dpi][0:64, 0:384], ones[:, 0:64], PT[pts[1]][:, 0:384], False, False, [constT, PTT[pts[1]]], [PST[dpi]])
                        mm(PS[dpi][0:64, 0:384], ones[0:1, 0:64], HB[0:1, 2048 + hf * 384:2048 + (hf + 1) * 384], False, True, [constT, ESRT], [PST[dpi]])
                        e_d = nxt("ev", 4)
                        P.op("dve", lambda e, e_d=e_d, dpi=dpi: e.reciprocal(out=EV[e_d][0:64, 0:384], in_=PS[dpi][0:64, 0:384]), [PST[dpi]], [EVT[e_d]])
                        P.op("dve", lambda e, e_d=e_d, opi=opi: e.tensor_tensor(out=EV[e_d][0:64, 0:384], in0=PS[opi][0:64, 0:384], in1=EV[e_d][0:64, 0:384], op=ALU.mult),
                             [PST[opi], EVT[e_d]], [EVT[e_d]])
                        e_z = nxt("ev", 4)
                        zv = EV[e_z][0:64, 0:384].rearrange("p (h q) -> p h q", q=128)
                        dma("sp", zv, projT[Z0 + h0 * 64:Z0 + (h0 + 3) * 64, qsl].rearrange("(h p) q -> p h q", p=64), rows(Z0 + h0 * 64, Z0 + (h0 + 3) * 64), [EVT[e_z]])
                        yi = nxt("yb", 5)
                        yoff = (1280, 1664, 2816, 3200, 3584)[yi]
                        yb = HB[0:64, yoff:yoff + 384]
                        P.op("dve", lambda e, e_d=e_d, e_z=e_z, yb=yb: e.tensor_tensor(out=yb, in0=EV[e_d][0:64, 0:384], in1=EV[e_z][0:64, 0:384], op=ALU.mult),
                             [EVT[e_d], EVT[e_z]], [YBT[yi]])
                        dma("sp", yT_d[h0 * 64:(h0 + 3) * 64, qsl].rearrange("(h p) q -> p h q", p=64), yb.rearrange("p (h q) -> p h q", q=128),
                            [YBT[yi]], yTT[(h0 * 64) // 128:((h0 + 3) * 64 - 1) // 128 + 1])
            mem_attention(layer, XQ0, Z0 + 3072)
            for half in range(2):
                out_proj(layer, half, False)

        onp0 = sb("onp0", [128, 64], BF16)
        P.op("dve", lambda e: e.memset(onp0[:], 1.0), [], [constT])
        P.op("dve", lambda e: e.tensor_scalar(out=onp0[:], in0=onp0[:], scalar1=csel[:, 12:13], scalar2=None, op0=ALU.mult), [constT], [constT])
        ONP_l = [onp0[:, :]] + [ones[:, 0:64]] * (NB - 1)

        for layer in range(n_layers):
            if layer % 2 == 0:
                mla_layer(layer, layer // 2)
            else:
                swa_layer(layer, layer // 2)
        load_gain(final_g[0:1, :])
        for tb in range(NB):
            def fin(xi, tb=tb):
                P.op("dve", lambda e: e.scalar_tensor_tensor(out=XS[xi][:], in0=XS[xi][:], scalar=stat[:, xi:xi + 1], in1=AUX[:, :],
                                                             op0=ALU.mult, op1=ALU.mult), [XST[xi], statT[xi], AUXT], [XST[xi]])
                dma("sp", out_d[tb * 128:(tb + 1) * 128, :], XS[xi][:], [XST[xi]], [])
            use_res = n_layers > 0 and "out" in RUN
            src = xres if use_res else x_in
            norm_block(src[tb * 128:(tb + 1) * 128, :], [xresT[tb]] if use_res else [], tb % 2, fin)

        P.emit(nc, es, None)
    return nc


def _tok_idx(s):
    return np.concatenate([np.arange(c * 512, (c + 1) * 512) for c in CHUNKS[s]])


def _const_tables(s):
    ident = np.eye(128, dtype=np.float32)
    inv = (10000.0 ** (-np.arange(0, 64, 2, dtype=np.float32) / 64)).astype(np.float32)
    rope = np.zeros((128, 2), np.float32)
    for p in range(128):
        rope[p, 0] = np.float32(inv[p % 32]) / np.float32(2 * math.pi)
        rope[p, 1] = -1.0 if (p % 64) < 32 else 1.0
    mask = np.zeros((128, 16, 512), np.float32)
    kk = np.arange(128)[:, None]
    qq = np.arange(512)[None, :]
    for par in range(2):
        gq = CHUNKS[s][par]
        for r in range(2):
            gk = CHUNKS[r][par]
            for kb in range(4):
                if gk < gq:
                    m = np.zeros((128, 512), np.float32)
                elif gk > gq:
                    m = np.full((128, 512), NEG, np.float32)
                else:
                    m = np.where(kb * 128 + kk <= qq, 0.0, NEG).astype(np.float32)
                mask[:, par * 8 + r * 4 + kb, :] = m
    sel = np.zeros((128, 8), np.float32)
    return ident, rope, mask, sel


def _sel_table(s):
    sel = np.zeros((128, 16), np.float32)
    for i in range(4):
        g = CHUNKS[s][i]
        if g == 0:
            continue
        prev = g - 1
        if prev in CHUNKS[s]:
            sel[:, 3 * i] = 1.0
        else:
            sel[:, 3 * i + 1 + (1 - s)] = 1.0
    sel[:, 12] = 0.0 if CHUNKS[s][0] == 0 else 1.0
    return sel


def kernel(**inputs):
    x = np.asarray(inputs["x"], np.float32)
    mem = np.asarray(inputs["mem"], np.float32)
    pos = np.asarray(inputs["positions"], np.int32)
    nc = build(N_LAYERS)
    shared = {k: np.ascontiguousarray(np.asarray(inputs[k], np.float32)) for k in
              ["norm_g", "mem_norm_g", "a_q_norm_g", "a_kv_norm_g", "b_sinks", "rel_bias"]}
    nA, nB = (N_LAYERS + 1) // 2, N_LAYERS // 2
    for l in range(N_LAYERS):
        shared[f"w_mem_kv_{l}"] = np.ascontiguousarray(np.asarray(inputs["w_mem_kv"][l], np.float32))
        shared[f"w_out_{l}"] = np.ascontiguousarray(np.asarray(inputs["w_out"][l], np.float32))
    for l in range(nA):
        shared[f"a_w_in_{l}"] = np.ascontiguousarray(np.asarray(inputs["a_w_in"][l], np.float32))
        shared[f"a_w_qb_{l}"] = np.ascontiguousarray(np.asarray(inputs["a_w_qb"][l], np.float32))
        shared[f"a_w_kvb_{l}"] = np.ascontiguousarray(np.asarray(inputs["a_w_kvb"][l], np.float32))
    for l in range(nB):
        shared[f"b_w_in_{l}"] = np.ascontiguousarray(np.asarray(inputs["b_w_in"][l], np.float32))
    shared["a_q_norm_g"] = np.ascontiguousarray(shared["a_q_norm_g"].reshape(2, 8, 128).transpose(0, 2, 1))
    shared["a_kv_norm_g"] = np.ascontiguousarray(shared["a_kv_norm_g"].reshape(2, 4, 128).transpose(0, 2, 1))
    shared["final_norm_g"] = np.ascontiguousarray(np.asarray(inputs["final_norm_g"], np.float32).reshape(1, D))
    need = {"a_w_in": "p1", "b_w_in": "p1", "a_w_qb": "p3", "a_w_kvb": "p4", "w_mem_kv": "mem", "w_out": "out"}
    for k in list(shared):
        for pre, ph in need.items():
            if k.startswith(pre) and ph not in RUN:
                shared[k] = np.zeros((1, 1), np.float32)
    in_maps = []
    for c in range(8):
        b, s = c // 2, c % 2
        idx = _tok_idx(s)
        ident, rope, mask, _ = _const_tables(s)
        m = dict(shared)
        m["x"] = np.ascontiguousarray(x[b][idx])
        m["mem"] = np.ascontiguousarray(mem[b])
        m["pos"] = np.ascontiguousarray(pos[b][idx].reshape(1, TOK))
        m["c_ident"] = ident
        m["c_rope"] = rope
        m["c_mask"] = mask
        m["c_sel"] = _sel_table(s)
        in_maps.append(m)
    res = run_bass_kernel_spmd(nc, in_maps, core_ids=list(range(8)))
    out = np.zeros((4, 4096, D), np.float32)
    for c in range(8):
        b, s = c // 2, c % 2
        out[b, _tok_idx(s)] = res.results[c]["out"]
    return out
```
